# Optimizing a Trainium2 kernel written in Bass

```python
import math
import jax
import jax.numpy as jnp
from jax import lax
import numpy as np

D_MODEL = 1024
BATCH = 16
SEQ = 2048
DEPTH = 4

GRID_W = 64
CTX_LEN = 256
N_EVEN = (DEPTH + 1) // 2
N_ODD = DEPTH // 2
EPS = 1e-6
NEG_INF = -1e30

SSD_HEADS = 16
SSD_HEAD_DIM = 64
SSD_INNER = SSD_HEADS * SSD_HEAD_DIM
SSD_GROUPS = 4
SSD_HPG = SSD_HEADS // SSD_GROUPS
SSD_STATE = 128
SSD_CHUNK = 128
SSD_CONV = 4
SSD_XBC = SSD_INNER + 2 * SSD_GROUPS * SSD_STATE

LRU_WIDTH = 1024
LRU_BLOCKS = 16
LRU_BLOCK_DIM = LRU_WIDTH // LRU_BLOCKS
LRU_CONV = 4
LRU_C = 8.0

STATE_COLS = SSD_XBC + 2 * SSD_HEADS + LRU_WIDTH
EVEN_IN = STATE_COLS + SSD_INNER + LRU_WIDTH
EVEN_MIX = SSD_INNER + LRU_WIDTH

NA_HEADS = 16
NA_HEAD_DIM = 64
NA_DIM = NA_HEADS * NA_HEAD_DIM
NA_KH = 8
NA_KW = 16

PEER_HEADS = 8
PEER_KEYS = 128
PEER_EXPERTS = PEER_KEYS * PEER_KEYS
PEER_QDIM = 256
PEER_TOPK = 16
PEER_BLOCK = 128

kernel_name = "hybrid_ssd_rglru_natten_peer_prefix_trunk"


def rms_norm(x, g):
    xf = x.astype(jnp.float32)
    y = xf * lax.rsqrt(jnp.mean(xf * xf, axis=-1, keepdims=True) + EPS)
    return (y * g.astype(jnp.float32)).astype(x.dtype)


def modulate(x, shift, scale):
    return x * (1 + scale) + shift


def flip_seq(t, rev):
    return jnp.flip(t, axis=1) if rev else t


def dw_conv(x, w, b):
    k, ch = w.shape
    y = lax.conv_general_dilated(x, w[:, None, :].astype(x.dtype), window_strides=(1,),
                                 padding=[(k // 2, k - 1 - k // 2)],
                                 dimension_numbers=("NWC", "WIO", "NWC"),
                                 feature_group_count=ch)
    return y + b


def segsum_exp(la):
    q = la.shape[-1]
    cs = jnp.cumsum(la, axis=-1)
    diff = cs[..., :, None] - cs[..., None, :]
    mask = jnp.tril(jnp.ones((q, q), dtype=bool))
    return jnp.where(mask, jnp.exp(jnp.where(mask, diff, 0.0)), 0.0)


def ssd_scan(xs, dt, a, bm, cm, h0, want_y):
    f32 = jnp.float32
    bsz, seq, g, r, p = xs.shape
    n = bm.shape[-1]
    nc, q = seq // SSD_CHUNK, SSD_CHUNK
    xd = (xs.astype(f32) * dt[..., None]).reshape(bsz, nc, q, g, r, p)
    la = jnp.moveaxis((dt * a).reshape(bsz, nc, q, g, r), 2, -1)
    bq = bm.astype(f32).reshape(bsz, nc, q, g, n)
    cs = jnp.cumsum(la, axis=-1)
    states = jnp.einsum("bcsgn,bcgrs,bcsgrp->bcgrpn", bq, jnp.exp(cs[..., -1:] - cs), xd)

    def step(h, inp):
        decay, s = inp
        return h * decay[..., None, None] + s, (h if want_y else None)

    final, h_in = lax.scan(step, h0, (jnp.moveaxis(jnp.exp(cs[..., -1]), 1, 0),
                                      jnp.moveaxis(states, 1, 0)))
    if not want_y:
        return None, final
    cq = cm.astype(f32).reshape(bsz, nc, q, g, n)
    cb = jnp.einsum("bclgn,bcsgn->bcgls", cq, bq)
    y_diag = jnp.einsum("bcgls,bcgrls,bcsgrp->bclgrp", cb, segsum_exp(la), xd)
    y_off = jnp.einsum("bclgn,bcgrpn,bcgrl->bclgrp", cq, jnp.moveaxis(h_in, 0, 1), jnp.exp(cs))
    return (y_diag + y_off).reshape(bsz, seq, g, r, p), final


def ssd_direction(xbc, dt_raw, a_log, dt_bias, h0, want_y):
    bsz, seq, _ = xbc.shape
    xs = xbc[..., :SSD_INNER].reshape(bsz, seq, SSD_GROUPS, SSD_HPG, SSD_HEAD_DIM)
    bm = xbc[..., SSD_INNER:SSD_INNER + SSD_GROUPS * SSD_STATE].reshape(bsz, seq, SSD_GROUPS, SSD_STATE)
    cm = xbc[..., SSD_INNER + SSD_GROUPS * SSD_STATE:].reshape(bsz, seq, SSD_GROUPS, SSD_STATE)
    dt = jax.nn.softplus(dt_raw.astype(jnp.float32) + dt_bias.astype(jnp.float32))
    dt = dt.reshape(bsz, seq, SSD_GROUPS, SSD_HPG)
    a = -jnp.exp(a_log.astype(jnp.float32)).reshape(SSD_GROUPS, SSD_HPG)
    return ssd_scan(xs, dt, a, bm, cm, h0, want_y)


def rglru_scan(xr, wa, ba, wx, bx, lam, h0):
    f32 = jnp.float32
    bsz, seq, w = xr.shape
    xb = xr.reshape(bsz, seq, LRU_BLOCKS, LRU_BLOCK_DIM)
    r = jax.nn.sigmoid(jnp.einsum("blni,nij->blnj", xb, wa.astype(f32)).reshape(bsz, seq, w) + ba.astype(f32))
    i = jax.nn.sigmoid(jnp.einsum("blni,nij->blnj", xb, wx.astype(f32)).reshape(bsz, seq, w) + bx.astype(f32))
    log_a = -LRU_C * r * jax.nn.softplus(-lam.astype(f32))
    a = jnp.exp(log_a)
    b = jnp.sqrt(-jnp.expm1(2.0 * log_a)) * (i * xr)
    b = b.at[:, 0].add(a[:, 0] * h0)
    _, h = lax.associative_scan(lambda lft, rgt: (lft[0] * rgt[0], rgt[0] * lft[1] + rgt[1]), (a, b), axis=1)
    return h, h[:, -1]


def even_stream(xbc, dt_raw, xl, z, gate, init, a_log, dt_bias, d_skip, ssd_g,
                wa, ba, wx, bx, lam, want_y):
    f32 = jnp.float32
    bsz, seq, _ = xbc.shape
    xl32 = xl.astype(f32)
    y_ssd, y_lru = 0.0, 0.0
    fin_ssd, fin_lru = [], []
    for d in range(2):
        rev = d == 1
        ys, s_fin = ssd_direction(flip_seq(xbc, rev),
                                  flip_seq(dt_raw[..., d * SSD_HEADS:(d + 1) * SSD_HEADS], rev),
                                  a_log[d], dt_bias[d], init[0][d], want_y)
        hl, l_fin = rglru_scan(flip_seq(xl32, rev), wa[d], ba[d], wx[d], bx[d], lam[d], init[1][d])
        fin_ssd.append(s_fin)
        fin_lru.append(l_fin)
        if want_y:
            y_ssd = y_ssd + flip_seq(ys, rev)
            y_lru = y_lru + flip_seq(hl, rev)
    finals = (fin_ssd, fin_lru)
    if not want_y:
        return None, finals
    xs = xbc[..., :SSD_INNER].reshape(bsz, seq, SSD_GROUPS, SSD_HPG, SSD_HEAD_DIM).astype(f32)
    y_ssd = (y_ssd + d_skip.astype(f32).reshape(SSD_GROUPS, SSD_HPG, 1) * xs).reshape(bsz, seq, SSD_INNER)
    gated = (y_ssd * jax.nn.silu(z.astype(f32))).reshape(bsz, seq, SSD_GROUPS, SSD_INNER // SSD_GROUPS)
    y_ssd = rms_norm(gated, ssd_g.reshape(SSD_GROUPS, -1)).reshape(bsz, seq, SSD_INNER)
    y_lru = y_lru * jax.nn.gelu(gate.astype(f32))
    return jnp.concatenate([y_ssd, y_lru], axis=-1).astype(xbc.dtype), finals


def even_mixer(hc, hx, w_in, conv_w, conv_b, a_log, dt_bias, d_skip, ssd_g,
               lconv_w, lconv_b, wa, ba, wx, bx, lam, w_out, want_ctx):
    bsz = hx.shape[0]

    def prep(h, cols):
        proj = h @ w_in[:, :cols]
        xbc = jax.nn.silu(dw_conv(proj[..., :SSD_XBC], conv_w, conv_b))
        dt_raw = proj[..., SSD_XBC:SSD_XBC + 2 * SSD_HEADS]
        xl = dw_conv(proj[..., SSD_XBC + 2 * SSD_HEADS:STATE_COLS], lconv_w, lconv_b)
        z = proj[..., STATE_COLS:STATE_COLS + SSD_INNER] if cols == EVEN_IN else None
        gate = proj[..., STATE_COLS + SSD_INNER:] if cols == EVEN_IN else None
        return xbc, dt_raw, xl, z, gate

    params = (a_log, dt_bias, d_skip, ssd_g, wa, ba, wx, bx, lam)
    z_ssd = jnp.zeros((bsz, SSD_GROUPS, SSD_HPG, SSD_HEAD_DIM, SSD_STATE), jnp.float32)
    z_lru = jnp.zeros((bsz, LRU_WIDTH), jnp.float32)
    mix_c, ctx_states = even_stream(*prep(hc, EVEN_IN if want_ctx else STATE_COLS),
                                    ([z_ssd, z_ssd], [z_lru, z_lru]), *params, want_ctx)
    mix_x, _ = even_stream(*prep(hx, EVEN_IN), ctx_states, *params, True)
    y_ctx = mix_c @ w_out if want_ctx else None
    return y_ctx, mix_x @ w_out


def na_mixer(hc, hx, w_qkv, q_g, k_g, rpb, w_o, want_ctx):
    f32 = jnp.float32
    bsz, seq, _ = hx.shape
    nh, hd = NA_HEADS, NA_HEAD_DIM
    scale = hd ** -0.5
    kv_c = (hc @ w_qkv[:, NA_DIM:]).reshape(bsz, -1, 2, nh, hd)
    kc = rms_norm(kv_c[:, :, 0], k_g)
    vc = kv_c[:, :, 1]
    qkv = (hx @ w_qkv).reshape(bsz, seq, 3, nh, hd)
    rows = seq // GRID_W
    kh = min(NA_KH, rows)
    qg = rms_norm(qkv[:, :, 0], q_g).reshape(bsz, rows, GRID_W, nh, hd)
    kg = rms_norm(qkv[:, :, 1], k_g).reshape(bsz, rows, GRID_W, nh, hd)
    vg = qkv[:, :, 2].reshape(bsz, rows, GRID_W, nh, hd)
    row_start = jnp.clip(jnp.arange(rows) - kh // 2, 0, rows - kh)
    col = jnp.arange(GRID_W)
    col_start = jnp.clip(col - NA_KW // 2, 0, GRID_W - NA_KW)
    col_mask = (col[None, :] >= col_start[:, None]) & (col[None, :] < col_start[:, None] + NA_KW)
    dc_idx = jnp.clip(col[None, :] - col[:, None] + NA_KW - 1, 0, 2 * NA_KW - 2)
    n_loc = kh * GRID_W

    def row_block(r):
        rs = row_start[r]
        q = lax.dynamic_index_in_dim(qg, r, axis=1, keepdims=False)
        ks = lax.dynamic_slice_in_dim(kg, rs, kh, axis=1)
        vs = lax.dynamic_slice_in_dim(vg, rs, kh, axis=1)
        dr_idx = rs + jnp.arange(kh) - r + NA_KH - 1
        bias = jnp.transpose(rpb[:, dr_idx][:, :, dc_idx], (0, 2, 1, 3)).astype(f32)
        s_loc = jnp.einsum("bqhd,bikhd->bhqik", q, ks).astype(f32) * scale + bias
        s_loc = jnp.where(col_mask[:, None, :], s_loc, NEG_INF)
        s_ctx = jnp.einsum("bqhd,bchd->bhqc", q, kc).astype(f32) * scale
        p = jax.nn.softmax(jnp.concatenate([s_loc.reshape(bsz, nh, GRID_W, n_loc), s_ctx], axis=-1), axis=-1)
        p = p.astype(vs.dtype)
        out = jnp.einsum("bhqik,bikhd->bqhd", p[..., :n_loc].reshape(bsz, nh, GRID_W, kh, GRID_W), vs)
        return out + jnp.einsum("bhqc,bchd->bqhd", p[..., n_loc:], vc)

    o = lax.map(row_block, jnp.arange(rows))
    y_lat = jnp.moveaxis(o, 0, 1).reshape(bsz, seq, NA_DIM) @ w_o
    y_ctx = None
    if want_ctx:
        qc = rms_norm((hc @ w_qkv[:, :NA_DIM]).reshape(bsz, -1, nh, hd), q_g)
        pc = jax.nn.softmax(jnp.einsum("bqhd,bkhd->bhqk", qc, kc).astype(f32) * scale, axis=-1)
        y_ctx = jnp.einsum("bhqk,bkhd->bqhd", pc.astype(vc.dtype), vc).reshape(bsz, -1, NA_DIM) @ w_o
    return y_ctx, y_lat


def peer(tokens, w_q, sub_keys, u, v):
    t_all, d = tokens.shape
    k = PEER_TOPK

    def block(xb):
        q = (xb @ w_q).reshape(-1, PEER_HEADS, 2, PEER_QDIM // 2)
        s = jnp.einsum("thzd,hzkd->thzk", q, sub_keys).astype(jnp.float32)
        s_top, i_top = lax.top_k(s, k)
        cand = (s_top[:, :, 0, :, None] + s_top[:, :, 1, None, :]).reshape(-1, PEER_HEADS, k * k)
        cand_idx = (i_top[:, :, 0, :, None] * PEER_KEYS + i_top[:, :, 1, None, :]).reshape(-1, PEER_HEADS, k * k)
        best, pos = lax.top_k(cand, k)
        expert = jnp.take_along_axis(cand_idx, pos, axis=-1)
        gate = jax.nn.softmax(best, axis=-1)
        act = jax.nn.gelu(jnp.einsum("td,thkd->thk", xb, u[expert]).astype(jnp.float32))
        return jnp.einsum("thk,thkd->td", (gate * act).astype(v.dtype), v[expert])

    return lax.map(block, tokens.reshape(t_all // PEER_BLOCK, PEER_BLOCK, d)).reshape(t_all, d)


def setup_inputs(seed: int = 0) -> dict:
    key = jax.random.key(seed)
    ks = iter(jax.random.split(key, 48))
    f32 = jnp.float32
    d = D_MODEL

    def nrm(shape, scale):
        return jax.random.normal(next(ks), shape, f32) * scale

    def unif(shape, lo, hi):
        return jax.random.uniform(next(ks), shape, f32, minval=lo, maxval=hi)

    dt0 = jnp.exp(unif((N_EVEN, 2, SSD_HEADS), math.log(1e-3), math.log(1e-1)))
    a_root = unif((N_EVEN, 2, LRU_WIDTH), 0.9, 0.999) ** (1.0 / LRU_C)
    return {
        "x": nrm((BATCH, SEQ, d), 1.0),
        "c": nrm((BATCH, d), 1.0),
        "ctx": nrm((BATCH, CTX_LEN, d), 1.0),
        "c_ctx": nrm((d,), 1.0),
        "ada_w": nrm((DEPTH, d, 6 * d), 0.5 * d ** -0.5),
        "ada_b": nrm((DEPTH, 6 * d), 0.02),
        "norm1_g": 1.0 + nrm((DEPTH, d), 0.05),
        "norm2_g": 1.0 + nrm((DEPTH, d), 0.05),
        "ev_w_in": nrm((N_EVEN, d, EVEN_IN), d ** -0.5),
        "ev_conv_w": nrm((N_EVEN, SSD_CONV, SSD_XBC), SSD_CONV ** -0.5),
        "ev_conv_b": nrm((N_EVEN, SSD_XBC), 0.02),
        "ev_a_log": jnp.log(unif((N_EVEN, 2, SSD_HEADS), 1.0, 16.0)),
        "ev_dt_bias": dt0 + jnp.log(-jnp.expm1(-dt0)),
        "ev_d": 1.0 + nrm((N_EVEN, SSD_HEADS), 0.1),
        "ev_ssd_norm_g": 1.0 + nrm((N_EVEN, SSD_INNER), 0.05),
        "ev_lru_conv_w": nrm((N_EVEN, LRU_CONV, LRU_WIDTH), LRU_CONV ** -0.5),
        "ev_lru_conv_b": nrm((N_EVEN, LRU_WIDTH), 0.02),
        "ev_lru_wa": nrm((N_EVEN, 2, LRU_BLOCKS, LRU_BLOCK_DIM, LRU_BLOCK_DIM), LRU_BLOCK_DIM ** -0.5),
        "ev_lru_ba": nrm((N_EVEN, 2, LRU_WIDTH), 0.02),
        "ev_lru_wx": nrm((N_EVEN, 2, LRU_BLOCKS, LRU_BLOCK_DIM, LRU_BLOCK_DIM), LRU_BLOCK_DIM ** -0.5),
        "ev_lru_bx": nrm((N_EVEN, 2, LRU_WIDTH), 0.02),
        "ev_lru_lam": jnp.log(a_root) - jnp.log1p(-a_root),
        "ev_w_out": nrm((N_EVEN, EVEN_MIX, d), EVEN_MIX ** -0.5),
        "od_w_qkv": nrm((N_ODD, d, 3 * NA_DIM), d ** -0.5),
        "od_q_norm_g": 1.0 + nrm((N_ODD, NA_HEAD_DIM), 0.05),
        "od_k_norm_g": 1.0 + nrm((N_ODD, NA_HEAD_DIM), 0.05),
        "od_rpb": nrm((N_ODD, NA_HEADS, 2 * NA_KH - 1, 2 * NA_KW - 1), 0.1),
        "od_w_o": nrm((N_ODD, NA_DIM, d), NA_DIM ** -0.5),
        "pe_w_q": nrm((DEPTH, d, PEER_HEADS * PEER_QDIM), d ** -0.5),
        "pe_keys": nrm((DEPTH, PEER_HEADS, 2, PEER_KEYS, PEER_QDIM // 2), (PEER_QDIM // 2) ** -0.5),
        "pe_u": nrm((DEPTH, PEER_EXPERTS, d), d ** -0.5),
        "pe_v": nrm((DEPTH, PEER_EXPERTS, d), PEER_HEADS ** -0.5),
    }


def reference(x, c, ctx, c_ctx, ada_w, ada_b, norm1_g, norm2_g,
              ev_w_in, ev_conv_w, ev_conv_b, ev_a_log, ev_dt_bias, ev_d, ev_ssd_norm_g,
              ev_lru_conv_w, ev_lru_conv_b, ev_lru_wa, ev_lru_ba, ev_lru_wx, ev_lru_bx, ev_lru_lam,
              ev_w_out, od_w_qkv, od_q_norm_g, od_k_norm_g, od_rpb, od_w_o,
              pe_w_q, pe_keys, pe_u, pe_v):
    bsz, seq, d = x.shape
    silu_c = jax.nn.silu(c)
    silu_cc = jax.nn.silu(c_ctx)
    for layer in range(DEPTH):
        last = layer == DEPTH - 1
        j = layer // 2
        mod_x = (silu_c @ ada_w[layer] + ada_b[layer])[:, None, :]
        sh1, sc1, g1, sh2, sc2, g2 = jnp.split(mod_x, 6, axis=-1)
        n_mod = 2 if last else 6
        mod_c = silu_cc @ ada_w[layer][:, :n_mod * d] + ada_b[layer][:n_mod * d]
        mc = jnp.split(mod_c, n_mod)

        hx = modulate(rms_norm(x, norm1_g[layer]), sh1, sc1)
        hc = modulate(rms_norm(ctx, norm1_g[layer]), mc[0], mc[1])
        if layer % 2 == 0:
            y_ctx, y_lat = even_mixer(hc, hx, ev_w_in[j], ev_conv_w[j], ev_conv_b[j], ev_a_log[j],
                                      ev_dt_bias[j], ev_d[j], ev_ssd_norm_g[j], ev_lru_conv_w[j],
                                      ev_lru_conv_b[j], ev_lru_wa[j], ev_lru_ba[j], ev_lru_wx[j],
                                      ev_lru_bx[j], ev_lru_lam[j], ev_w_out[j], not last)
        else:
            y_ctx, y_lat = na_mixer(hc, hx, od_w_qkv[j], od_q_norm_g[j], od_k_norm_g[j],
                                    od_rpb[j], od_w_o[j], not last)
        x = x + g1 * y_lat

        hx = modulate(rms_norm(x, norm2_g[layer]), sh2, sc2)
        if last:
            x = x + g2 * peer(hx.reshape(-1, d), pe_w_q[layer], pe_keys[layer],
                              pe_u[layer], pe_v[layer]).reshape(bsz, seq, d)
        else:
            ctx = ctx + mc[2] * y_ctx
            hc = modulate(rms_norm(ctx, norm2_g[layer]), mc[3], mc[4])
            tok = jnp.concatenate([hx.reshape(-1, d), hc.reshape(-1, d)], axis=0)
            out = peer(tok, pe_w_q[layer], pe_keys[layer], pe_u[layer], pe_v[layer])
            x = x + g2 * out[:bsz * seq].reshape(bsz, seq, d)
            ctx = ctx + mc[5] * out[bsz * seq:].reshape(ctx.shape)
    return x
```

```python
import contextlib
import numpy as np
import ml_dtypes
import concourse.bass as bass
import concourse.mybir as mybir
from concourse.ap import AP
from concourse.bass_utils import run_bass_kernel_spmd

F32 = mybir.dt.float32
BF16 = mybir.dt.bfloat16
U32 = mybir.dt.uint32
AF = mybir.ActivationFunctionType
ALU = mybir.AluOpType
AX = mybir.AxisListType

ENGS = ("pe", "dve", "act", "pool", "sp")
NEG = -1.0e30


class Sched:
    NSLOT = 24
    ROT = 20000

    def __init__(self, nc):
        self.nc = nc
        self.ops = []
        self.eng = {"pe": nc.tensor, "dve": nc.vector, "act": nc.scalar,
                    "pool": nc.gpsimd, "sp": nc.sync}

    def op(self, eng, emit, reads=(), writes=(), dma=False):
        self.ops.append(dict(eng=eng, emit=emit, reads=tuple(reads),
                             writes=tuple(writes), dma=dma, bar=False))

    def dma(self, q, out, in_, reads, writes, **kw):
        self.op(q, lambda e: e.dma_start(out=out, in_=in_, **kw), reads, writes, dma=True)

    def barrier(self):
        self.ops.append(dict(bar=True))

    def finalize(self):
        nc = self.nc
        raw = self.ops
        ops = []
        last_w, readers = {}, {}
        slot_last = [None] * self.NSLOT
        nslot = 0
        last_on = {}
        pending_bar = {}
        for o in raw:
            if o["bar"]:
                extra = set(last_on.values()) | {s for s in slot_last if s is not None}
                for e in ENGS:
                    pending_bar[e] = set(extra) | pending_bar.get(e, set())
                continue
            i = len(ops)
            ops.append(o)
            d = set()
            for r in o["reads"]:
                if r in last_w:
                    d.add(last_w[r])
            for w in o["writes"]:
                if w in last_w:
                    d.add(last_w[w])
                d.update(readers.get(w, ()))
            if o["dma"]:
                s = nslot % self.NSLOT
                nslot += 1
                o["slot"] = s
                if slot_last[s] is not None:
                    d.add(slot_last[s])
                slot_last[s] = i
            if o["eng"] in pending_bar:
                d.update(pending_bar.pop(o["eng"]))
            d.discard(i)
            if o["eng"] == "pe" and not o["dma"]:
                d = {j for j in d if not (ops[j]["eng"] == "pe" and not ops[j]["dma"])}
            o["deps"] = d
            for w in o["writes"]:
                last_w[w] = i
                readers[w] = []
            for r in o["reads"]:
                if r not in o["writes"]:
                    readers.setdefault(r, []).append(i)
            last_on[o["eng"]] = i
        n = len(ops)
        signal = [False] * n
        for o in ops:
            for j in o["deps"]:
                signal[j] = True
        cnt = {e: 0 for e in ENGS}
        slot_cnt = [0] * self.NSLOT
        for i, o in enumerate(ops):
            if o["dma"]:
                slot_cnt[o["slot"]] += 1
                o["sig"] = ("slot", o["slot"], 16 * slot_cnt[o["slot"]])
            elif signal[i]:
                e = o["eng"]
                o["sig"] = (e, cnt[e] // self.ROT, cnt[e] % self.ROT + 1)
                cnt[e] += 1
            else:
                o["sig"] = None
        sems = {}
        for e in ENGS:
            for k in range(max((cnt[e] + self.ROT - 1) // self.ROT, 1)):
                sems[(e, k)] = nc.alloc_semaphore(f"s_{e}_{k}")
        for s in range(self.NSLOT):
            sems[("slot", s)] = nc.alloc_semaphore(f"s_dma_{s}")
        self.sems = sems
        waited = {e: {} for e in ENGS}
        for o in ops:
            e = o["eng"]
            eobj = self.eng[e]
            need = {}
            for j in o["deps"]:
                sg = ops[j]["sig"]
                key = (sg[0], sg[1])
                need[key] = max(need.get(key, 0), sg[2])
            for key, v in need.items():
                if waited[e].get(key, 0) >= v:
                    continue
                eobj.wait_ge(sems[key], v)
                waited[e][key] = v
            ins = o["emit"](eobj)
            sg = o["sig"]
            if sg is not None:
                if sg[0] == "slot":
                    ins.then_inc(sems[("slot", sg[1])], 16)
                else:
                    ins.then_inc(sems[(sg[0], sg[1])], 1)
        fin = {}
        for o in ops:
            if o["dma"]:
                fin[o["slot"]] = o["sig"][2]
        for s, v in fin.items():
            self.eng["sp"].wait_ge(sems[("slot", s)], v)
        self.ops = ops
        return n


def vw(base, off, dims):
    return AP(base.tensor, base.offset + off, [list(base.ap[0])] + [list(d) for d in dims])


class Ctx:
    pass


_UNIQ = [0]


def sb(C, st, name, shape, dt):
    _UNIQ[0] += 1
    return st.enter_context(C.nc.sbuf_tensor(f"{name}_{_UNIQ[0]}", shape, dt)).ap()


def peer_route(C, hT_d, wq_d, keysT_d, rout_d, NT):
    S, nc, ps = C.S, C.nc, C.ps
    S.barrier()
    with contextlib.ExitStack() as st:
        wq = sb(C, st, "pr_wq", [128, 8, 2048], BF16)
        wst = sb(C, st, "pr_wst", [128, 8, 512], F32)
        keys = sb(C, st, "pr_keys", [128, 16, 128], F32)
        hT = [sb(C, st, f"pr_hT{i}", [128, 8, 512], BF16) for i in range(2)]
        qTs = [sb(C, st, f"pr_qT{i}", [128, 16, 128], F32) for i in range(2)]
        scs = [sb(C, st, f"pr_sc{i}", [128, 2048], F32) for i in range(2)]
        tmp16 = sb(C, st, "pr_tmp16", [128, 16, 128], F32)
        tmp8 = sb(C, st, "pr_tmp8", [128, 8, 256], F32)
        stop = sb(C, st, "pr_stop", [128, 16, 16], F32)
        itop = sb(C, st, "pr_itop", [128, 16, 16], U32)
        itf = sb(C, st, "pr_itf", [128, 16, 16], F32)
        cand = sb(C, st, "pr_cand", [128, 8, 256], F32)
        best = sb(C, st, "pr_best", [128, 8, 16], F32)
        pos = sb(C, st, "pr_pos", [128, 8, 16], U32)
        posf = sb(C, st, "pr_posf", [128, 128], F32)
        av = sb(C, st, "pr_av", [128, 128], F32)
        bv = sb(C, st, "pr_bv", [128, 128], F32)
        eq = sb(C, st, "pr_eq", [128, 2048], F32)
        sel = sb(C, st, "pr_sel", [128, 3, 128], F32)
        zs = sb(C, st, "pr_zs", [128, 8], F32)
        selT = [sb(C, st, f"pr_selT{i}", [128, 3, 128], F32) for i in range(2)]
        thr = sb(C, st, "pr_thr", [128, 15], F32)
        wqv = wq_d.rearrange("(k p) f -> p k f", p=128)
        for c in range(4):
            S.dma("sp", wst, wqv[:, :, c * 512:(c + 1) * 512], ["wq_d"], ["pr_wst"])
            S.op("act", lambda e, c=c: e.activation(out=wq[:, :, c * 512:(c + 1) * 512], in_=wst, func=AF.Copy),
                 ["pr_wst"], ["pr_wq"])
        S.dma("sp", keys, keysT_d, ["keys_d"], ["pr_keys"])
        S.op("dve", lambda e: e.tensor_scalar(out=thr, in0=C.iota[:, 1:16], scalar1=16.0, scalar2=None, op0=ALU.mult), ["cst"], ["pr_thr"])
        hv = hT_d.rearrange("(k p) t -> p k t", p=128)
        rv = rout_d.rearrange("a s t -> s a t")
        STA = [f"pr_st{i}a" for i in range(16)]
        STB = [f"pr_st{i}b" for i in range(16)]
        ITA = [f"pr_it{i}a" for i in range(16)]
        ITB = [f"pr_it{i}b" for i in range(16)]
        BSA = [f"pr_bs{i}a" for i in range(8)]
        BSB = [f"pr_bs{i}b" for i in range(8)]
        PSA = [f"pr_ps{i}a" for i in range(8)]
        PSB = [f"pr_ps{i}b" for i in range(8)]
        nt = 0
        for blk in range(NT // 512):
            hT_ = hT[blk % 2]
            hk = f"pr_hT{blk % 2}"
            S.dma("sp", hT_, hv[:, :, blk * 512:(blk + 1) * 512], ["hT_d"], [hk])
            for tt in range(4):
                t0 = tt * 128
                qT, sc = qTs[nt % 2], scs[nt % 2]
                qk, sk = f"pr_qT{nt % 2}", f"pr_sc{nt % 2}"
                selT_ = selT[nt % 2]
                stk = f"pr_selT{nt % 2}"
                nt += 1
                for qc in range(16):
                    b = qc // 4
                    for k in range(8):
                        S.op("pe", lambda e, qc=qc, k=k, b=b, t0=t0, hT_=hT_: e.matmul(
                            ps[b][:, (qc % 4) * 128:(qc % 4 + 1) * 128], lhsT=wq[:, k, qc * 128:(qc + 1) * 128],
                            rhs=hT_[:, k, t0:t0 + 128], start=(k == 0), stop=(k == 7)),
                            ["pr_wq", hk], [f"ps{b}"])
                for b in range(4):
                    S.op("act", lambda e, b=b, qT=qT: e.activation(out=qT[:, 4 * b:4 * b + 4, :], in_=ps[b].rearrange("p (a t) -> p a t", a=4), func=AF.Copy),
                         [f"ps{b}"], [qk])
                for hz in range(16):
                    b = 4 + hz // 4
                    S.op("pe", lambda e, hz=hz, b=b, qT=qT: e.matmul(
                        ps[b][:, (hz % 4) * 128:(hz % 4 + 1) * 128], lhsT=qT[:, hz, :], rhs=keys[:, hz, :],
                        start=True, stop=True), [qk, "pr_keys"], [f"ps{b}"])
                for b in range(4):
                    S.op("act", lambda e, b=b, sc=sc: e.activation(out=sc[:, b * 512:(b + 1) * 512], in_=ps[4 + b], func=AF.Copy),
                         [f"ps{4 + b}"], [sk])
                G16 = range(16)
                for hz in G16:
                    S.op("dve", lambda e, hz=hz, sc=sc: e.max(out=stop[:, hz, 0:8], in_=sc[:, hz * 128:(hz + 1) * 128]), [sk], [STA[hz]])
                for hz in G16:
                    S.op("dve", lambda e, hz=hz, sc=sc: e.max_index(out=itop[:, hz, 0:8], in_max=stop[:, hz, 0:8], in_values=sc[:, hz * 128:(hz + 1) * 128]),
                         [sk, STA[hz]], [ITA[hz]])
                for hz in G16:
                    S.op("dve", lambda e, hz=hz, sc=sc: e.match_replace(out=tmp16[:, hz, :], in_to_replace=stop[:, hz, 0:8], in_values=sc[:, hz * 128:(hz + 1) * 128], imm_value=NEG),
                         [sk, STA[hz]], [f"pr_tm{hz}"])
                for hz in G16:
                    S.op("dve", lambda e, hz=hz: e.max(out=stop[:, hz, 8:16], in_=tmp16[:, hz, :]), [f"pr_tm{hz}"], [STB[hz]])
                for hz in G16:
                    S.op("dve", lambda e, hz=hz: e.max_index(out=itop[:, hz, 8:16], in_max=stop[:, hz, 8:16], in_values=tmp16[:, hz, :]),
                         [f"pr_tm{hz}", STB[hz]], [ITB[hz]])
                S.op("dve", lambda e: e.tensor_copy(out=itf, in_=itop), ITA + ITB, ["pr_itf"])
                in0 = vw(stop, 0, [[32, 8], [1, 16], [0, 16]])
                in1 = vw(stop, 16, [[32, 8], [0, 16], [1, 16]])
                S.op("dve", lambda e, in0=in0, in1=in1: e.tensor_tensor(out=cand.rearrange("p h (a b) -> p h a b", a=16), in0=in0, in1=in1, op=ALU.add),
                     STA + STB, ["pr_cand"])
                G8 = range(8)
                for h in G8:
                    S.op("dve", lambda e, h=h: e.max(out=best[:, h, 0:8], in_=cand[:, h, :]), ["pr_cand"], [BSA[h]])
                for h in G8:
                    S.op("dve", lambda e, h=h: e.max_index(out=pos[:, h, 0:8], in_max=best[:, h, 0:8], in_values=cand[:, h, :]), ["pr_cand", BSA[h]], [PSA[h]])
                for h in G8:
                    S.op("dve", lambda e, h=h: e.match_replace(out=tmp8[:, h, :], in_to_replace=best[:, h, 0:8], in_values=cand[:, h, :], imm_value=NEG),
                         ["pr_cand", BSA[h]], [f"pr_t8{h}"])
                for h in G8:
                    S.op("dve", lambda e, h=h: e.max(out=best[:, h, 8:16], in_=tmp8[:, h, :]), [f"pr_t8{h}"], [BSB[h]])
                for h in G8:
                    S.op("dve", lambda e, h=h: e.max_index(out=pos[:, h, 8:16], in_max=best[:, h, 8:16], in_values=tmp8[:, h, :]), [f"pr_t8{h}", BSB[h]], [PSB[h]])
                S.op("dve", lambda e: e.tensor_copy(out=posf, in_=pos.rearrange("p h k -> p (h k)")), PSA + PSB, ["pr_posf"])
                S.op("dve", lambda e: e.tensor_tensor(out=eq[:, 0:1920].rearrange("p (s m) -> p s m", m=15), in0=vw(posf, 0, [[1, 128], [0, 15]]),
                                                      in1=vw(thr, 0, [[0, 128], [1, 15]]), op=ALU.is_ge), ["pr_posf", "pr_thr"], ["pr_eq"])
                S.op("dve", lambda e: e.tensor_reduce(out=av, in_=eq[:, 0:1920].rearrange("p (s m) -> p s m", m=15), axis=AX.X, op=ALU.add), ["pr_eq"], ["pr_av"])
                S.op("dve", lambda e: e.scalar_tensor_tensor(out=bv, in0=av, scalar=-16.0, in1=posf, op0=ALU.mult, op1=ALU.add),
                     ["pr_posf", "pr_av"], ["pr_bv"])
                iota16 = vw(C.iota, 0, [[0, 8], [0, 16], [1, 16]])
                eq4 = eq.rearrange("p (h k a) -> p h k a", h=8, k=16)
                for z, src_ab in ((0, av), (1, bv)):
                    abb = vw(src_ab, 0, [[16, 8], [1, 16], [0, 16]])
                    itb = vw(itf, 16 * z, [[32, 8], [0, 16], [1, 16]])
                    S.op("dve", lambda e, abb=abb: e.tensor_tensor(out=eq4, in0=abb, in1=iota16, op=ALU.is_equal),
                         ["pr_av", "pr_bv"], ["pr_eq"])
                    S.op("dve", lambda e, itb=itb: e.tensor_tensor(out=eq4, in0=eq4, in1=itb, op=ALU.mult),
                         ["pr_eq", "pr_itf"], ["pr_eq"])
                    S.op("dve", lambda e, z=z: e.tensor_reduce(out=sel[:, z, :], in_=eq.rearrange("p (s a) -> p s a", a=16), axis=AX.X, op=ALU.add),
                         ["pr_eq"], ["pr_sel"])
                g3 = sel[:, 2, :].rearrange("p (h k) -> p h k", h=8)
                b0 = vw(best, 0, [[16, 8], [0, 16]])
                S.op("dve", lambda e: e.tensor_tensor(out=g3, in0=best, in1=b0, op=ALU.subtract), BSA + BSB, ["pr_sel"])
                S.op("act", lambda e: e.activation(out=sel[:, 2, :], in_=sel[:, 2, :], func=AF.Exp), ["pr_sel"], ["pr_sel"])
                S.op("dve", lambda e: e.tensor_reduce(out=zs, in_=g3, axis=AX.X, op=ALU.add), ["pr_sel"], ["pr_zs"])
                S.op("dve", lambda e: e.reciprocal(out=zs, in_=zs), ["pr_zs"], ["pr_zs"])
                zb = vw(zs, 0, [[1, 8], [0, 16]])
                S.op("dve", lambda e: e.tensor_tensor(out=g3, in0=g3, in1=zb, op=ALU.mult), ["pr_sel", "pr_zs"], ["pr_sel"])
                for a in range(3):
                    S.op("pe", lambda e, a=a: e.transpose(out=ps[0][:, a * 128:(a + 1) * 128], in_=sel[:, a, :], identity=C.ident),
                         ["pr_sel"], ["ps0"])
                S.op("act", lambda e, selT_=selT_: e.activation(out=selT_, in_=ps[0][:, 0:384].rearrange("p (a t) -> p a t", a=3), func=AF.Copy),
                     ["ps0"], [stk])
                c0 = blk * 512 + t0
                S.dma("sp", rv[:, :, c0:c0 + 128], selT_, [stk], ["rout_d"])
    S.barrier()


def peer_expert(C, hT_d, rout_d, uT_d, v_d, xT_d, g2cols, NT, seg_of_col):
    S, nc, ps = C.S, C.nc, C.ps
    TB = 512
    SBT = 16
    NST = 4
    S.barrier()
    with contextlib.ExitStack() as st:
        hT = sb(C, st, "px_hT", [128, 8, TB], BF16)
        rt = sb(C, st, "px_rt", [128, 3, TB], F32)
        A = [sb(C, st, f"px_A{i}", [128, SBT, 128], BF16) for i in range(2)]
        B = [sb(C, st, f"px_B{i}", [128, SBT, 128], BF16) for i in range(2)]
        G = sb(C, st, "px_G", [128, 128, TB], BF16)
        wt = [sb(C, st, f"px_w{i}", [128, 1024], BF16) for i in range(3)]
        stg = [sb(C, st, f"px_s{i}", [128, 1024], F32) for i in range(NST)]
        gl = [sb(C, st, f"px_gl{i}", [128, TB], BF16) for i in range(2)]
        xt = [sb(C, st, f"px_x{i}", [128, TB], F32) for i in range(2)]
        hv = hT_d.rearrange("(k p) t -> p k t", p=128)
        rv = rout_d.rearrange("a s t -> s a t")
        vv = v_d.rearrange("(i j) d -> j i d", j=128)
        xv = xT_d.rearrange("(k p) t -> k p t", p=128)
        nld = 0
        for blk in range(NT // TB):
            c0 = blk * TB
            S.dma("sp", hT, hv[:, :, c0:c0 + TB], ["hT_d"], ["px_hT"])
            S.dma("sp", rt, rv[:, :, c0:c0 + TB], ["rout_d"], ["px_rt"])
            def onehots(sbi):
                tb0 = sbi * SBT
                Ab, Bb = A[sbi % 2], B[sbi % 2]
                i1b = vw(rt, tb0, [[1, SBT], [0, 128]])
                i2b = vw(rt, TB + tb0, [[1, SBT], [0, 128]])
                gb = vw(rt, 2 * TB + tb0, [[1, SBT], [0, 128]])
                io = vw(C.iota, 0, [[0, SBT], [1, 128]])
                S.op("dve", lambda e, Ab=Ab, i1b=i1b, io=io: e.tensor_tensor(out=Ab, in0=i1b, in1=io, op=ALU.is_equal),
                     ["px_rt"], [f"px_A{sbi % 2}"])
                S.op("dve", lambda e, Bb=Bb, i2b=i2b, io=io: e.tensor_tensor(out=Bb, in0=i2b, in1=io, op=ALU.is_equal),
                     ["px_rt"], [f"px_B{sbi % 2}"])
                S.op("pool", lambda e, Bb=Bb, gb=gb: e.tensor_tensor(out=Bb, in0=Bb, in1=gb, op=ALU.mult),
                     ["px_rt", f"px_B{sbi % 2}"], [f"px_B{sbi % 2}"])

            nsb = TB // SBT
            onehots(0)
            for sbi in range(nsb):
                if sbi + 1 < nsb:
                    onehots(sbi + 1)
                tb0 = sbi * SBT
                Ab, Bb = A[sbi % 2], B[sbi % 2]
                for q4 in range(SBT // 4):
                    pb = 4 + (sbi * (SBT // 4) + q4) % 2
                    for t in range(4):
                        tl = q4 * 4 + t
                        S.op("pe", lambda e, Ab=Ab, Bb=Bb, tl=tl, t=t, pb=pb: e.matmul(
                            ps[pb][:, t * 128:(t + 1) * 128], lhsT=Ab[:, tl, :], rhs=Bb[:, tl, :], start=True, stop=True),
                            [f"px_A{sbi % 2}", f"px_B{sbi % 2}"], [f"ps{pb}"])
                    tg = tb0 + q4 * 4
                    outv = vw(G, tg, [[TB, 128], [1, 4]])
                    inv = ps[pb].rearrange("p (t j) -> p j t", t=4)
                    S.op("act", lambda e, outv=outv, inv=inv: e.activation(out=outv, in_=inv, func=AF.Copy), [f"ps{pb}"], ["px_G"])
            ubuf = {}
            for jx in range(128 + 2):
                if jx < 128:
                    j = jx
                    s_, w_ = stg[nld % NST], wt[nld % 3]
                    sk, wk = f"px_s{nld % NST}", f"px_w{nld % 3}"
                    nld += 1
                    S.dma("sp", s_, uT_d[j].rearrange("p k i -> p (k i)"), ["uT_d"], [sk])
                    if j % 2 == 0:
                        S.op("act", lambda e, s_=s_, w_=w_: e.activation(out=w_, in_=s_, func=AF.Copy), [sk], [wk])
                    else:
                        S.op("dve", lambda e, s_=s_, w_=w_: e.tensor_copy(out=w_, in_=s_), [sk], [wk])
                    ubuf[j] = (w_, wk)
                j = jx - 2
                if j < 0:
                    continue
                w_, wk = ubuf.pop(j)
                pb = 6 + j % 2
                for k in range(8):
                    S.op("pe", lambda e, w_=w_, k=k, pb=pb: e.matmul(ps[pb], lhsT=w_[:, k * 128:(k + 1) * 128], rhs=hT[:, k, :], start=(k == 0), stop=(k == 7)),
                         [wk, "px_hT"], [f"ps{pb}"])
                g_ = gl[j % 2]
                S.op("act", lambda e, g_=g_, pb=pb: e.activation(out=g_, in_=ps[pb], func=AF.Gelu), [f"ps{pb}"], [f"px_gl{j % 2}"])
                S.op("dve", lambda e, g_=g_, j=j: e.tensor_tensor(out=G[:, j, :], in0=G[:, j, :], in1=g_, op=ALU.mult),
                     [f"px_gl{j % 2}", "px_G"], ["px_G"])
            for j in range(128):
                s_, w_ = stg[nld % NST], wt[nld % 3]
                sk, wk = f"px_s{nld % NST}", f"px_w{nld % 3}"
                nld += 1
                S.dma("sp", s_, vv[j], ["v_d"], [sk])
                if j % 2 == 0:
                    S.op("act", lambda e, s_=s_, w_=w_: e.activation(out=w_, in_=s_, func=AF.Copy), [sk], [wk])
                else:
                    S.op("dve", lambda e, s_=s_, w_=w_: e.tensor_copy(out=w_, in_=s_), [sk], [wk])
                for c in range(8):
                    S.op("pe", lambda e, w_=w_, c=c, j=j: e.matmul(ps[c], lhsT=w_[:, c * 128:(c + 1) * 128], rhs=G[:, j, :],
                                                                  start=(j == 0), stop=(j == 127)),
                         [wk, "px_G"], [f"ps{c}"])
            for ch in range(8):
                x_ = xt[ch % 2]
                S.dma("sp", x_, xv[ch, :, c0:c0 + TB], ["xT_d"], [f"px_x{ch % 2}"])
                for hseg in range(TB // 256):
                    m = seg_of_col(c0 + hseg * 256)
                    sl = slice(hseg * 256, (hseg + 1) * 256)
                    S.op("dve", lambda e, x_=x_, sl=sl, ch=ch, m=m: e.scalar_tensor_tensor(
                        out=x_[:, sl], in0=ps[ch][:, sl], scalar=g2cols(ch, m), in1=x_[:, sl], op0=ALU.mult, op1=ALU.add),
                        [f"ps{ch}", f"px_x{ch % 2}", "mods"], [f"px_x{ch % 2}"])
                S.dma("sp", xv[ch, :, c0:c0 + TB], x_, [f"px_x{ch % 2}"], ["xT_d"])
    S.barrier()


EPS = 1e-6
SEG = 2304


def seg_m(col):
    s, pos = divmod(col, SEG)
    return 2 if pos < 256 else s


def adaln(C, l, adaw_d, adabT_d, ngT_d):
    S, ps = C.S, C.ps
    S.barrier()
    mods, modA = C.mods[l], C.modA[l]
    with contextlib.ExitStack() as st:
        w = [sb(C, st, f"ad_w{i}", [128, 8, 512], F32) for i in range(2)]
        bias = sb(C, st, "ad_b", [128, 48], F32)
        ng = sb(C, st, "ad_ng", [128, 2, 8], F32)
        tmp = sb(C, st, "ad_tmp", [128, 8, 3], F32)
        S.dma("sp", bias, adabT_d, ["adab"], ["ad_b"])
        S.dma("sp", ng, ngT_d, ["ng"], ["ad_ng"])
        wv = adaw_d.rearrange("(k p) f -> p k f", p=128)
        for cb in range(12):
            w_ = w[cb % 2]
            S.dma("sp", w_, wv[:, :, cb * 512:(cb + 1) * 512], ["adaw"], [f"ad_w{cb % 2}"])
            for j in range(4):
                col = (cb * 4 + j) * 3
                for k in range(8):
                    S.op("pe", lambda e, w_=w_, j=j, k=k, col=col: e.matmul(
                        ps[0][:, col:col + 3], lhsT=w_[:, k, j * 128:(j + 1) * 128], rhs=C.scT[:, k, :],
                        start=(k == 0), stop=(k == 7)), [f"ad_w{cb % 2}", "scT"], ["ps0"])
        S.op("dve", lambda e: e.tensor_tensor(out=mods, in0=ps[0][:, 0:144].rearrange("p (j m) -> p j m", m=3),
                                              in1=vw(bias, 0, [[1, 48], [0, 3]]), op=ALU.add), ["ps0", "ad_b"], ["mods"])
        for n, base in ((0, 8), (1, 32)):
            S.op("dve", lambda e, base=base: e.tensor_scalar(out=tmp, in0=mods[:, base:base + 8, :], scalar1=1.0, scalar2=None, op0=ALU.add),
                 ["mods"], ["ad_tmp"])
            S.op("dve", lambda e, n=n: e.tensor_tensor(out=modA[:, n, :, :], in0=tmp, in1=vw(ng, n * 8, [[1, 8], [0, 3]]), op=ALU.mult),
                 ["ad_tmp", "ad_ng"], ["mods"])
    S.barrier()


def norm_mod(C, l, which, xT_d, hT_d, NT):
    S, ps = C.S, C.ps
    S.barrier()
    mods, modA = C.mods[l], C.modA[l]
    shb = 0 if which == 0 else 24
    with contextlib.ExitStack() as st:
        x = [sb(C, st, f"nm_x{i}", [128, 8, 512], F32) for i in range(2)]
        sq = sb(C, st, "nm_sq", [128, 8, 512], F32)
        r = sb(C, st, "nm_r", [128, 512], F32)
        t1 = sb(C, st, "nm_t1", [128, 8, 512], F32)
        h = [sb(C, st, f"nm_h{i}", [128, 8, 512], BF16) for i in range(2)]
        xv = xT_d.rearrange("(k p) t -> p k t", p=128)
        hv = hT_d.rearrange("(k p) t -> p k t", p=128)
        for ti in range(NT // 512):
            c0 = ti * 512
            x_, h_ = x[ti % 2], h[ti % 2]
            S.dma("sp", x_, xv[:, :, c0:c0 + 512], ["xT_d"], [f"nm_x{ti % 2}"])
            S.op("act", lambda e, x_=x_: e.activation(out=sq, in_=x_, func=AF.Square), [f"nm_x{ti % 2}"], ["nm_sq"])
            for k in range(8):
                S.op("pe", lambda e, k=k: e.matmul(ps[1], lhsT=C.onesm, rhs=sq[:, k, :], start=(k == 0), stop=(k == 7)),
                     ["nm_sq", "cst"], ["ps1"])
            S.op("act", lambda e: e.activation(out=r, in_=ps[1], func=AF.Sqrt, bias=C.epsc, scale=1.0), ["ps1", "cst"], ["nm_r"])
            S.op("dve", lambda e: e.reciprocal(out=r, in_=r), ["nm_r"], ["nm_r"])
            for k in range(8):
                for hf in range(2):
                    m = seg_m(c0 + hf * 256)
                    sl = slice(hf * 256, (hf + 1) * 256)
                    S.op("dve", lambda e, x_=x_, k=k, sl=sl, m=m: e.scalar_tensor_tensor(
                        out=t1[:, k, sl], in0=x_[:, k, sl], scalar=modA[:, which, k, m:m + 1], in1=r[:, sl], op0=ALU.mult, op1=ALU.mult),
                        [f"nm_x{ti % 2}", "nm_r", "mods"], ["nm_t1"])
                    S.op("act", lambda e, h_=h_, k=k, sl=sl, m=m: e.activation(
                        out=h_[:, k, sl], in_=t1[:, k, sl], func=AF.Identity, bias=mods[:, shb + k, m:m + 1], scale=1.0),
                        ["nm_t1", "mods"], [f"nm_h{ti % 2}"])
            S.dma("sp", hv[:, :, c0:c0 + 512], h_, [f"nm_h{ti % 2}"], ["hT_d"])
    S.barrier()


def resid_linear(C, l, inT_d, KC, w_d, gbase, xT_d, NT):
    S, ps = C.S, C.ps
    S.barrier()
    mods = C.mods[l]
    with contextlib.ExitStack() as st:
        w = sb(C, st, "rl_w", [128, KC, 1024], BF16)
        wst = sb(C, st, "rl_wst", [128, KC, 256], F32)
        a = [sb(C, st, f"rl_a{i}", [128, KC, 512], BF16) for i in range(2)]
        x = [sb(C, st, f"rl_x{i}", [128, 512], F32) for i in range(2)]
        wv = w_d.rearrange("(k p) f -> p k f", p=128)
        for c in range(4):
            S.dma("sp", wst, wv[:, :, c * 256:(c + 1) * 256], ["w_d"], ["rl_wst"])
            S.op("act", lambda e, c=c: e.activation(out=w[:, :, c * 256:(c + 1) * 256], in_=wst, func=AF.Copy), ["rl_wst"], ["rl_w"])
        av = inT_d.rearrange("(k p) t -> p k t", p=128)
        xv = xT_d.rearrange("(k p) t -> k p t", p=128)
        n = 0
        for ti in range(NT // 512):
            c0 = ti * 512
            a_ = a[ti % 2]
            S.dma("sp", a_, av[:, :, c0:c0 + 512], ["inT_d"], [f"rl_a{ti % 2}"])
            for oc in range(8):
                pb = 2 + n % 2
                x_ = x[n % 2]
                for k in range(KC):
                    S.op("pe", lambda e, a_=a_, k=k, oc=oc, pb=pb: e.matmul(ps[pb], lhsT=w[:, k, oc * 128:(oc + 1) * 128], rhs=a_[:, k, :],
                                                                         start=(k == 0), stop=(k == KC - 1)),
                         ["rl_w", f"rl_a{ti % 2}"], [f"ps{pb}"])
                S.dma("sp", x_, xv[oc, :, c0:c0 + 512], ["xT_d"], [f"rl_x{n % 2}"])
                for hf in range(2):
                    m = seg_m(c0 + hf * 256)
                    sl = slice(hf * 256, (hf + 1) * 256)
                    S.op("dve", lambda e, x_=x_, pb=pb, sl=sl, oc=oc, m=m: e.scalar_tensor_tensor(
                        out=x_[:, sl], in0=ps[pb][:, sl], scalar=mods[:, gbase + oc, m:m + 1], in1=x_[:, sl], op0=ALU.mult, op1=ALU.add),
                        [f"ps{pb}", f"rl_x{n % 2}", "mods"], [f"rl_x{n % 2}"])
                S.dma("sp", xv[oc, :, c0:c0 + 512], x_, [f"rl_x{n % 2}"], ["xT_d"])
                n += 1
    S.barrier()


def na_proj(C, hT_d, wqkv_d, qg2_d, kg2_d, qkT_d, v0_d, v1_d, NT):
    S, ps = C.S, C.ps
    NS = NT // SEG
    S.barrier()
    with contextlib.ExitStack() as st:
        hT = sb(C, st, "np_hT", [128, 8, NT], BF16)
        wst = [sb(C, st, f"np_wst{i}", [128, 8, 128], F32) for i in range(2)]
        wb = [sb(C, st, f"np_wb{i}", [128, 8, 128], BF16) for i in range(2)]
        raw = sb(C, st, "np_raw", [128, 512], F32)
        sq = sb(C, st, "np_sq", [128, 512], F32)
        r = sb(C, st, "np_r", [128, 512], F32)
        row = [sb(C, st, f"np_row{i}", [128, NT], BF16) for i in range(2)]
        gc = sb(C, st, "np_gc", [128, 2], F32)
        wv = sb(C, st, "np_wv", [128, 8, 1024], BF16)
        wvst = sb(C, st, "np_wvst", [128, 8, 512], F32)
        vo = [sb(C, st, f"np_vo{i}", [128, 1024], BF16) for i in range(2)]
        hv = hT_d.rearrange("(k p) t -> p k t", p=128)
        for ti in range(NT // 512):
            S.dma("sp", hT[:, :, ti * 512:(ti + 1) * 512], hv[:, :, ti * 512:(ti + 1) * 512], ["hT_d"], ["np_hT"])
        S.dma("sp", gc[:, 0:1], qg2_d, ["qg"], ["np_gc"])
        S.dma("sp", gc[:, 1:2], kg2_d, ["kg"], ["np_gc"])
        wqv = wqkv_d.rearrange("(k p) f -> p k f", p=128)
        for j in range(16):
            ws_, wb_, row_ = wst[j % 2], wb[j % 2], row[j % 2]
            S.dma("sp", ws_, wqv[:, :, j * 128:(j + 1) * 128], ["wqkv"], [f"np_wst{j % 2}"])
            S.op("act", lambda e, ws_=ws_, wb_=wb_: e.activation(out=wb_, in_=ws_, func=AF.Copy), [f"np_wst{j % 2}"], [f"np_wb{j % 2}"])
            for ti in range(NT // 512):
                c0 = ti * 512
                for k in range(8):
                    S.op("pe", lambda e, wb_=wb_, k=k, c0=c0: e.matmul(ps[0], lhsT=wb_[:, k, :], rhs=hT[:, k, c0:c0 + 512], start=(k == 0), stop=(k == 7)),
                         [f"np_wb{j % 2}", "np_hT"], ["ps0"])
                S.op("act", lambda e: e.activation(out=raw, in_=ps[0], func=AF.Copy), ["ps0"], ["np_raw"])
                S.op("act", lambda e: e.activation(out=sq, in_=ps[0], func=AF.Square), ["ps0"], ["np_sq"])
                S.op("pe", lambda e: e.matmul(ps[1], lhsT=C.bd64, rhs=sq, start=True, stop=True), ["np_sq", "cst"], ["ps1"])
                S.op("act", lambda e: e.activation(out=r, in_=ps[1], func=AF.Sqrt, bias=C.epsc, scale=1.0), ["ps1", "cst"], ["np_r"])
                S.op("dve", lambda e: e.reciprocal(out=r, in_=r), ["np_r"], ["np_r"])
                gi = 0 if j < 8 else 1
                S.op("dve", lambda e, row_=row_, c0=c0, gi=gi: e.scalar_tensor_tensor(
                    out=row_[:, c0:c0 + 512], in0=raw, scalar=gc[:, gi:gi + 1], in1=r, op0=ALU.mult, op1=ALU.mult),
                    ["np_raw", "np_r", "np_gc"], [f"np_row{j % 2}"])
            S.dma("sp", qkT_d[j], row_, [f"np_row{j % 2}"], ["qkT_d"])
        for c in range(2):
            S.dma("sp", wvst, wqv[:, :, 2048 + c * 512:2048 + (c + 1) * 512], ["wqkv"], ["np_wvst"])
            S.op("act", lambda e, c=c: e.activation(out=wv[:, :, c * 512:(c + 1) * 512], in_=wvst, func=AF.Copy), ["np_wvst"], ["np_wv"])
        jobs = [(v0_d, T * 128, T * 128) for T in range(NT // 128)]
        for s in range(NS):
            for T in range(15):
                jobs.append((v1_d, (s * 15 + T) * 128, s * SEG + 320 + 128 * T))
        for n, (dst, r0, c0) in enumerate(jobs):
            vo_ = vo[n % 2]
            for hf in range(2):
                pb = 2 + hf
                for k in range(8):
                    S.op("pe", lambda e, k=k, c0=c0, hf=hf, pb=pb: e.matmul(ps[pb], lhsT=hT[:, k, c0:c0 + 128], rhs=wv[:, k, hf * 512:(hf + 1) * 512],
                                                                         start=(k == 0), stop=(k == 7)), ["np_hT", "np_wv"], [f"ps{pb}"])
                S.op("act", lambda e, vo_=vo_, hf=hf, pb=pb: e.activation(out=vo_[:, hf * 512:(hf + 1) * 512], in_=ps[pb], func=AF.Copy),
                     [f"ps{pb}"], [f"np_vo{n % 2}"])
            S.dma("sp", dst[r0:r0 + 128, :], vo_, [f"np_vo{n % 2}"], ["v_d"])
    S.barrier()


def na_attn(C, qkT_d, v0_d, v1_d, rpbG_d, oT_d, NT):
    S, ps = C.S, C.ps
    NS = NT // SEG
    S.barrier()
    with contextlib.ExitStack() as st:
        EB = sb(C, st, "na_EB", [128, 16, 15, 64], BF16)
        rst = sb(C, st, "na_rst", [128, 15, 64], F32)
        q = [sb(C, st, f"na_q{i}", [128, SEG], BF16) for i in range(2)]
        kk = [sb(C, st, f"na_k{i}", [128, SEG], BF16) for i in range(2)]
        V0 = [sb(C, st, f"na_v0{i}", [128, 18, 128], BF16) for i in range(2)]
        V1 = [sb(C, st, f"na_v1{i}", [128, 15, 128], BF16) for i in range(2)]
        o = [sb(C, st, f"na_o{i}", [128, SEG], BF16) for i in range(2)]
        P = [sb(C, st, f"na_P{i}", [128, 512], BF16) for i in range(3)]
        rd = [sb(C, st, f"na_rd{i}", [128, 512], F32) for i in range(2)]
        for h in range(16):
            S.dma("sp", rst, rpbG_d[:, h], ["rpbG"], ["na_rst"])
            S.op("act", lambda e: e.activation(out=rst, in_=rst, func=AF.Exp), ["na_rst"], ["na_rst"])
            S.op("dve", lambda e, h=h: e.tensor_tensor(out=EB[:, h], in0=rst, in1=vw(C.mask01, 0, [[0, 15], [1, 64]]), op=ALU.mult),
                 ["na_rst", "cst"], ["na_EB"])
        v0v = v0_d.rearrange("(t p) f -> p t f", p=128)
        v1v = v1_d.rearrange("(t p) f -> p t f", p=128)
        it = 0
        npx = 0
        for s in range(NS):
            for j in range(8):
                b = it % 2
                it += 1
                q_, k_, V0_, V1_, o_ = q[b], kk[b], V0[b], V1[b], o[b]
                S.dma("sp", q_, qkT_d[j][:, s * SEG:(s + 1) * SEG], ["qkT_d"], [f"na_q{b}"])
                S.dma("sp", k_, qkT_d[8 + j][:, s * SEG:(s + 1) * SEG], ["qkT_d"], [f"na_k{b}"])
                S.dma("sp", V0_, v0v[:, s * 18:(s + 1) * 18, j * 128:(j + 1) * 128], ["v_d"], [f"na_v0{b}"])
                S.dma("sp", V1_, v1v[:, s * 15:(s + 1) * 15, j * 128:(j + 1) * 128], ["v_d"], [f"na_v1{b}"])
                rdeps = [f"na_q{b}", f"na_k{b}"]
                items = []
                for hh in range(2):
                    items.append((hh, "ctx", -1))
                    for r in range(32):
                        items.append((hh, "row", r))
                LAG = 1
                pend = {}

                def stage1(hh, kind, r, npx):
                    h = 2 * j + hh
                    lo, hi = 64 * hh, 64 * hh + 64
                    sbk = npx % 2
                    P_ = P[npx % 3]
                    pk = f"na_P{npx % 3}"
                    if kind == "ctx":
                        for c in range(2):
                            S.op("pe", lambda e, c=c, lo=lo, hi=hi, sbk=sbk, k_=k_, q_=q_: e.matmul(
                                ps[sbk][:, c * 256:(c + 1) * 256], lhsT=k_[lo:hi, c * 128:(c + 1) * 128], rhs=q_[lo:hi, 0:256], start=True, stop=True),
                                rdeps, [f"ps{sbk}"])
                        S.op("act", lambda e, P_=P_, sbk=sbk: e.activation(out=P_, in_=ps[sbk], func=AF.Exp, scale=0.125), [f"ps{sbk}"], [pk])
                        return (P_, pk, None)
                    rs = min(max(r - 4, 0), 24)
                    qc = 256 + 64 * r
                    kcols = [256 + 64 * (rs + 2 * c) for c in range(4)] + [0, 128]
                    for c in range(6):
                        S.op("pe", lambda e, c=c, lo=lo, hi=hi, sbk=sbk, kc=kcols[c], qc=qc, k_=k_, q_=q_: e.matmul(
                            ps[sbk][:, c * 64:(c + 1) * 64], lhsT=k_[lo:hi, kc:kc + 128], rhs=q_[lo:hi, qc:qc + 64], start=True, stop=True),
                            rdeps, [f"ps{sbk}"])
                    S.op("act", lambda e, P_=P_, sbk=sbk: e.activation(out=P_[:, 0:384], in_=ps[sbk][:, 0:384], func=AF.Exp, scale=0.125),
                         [f"ps{sbk}"], [pk])
                    d0 = rs - r + 7
                    ebv = vw(EB, (h * 15 + d0) * 64, [[128, 4], [1, 64]])
                    S.op("dve", lambda e, P_=P_, ebv=ebv: e.tensor_tensor(out=P_[:, 0:256].rearrange("p (c q) -> p c q", c=4),
                                                                          in0=P_[:, 0:256].rearrange("p (c q) -> p c q", c=4), in1=ebv, op=ALU.mult),
                         [pk, "na_EB"], [pk])
                    return (P_, pk, rs)

                def stage2(hh, kind, r, P_, pk, rs):
                    lo, hi = 64 * hh, 64 * hh + 64
                    if kind == "ctx":
                        for c in range(2):
                            S.op("pe", lambda e, c=c, lo=lo, hi=hi, P_=P_, V0_=V0_: e.matmul(
                                ps[2][lo:hi, 0:256], lhsT=V0_[:, c, lo:hi], rhs=P_[:, c * 256:(c + 1) * 256], start=(c == 0), stop=(c == 1)),
                                [f"na_v0{b}", pk], ["ps2"])
                        for c in range(2):
                            S.op("pe", lambda e, c=c, lo=lo, hi=hi, P_=P_: e.matmul(
                                ps[3][lo:hi, 0:256], lhsT=C.onesb[:, 0:64], rhs=P_[:, c * 256:(c + 1) * 256], start=(c == 0), stop=(c == 1)),
                                ["cst", pk], ["ps3"])
                        rd_ = rd[0]
                        S.op("dve", lambda e, rd_=rd_, lo=lo, hi=hi: e.reciprocal(out=rd_[lo:hi, 0:256], in_=ps[3][lo:hi, 0:256]), ["ps3"], ["na_rd0"])
                        S.op("dve", lambda e, rd_=rd_, lo=lo, hi=hi, o_=o_: e.tensor_tensor(out=o_[lo:hi, 0:256], in0=ps[2][lo:hi, 0:256], in1=rd_[lo:hi, 0:256], op=ALU.mult),
                             ["ps2", "na_rd0"], [f"na_o{b}"])
                        return
                    rg, rr = divmod(r, 8)
                    ob, db = 4 + 2 * (rg % 2), 5 + 2 * (rg % 2)
                    vts = []
                    for c in range(4):
                        row0 = rs + 2 * c
                        if rs % 2 == 0:
                            vts.append((V0_, 2 + row0 // 2, f"na_v0{b}"))
                        else:
                            vts.append((V1_, (row0 - 1) // 2, f"na_v1{b}"))
                    vts += [(V0_, 0, f"na_v0{b}"), (V0_, 1, f"na_v0{b}")]
                    for c in range(6):
                        Vt, ti, vk = vts[c]
                        S.op("pe", lambda e, Vt=Vt, ti=ti, P_=P_, c=c, lo=lo, hi=hi, ob=ob, rr=rr: e.matmul(
                            ps[ob][lo:hi, rr * 64:(rr + 1) * 64], lhsT=Vt[:, ti, lo:hi], rhs=P_[:, c * 64:(c + 1) * 64], start=(c == 0), stop=(c == 5)),
                            [vk, pk], [f"ps{ob}"])
                    for c in range(6):
                        S.op("pe", lambda e, P_=P_, c=c, lo=lo, hi=hi, db=db, rr=rr: e.matmul(
                            ps[db][lo:hi, rr * 64:(rr + 1) * 64], lhsT=C.onesb[:, 0:64], rhs=P_[:, c * 64:(c + 1) * 64], start=(c == 0), stop=(c == 5)),
                            ["cst", pk], [f"ps{db}"])
                    if rr == 7:
                        rd_ = rd[rg % 2]
                        oc0 = 256 + rg * 512
                        S.op("dve", lambda e, rd_=rd_, lo=lo, hi=hi, db=db: e.reciprocal(out=rd_[lo:hi, :], in_=ps[db][lo:hi, :]), [f"ps{db}"], [f"na_rd{rg % 2}"])
                        S.op("dve", lambda e, rd_=rd_, lo=lo, hi=hi, ob=ob, oc0=oc0, o_=o_: e.tensor_tensor(
                            out=o_[lo:hi, oc0:oc0 + 512], in0=ps[ob][lo:hi, :], in1=rd_[lo:hi, :], op=ALU.mult),
                            [f"ps{ob}", f"na_rd{rg % 2}"], [f"na_o{b}"])

                for ix in range(len(items) + LAG):
                    if ix < len(items):
                        pend[ix] = stage1(*items[ix], npx)
                        npx += 1
                    jx = ix - LAG
                    if jx >= 0:
                        stage2(*items[jx], *pend.pop(jx))
                S.dma("sp", oT_d[j][:, s * SEG:(s + 1) * SEG], o_, [f"na_o{b}"], ["oT_d"])
    S.barrier()


def conv_segments(NT):
    segs = []
    for s in range(NT // SEG):
        segs.append((s * SEG, 256))
        segs.append((s * SEG + 256, 2048))
    return segs


def even_inproj(C, hT_d, win_d, cw_d, cb_d, dtb_d, alog_d, xbcT_d, lxT_d, zT_d, gT_d, csT_d, cstok_d, xd_d, NT):
    S, ps = C.S, C.ps
    S.barrier()
    NTT = NT // 128
    with contextlib.ExitStack() as st:
        hT = sb(C, st, "ei_hT", [128, 8, NT], BF16)
        wst = [sb(C, st, f"ei_wst{i}", [128, 8, 128], F32) for i in range(2)]
        wb = [sb(C, st, f"ei_wb{i}", [128, 8, 128], BF16) for i in range(2)]
        prow = sb(C, st, "ei_prow", [128, NT], F32)
        yrow = sb(C, st, "ei_yrow", [128, NT], F32)
        orow = sb(C, st, "ei_orow", [128, NT], BF16)
        cw = sb(C, st, "ei_cw", [128, 24, 4], F32)
        cb = sb(C, st, "ei_cb", [128, 24], F32)
        dtb = sb(C, st, "ei_dtb", [64, 2], F32)
        hv = hT_d.rearrange("(k p) t -> p k t", p=128)
        for ti in range(NT // 512):
            S.dma("sp", hT[:, :, ti * 512:(ti + 1) * 512], hv[:, :, ti * 512:(ti + 1) * 512], ["hT_d"], ["ei_hT"])
        S.dma("sp", cw, cw_d, ["cw_d"], ["ei_cw"])
        S.dma("sp", cb, cb_d, ["cb_d"], ["ei_cb"])
        S.dma("sp", dtb[:, 0:1], dtb_d, ["dtb_d"], ["ei_dtb"])
        S.dma("sp", dtb[:, 1:2], alog_d, ["alog_d"], ["ei_dtb"])
        wv = win_d.rearrange("(k p) f -> p k f", p=128)
        n = 0
        for j in range(41):
            ws_, wb_ = wst[j % 2], wb[j % 2]
            fw = 128 if j < 40 else 64
            S.dma("sp", ws_[:, :, 0:fw], wv[:, :, j * 128:j * 128 + fw], ["win_d"], [f"ei_wst{j % 2}"])
            S.op("act", lambda e, ws_=ws_, wb_=wb_, fw=fw: e.activation(out=wb_[:, :, 0:fw], in_=ws_[:, :, 0:fw], func=AF.Copy),
                 [f"ei_wst{j % 2}"], [f"ei_wb{j % 2}"])
            for ti in range(NT // 512):
                c0 = ti * 512
                pb = n % 2
                n += 1
                for k in range(8):
                    S.op("pe", lambda e, wb_=wb_, k=k, c0=c0, pb=pb, fw=fw: e.matmul(ps[pb][0:fw, :], lhsT=wb_[:, k, 0:fw], rhs=hT[:, k, c0:c0 + 512],
                                                                                start=(k == 0), stop=(k == 7)), [f"ei_wb{j % 2}", "ei_hT"], [f"ps{pb}"])
                fn = AF.Copy if (j < 24 or j == 40) else (AF.Silu if j < 32 else AF.Gelu)
                S.op("act", lambda e, pb=pb, c0=c0, fn=fn, fw=fw: e.activation(out=prow[0:fw, c0:c0 + 512], in_=ps[pb][0:fw, :], func=fn),
                     [f"ps{pb}"], ["ei_prow"])
            if j < 24:
                for (s0, L) in conv_segments(NT):
                    S.op("dve", lambda e, j=j, s0=s0, L=L: e.tensor_scalar(out=yrow[:, s0:s0 + L], in0=prow[:, s0:s0 + L], scalar1=cw[:, j, 2:3],
                                                                          scalar2=cb[:, j:j + 1], op0=ALU.mult, op1=ALU.add), ["ei_prow", "ei_cw", "ei_cb"], ["ei_yrow"])
                    for (kk, do, so, ln) in ((1, 1, 0, L - 1), (0, 2, 0, L - 2), (3, 0, 1, L - 1)):
                        S.op("dve", lambda e, j=j, s0=s0, kk=kk, do=do, so=so, ln=ln: e.scalar_tensor_tensor(
                            out=yrow[:, s0 + do:s0 + do + ln], in0=prow[:, s0 + so:s0 + so + ln], scalar=cw[:, j, kk:kk + 1],
                            in1=yrow[:, s0 + do:s0 + do + ln], op0=ALU.mult, op1=ALU.add), ["ei_prow", "ei_cw", "ei_yrow"], ["ei_yrow"])
                if j < 16:
                    S.op("act", lambda e: e.activation(out=orow, in_=yrow, func=AF.Silu), ["ei_yrow"], ["ei_orow"])
                    S.dma("sp", xbcT_d[j * 128:(j + 1) * 128, :], orow, ["ei_orow"], ["xbcT_d"])
                else:
                    S.dma("sp", lxT_d[(j - 16) * 128:(j - 15) * 128, :], yrow, ["ei_yrow"], ["lxT_d"])
            elif j < 32:
                S.dma("sp", zT_d[(j - 24) * 128:(j - 23) * 128, :], prow, ["ei_prow"], ["zT_d"])
            elif j < 40:
                S.dma("sp", gT_d[(j - 32) * 128:(j - 31) * 128, :], prow, ["ei_prow"], ["gT_d"])
        xx = yrow[0:64, :]
        t2 = sb(C, st, "ei_t2", [64, NT], F32)
        ea = sb(C, st, "ei_ea", [64, 1], F32)
        S.op("dve", lambda e: e.tensor_scalar(out=xx, in0=prow[0:64, :], scalar1=dtb[:, 0:1], scalar2=None, op0=ALU.add), ["ei_prow", "ei_dtb"], ["ei_yrow"])
        S.op("act", lambda e: e.activation(out=t2, in_=xx, func=AF.Abs), ["ei_yrow"], ["ei_t2"])
        S.op("act", lambda e: e.activation(out=t2, in_=t2, func=AF.Exp, scale=-1.0), ["ei_t2"], ["ei_t2"])
        S.op("act", lambda e: e.activation(out=t2, in_=t2, func=AF.Ln, bias=C.onec[0:64, :], scale=1.0), ["ei_t2", "cst"], ["ei_t2"])
        S.op("dve", lambda e: e.scalar_tensor_tensor(out=xx, in0=xx, scalar=0.0, in1=t2, op0=ALU.max, op1=ALU.add), ["ei_yrow", "ei_t2"], ["ei_yrow"])
        S.op("act", lambda e: e.activation(out=ea, in_=dtb[:, 1:2], func=AF.Exp), ["ei_dtb"], ["ei_ea"])
        la = prow[0:64, :]
        S.op("dve", lambda e: e.tensor_scalar(out=la, in0=xx, scalar1=ea[:, 0:1], scalar2=-1.0, op0=ALU.mult, op1=ALU.mult), ["ei_yrow", "ei_ea"], ["ei_prow"])
        cs = t2
        for s in range(NT // SEG):
            c0 = s * SEG
            S.op("dve", lambda e, c0=c0: e.tensor_tensor_scan(out=cs[0:32, c0:c0 + SEG], data0=vw(C.onec[0:32, :], 0, [[0, SEG]]), data1=la[0:32, c0:c0 + SEG],
                                                               initial=0.0, op0=ALU.mult, op1=ALU.add), ["ei_prow", "cst"], ["ei_t2"])

            def rev(ap, a0, L):
                return vw(ap[32:64, :], a0 + L - 1, [[-1, L]])
            S.op("dve", lambda e, c0=c0: e.tensor_tensor_scan(out=rev(cs, c0, 256), data0=vw(C.onec[32:64, :], 0, [[0, 256]]), data1=rev(la, c0, 256),
                                                               initial=0.0, op0=ALU.mult, op1=ALU.add), ["ei_prow", "cst"], ["ei_t2"])
            S.op("dve", lambda e, c0=c0: e.tensor_tensor_scan(out=rev(cs, c0 + 256, 2048), data0=vw(C.onec[32:64, :], 0, [[0, 2048]]), data1=rev(la, c0 + 256, 2048),
                                                               initial=cs[32:64, c0:c0 + 1], op0=ALU.mult, op1=ALU.add), ["ei_prow", "cst", "ei_t2"], ["ei_t2"])
        S.dma("sp", csT_d, cs, ["ei_t2"], ["csT_d"])
        cstok = sb(C, st, "ei_cstok", [128, NTT, 64], F32)
        dttok = sb(C, st, "ei_dttok", [128, NTT, 64], F32)
        for tt in range(NTT):
            pb = 2 + tt % 2
            S.op("pe", lambda e, tt=tt, pb=pb: e.transpose(out=ps[pb][:, 0:64], in_=cs[:, tt * 128:(tt + 1) * 128], identity=C.ident[0:64, 0:64]),
                 ["ei_t2", "cst"], [f"ps{pb}"])
            S.op("pe", lambda e, tt=tt, pb=pb: e.transpose(out=ps[pb][:, 64:128], in_=xx[:, tt * 128:(tt + 1) * 128], identity=C.ident[0:64, 0:64]),
                 ["ei_yrow", "cst"], [f"ps{pb}"])
            S.op("act", lambda e, tt=tt, pb=pb: e.activation(out=cstok[:, tt, :], in_=ps[pb][:, 0:64], func=AF.Copy), [f"ps{pb}"], ["ei_cstok"])
            S.op("act", lambda e, tt=tt, pb=pb: e.activation(out=dttok[:, tt, :], in_=ps[pb][:, 64:128], func=AF.Copy), [f"ps{pb}"], ["ei_dttok"])
        S.dma("sp", cstok_d.rearrange("(t p) f -> p t f", p=128), cstok, ["ei_cstok"], ["cstok_d"])
        xb = [sb(C, st, f"ei_xb{i}", [128, 8, 512], BF16) for i in range(2)]
        xdt = [sb(C, st, f"ei_xdt{i}", [128, 1024], BF16) for i in range(2)]
        xbv = xbcT_d[0:1024, :].rearrange("(k p) t -> p k t", p=128)
        m = 0
        for bi in range(NT // 512):
            xb_ = xb[bi % 2]
            S.dma("sp", xb_, xbv[:, :, bi * 512:(bi + 1) * 512], ["xbcT_d"], [f"ei_xb{bi % 2}"])
            for t4 in range(4):
                tt = bi * 4 + t4
                pb = 4 + tt % 2
                pbf = ps[pb].bitcast(BF16)
                for k in range(8):
                    S.op("pe", lambda e, xb_=xb_, k=k, t4=t4, pbf=pbf: e.transpose(out=pbf[:, k * 128:(k + 1) * 128], in_=xb_[:, k, t4 * 128:(t4 + 1) * 128], identity=C.identb),
                         [f"ei_xb{bi % 2}", "cst"], [f"ps{pb}"])
                for d in range(2):
                    xd_ = xdt[m % 2]
                    S.op("dve", lambda e, xd_=xd_, pbf=pbf, tt=tt, d=d: e.tensor_tensor(
                        out=xd_.rearrange("p (h q) -> p h q", h=16), in0=pbf.rearrange("p (h q) -> p h q", h=16),
                        in1=vw(dttok, tt * 64 + d * 32, [[1, 16], [0, 64]]), op=ALU.mult), [f"ps{pb}", "ei_dttok"], [f"ei_xdt{m % 2}"])
                    S.dma("sp", xd_d[d][tt * 128:(tt + 1) * 128, :], xd_, [f"ei_xdt{m % 2}"], ["xd_d"])
                    m += 1
    S.barrier()


def ssd_allowed(d, lt):
    if lt == 0:
        return [(0, 0), (1, 1)]
    base = 2 + 4 * (lt - 1)
    if d == 0:
        return [(stl, None) for stl in range(base)] + [(base + r, r) for r in range(4)]
    return [(0, None), (1, None)] + [(base + r, r) for r in range(4)] + [(stl, None) for stl in range(base + 4, 18)]


def ssd_phase(C, xbcT_d, csT_d, cstok_d, xd_d, zT_d, dcol_d, sg_d, ymixT_d, NT):
    S, ps = C.S, C.ps
    S.barrier()
    with contextlib.ExitStack() as st:
        csT = sb(C, st, "sd_csT", [64, SEG], F32)
        cstok = sb(C, st, "sd_cstok", [128, 18, 64], F32)
        Bg = sb(C, st, "sd_B", [128, SEG], BF16)
        Cg = sb(C, st, "sd_C", [128, SEG], BF16)
        xdg = [sb(C, st, f"sd_xd{d}", [128, 18, 256], BF16) for d in range(2)]
        xg = sb(C, st, "sd_x", [128, 2, SEG], BF16)
        csb = [sb(C, st, f"sd_csb{i}", [128, 512], F32) for i in range(8)]
        Gs = [sb(C, st, f"sd_G{i}", [128, 512], BF16) for i in range(3)]
        dd = [sb(C, st, f"sd_dd{i}", [128, 512], F32) for i in range(3)]
        ee = [sb(C, st, f"sd_ee{i}", [128, 512], BF16) for i in range(3)]
        M = [sb(C, st, f"sd_M{i}", [128, 512], BF16) for i in range(3)]
        zt = sb(C, st, "sd_z", [128, 2, 512], F32)
        y = sb(C, st, "sd_y", [128, 2, 512], F32)
        sq = sb(C, st, "sd_sq", [128, 512], F32)
        r = sb(C, st, "sd_r", [128, 512], F32)
        yo = sb(C, st, "sd_yo", [128, 2, 512], BF16)
        dcol = sb(C, st, "sd_dcol", [128, 8], F32)
        sg = sb(C, st, "sd_sg", [128, 8], F32)
        S.dma("sp", dcol, dcol_d, ["dcol_d"], ["sd_dcol"])
        S.dma("sp", sg, sg_d, ["sg_d"], ["sd_sg"])
        ctv = cstok_d.rearrange("(t p) f -> p t f", p=128)
        xdv = [xd_d[d].rearrange("(t p) f -> p t f", p=128) for d in range(2)]
        xv = xbcT_d[0:1024, :].rearrange("(k p) t -> p k t", p=128)
        zv = zT_d.rearrange("(k p) t -> p k t", p=128)
        yv = ymixT_d[0:1024, :].rearrange("(k p) t -> p k t", p=128)
        ng = 0
        nh = 0
        for s in range(NT // SEG):
            sc0 = s * SEG
            S.dma("sp", csT, csT_d[:, sc0:sc0 + SEG], ["csT_d"], ["sd_csT"])
            S.dma("sp", cstok, ctv[:, s * 18:(s + 1) * 18, :], ["cstok_d"], ["sd_cstok"])
            for g in range(4):
                S.dma("sp", Bg, xbcT_d[1024 + g * 128:1024 + (g + 1) * 128, sc0:sc0 + SEG], ["xbcT_d"], ["sd_B"])
                S.dma("sp", Cg, xbcT_d[1536 + g * 128:1536 + (g + 1) * 128, sc0:sc0 + SEG], ["xbcT_d"], ["sd_C"])
                for d in range(2):
                    S.dma("sp", xdg[d], xdv[d][:, s * 18:(s + 1) * 18, g * 256:(g + 1) * 256], ["xd_d"], [f"sd_xd{d}"])
                S.dma("sp", xg, xv[:, 2 * g:2 * g + 2, sc0:sc0 + SEG], ["xbcT_d"], ["sd_x"])
                for lt in range(5):
                    l0, W = (0, 256) if lt == 0 else (256 + 512 * (lt - 1), 512)
                    S.dma("sp", zt[:, :, 0:W], zv[:, 2 * g:2 * g + 2, sc0 + l0:sc0 + l0 + W], ["zT_d"], ["sd_z"])
                    for d in range(2):
                        for hh in range(4):
                            row = d * 32 + 4 * g + hh
                            i8 = d * 4 + hh
                            S.op("pe", lambda e, row=row, l0=l0, W=W: e.matmul(ps[7][:, 0:W], lhsT=C.esel[:, row, :], rhs=csT[:, l0:l0 + W], start=True, stop=True),
                                 ["cst", "sd_csT"], ["ps7"])
                            S.op("act", lambda e, i8=i8, W=W: e.activation(out=csb[i8][:, 0:W], in_=ps[7][:, 0:W], func=AF.Copy), ["ps7"], [f"sd_csb{i8}"])
                    started = [False] * 4
                    pairs = [(d, stl, mi) for d in range(2) for (stl, mi) in ssd_allowed(d, lt)]
                    last_b = [p_ for p_ in pairs if p_[0] == 1][-1][1]
                    items = [(pi, d, stl, mi, hh) for pi, (d, stl, mi) in enumerate(pairs) for hh in range(4)]
                    LAG = 2
                    pend = {}
                    cur = {}
                    for ix in range(len(items) + LAG):
                        if ix < len(items):
                            pi, d, stl, mi, hh = items[ix]
                            if hh == 0:
                                gb = 5 + ng % 2
                                G_ = Gs[ng % 3]
                                gk = f"sd_G{ng % 3}"
                                ng += 1
                                S.op("pe", lambda e, stl=stl, l0=l0, W=W, gb=gb: e.matmul(ps[gb][:, 0:W], lhsT=Bg[:, stl * 128:(stl + 1) * 128], rhs=Cg[:, l0:l0 + W], start=True, stop=True),
                                     ["sd_B", "sd_C"], [f"ps{gb}"])
                                S.op("act", lambda e, G_=G_, gb=gb, W=W: e.activation(out=G_[:, 0:W], in_=ps[gb][:, 0:W], func=AF.Copy), [f"ps{gb}"], [gk])
                                cur = dict(G_=G_, gk=gk)
                            row = d * 32 + 4 * g + hh
                            i8 = d * 4 + hh
                            dd_, ee_, M_ = dd[nh % 3], ee[nh % 3], M[nh % 3]
                            dk, ek, mk = f"sd_dd{nh % 3}", f"sd_ee{nh % 3}", f"sd_M{nh % 3}"
                            nh += 1
                            col = cstok[:, stl, row:row + 1]
                            if mi is None:
                                S.op("dve", lambda e, dd_=dd_, i8=i8, col=col, W=W: e.tensor_scalar(out=dd_[:, 0:W], in0=csb[i8][:, 0:W], scalar1=col, scalar2=0.0,
                                                                                                   op0=ALU.subtract, op1=ALU.min), [f"sd_csb{i8}", "sd_cstok"], [dk])
                            else:
                                S.op("dve", lambda e, dd_=dd_, i8=i8, col=col, W=W, d=d, mi=mi: e.scalar_tensor_tensor(
                                    out=dd_[:, 0:W], in0=csb[i8][:, 0:W], scalar=col, in1=C.negm[:, d * 4 + mi, 0:W], op0=ALU.subtract, op1=ALU.add),
                                    [f"sd_csb{i8}", "sd_cstok", "cst"], [dk])
                                S.op("dve", lambda e, dd_=dd_, W=W: e.tensor_scalar(out=dd_[:, 0:W], in0=dd_[:, 0:W], scalar1=0.0, scalar2=None, op0=ALU.min), [dk], [dk])
                            S.op("act", lambda e, dd_=dd_, ee_=ee_, W=W: e.activation(out=ee_[:, 0:W], in_=dd_[:, 0:W], func=AF.Exp), [dk], [ek])
                            pend[ix] = (d, stl, hh, ee_, ek, M_, mk, cur["G_"], cur["gk"])
                        jx = ix - LAG
                        if jx < 0:
                            continue
                        d, stl, hh, ee_, ek, M_, mk, G_, gk = pend.pop(jx)
                        S.op("dve", lambda e, ee_=ee_, M_=M_, G_=G_, W=W: e.tensor_tensor(out=M_[:, 0:W], in0=ee_[:, 0:W], in1=G_[:, 0:W], op=ALU.mult), [ek, gk], [mk])
                        is_last = (d == 1 and stl == last_b)
                        ab = hh // 2
                        lo = (hh % 2) * 64
                        S.op("pe", lambda e, d=d, stl=stl, hh=hh, M_=M_, W=W, ab=ab, lo=lo, st_=(not started[hh]), is_last=is_last: e.matmul(
                            ps[ab][lo:lo + 64, 0:W], lhsT=xdg[d][:, stl, hh * 64:(hh + 1) * 64], rhs=M_[:, 0:W], start=st_, stop=is_last),
                            [f"sd_xd{d}", mk], [f"ps{ab}"])
                        started[hh] = True
                    for pr in range(2):
                        ch = 2 * g + pr
                        S.op("dve", lambda e, pr=pr, ch=ch, l0=l0, W=W: e.scalar_tensor_tensor(out=y[:, pr, 0:W], in0=xg[:, pr, l0:l0 + W], scalar=dcol[:, ch:ch + 1],
                                                                                            in1=ps[pr][:, 0:W], op0=ALU.mult, op1=ALU.add),
                             ["sd_x", "sd_dcol", f"ps{pr}"], ["sd_y"])
                        S.op("dve", lambda e, pr=pr, W=W: e.tensor_tensor(out=y[:, pr, 0:W], in0=y[:, pr, 0:W], in1=zt[:, pr, 0:W], op=ALU.mult), ["sd_y", "sd_z"], ["sd_y"])
                        S.op("act", lambda e, pr=pr, W=W: e.activation(out=sq[:, 0:W], in_=y[:, pr, 0:W], func=AF.Square), ["sd_y"], ["sd_sq"])
                        S.op("pe", lambda e, pr=pr, W=W: e.matmul(ps[4][:, 0:W], lhsT=C.ones256, rhs=sq[:, 0:W], start=(pr == 0), stop=(pr == 1)), ["sd_sq", "cst"], ["ps4"])
                    S.op("act", lambda e, W=W: e.activation(out=r[:, 0:W], in_=ps[4][:, 0:W], func=AF.Sqrt, bias=C.epsc, scale=1.0), ["ps4", "cst"], ["sd_r"])
                    S.op("dve", lambda e, W=W: e.reciprocal(out=r[:, 0:W], in_=r[:, 0:W]), ["sd_r"], ["sd_r"])
                    for pr in range(2):
                        ch = 2 * g + pr
                        S.op("dve", lambda e, pr=pr, ch=ch, W=W: e.scalar_tensor_tensor(out=yo[:, pr, 0:W], in0=y[:, pr, 0:W], scalar=sg[:, ch:ch + 1], in1=r[:, 0:W],
                                                                                     op0=ALU.mult, op1=ALU.mult), ["sd_y", "sd_sg", "sd_r"], ["sd_yo"])
                    S.dma("sp", yv[:, 2 * g:2 * g + 2, sc0 + l0:sc0 + l0 + W], yo[:, :, 0:W], ["sd_yo"], ["ymixT_d"])
    S.barrier()


def lru_phase(C, lxT_d, gT_d, wbd_d, bcol_d, lam_d, ymixT_d, NT):
    S, ps = C.S, C.ps
    S.barrier()
    with contextlib.ExitStack() as st:
        lx = sb(C, st, "lr_x", [128, SEG], F32)
        gt = sb(C, st, "lr_g", [128, SEG], F32)
        W = sb(C, st, "lr_W", [128, 4, 128], F32)
        bc = sb(C, st, "lr_bc", [128, 8, 4], F32)
        nsp = sb(C, st, "lr_nsp", [128, 8, 2, 2], F32)
        lam = sb(C, st, "lr_lam", [128, 8, 2], F32)
        rr = sb(C, st, "lr_r", [128, SEG], F32)
        ii = sb(C, st, "lr_i", [128, SEG], F32)
        aa = sb(C, st, "lr_a", [128, SEG], F32)
        bb = sb(C, st, "lr_b", [128, SEG], F32)
        hh = [sb(C, st, f"lr_h{d}", [128, SEG], F32) for d in range(2)]
        yo = sb(C, st, "lr_yo", [128, SEG], BF16)
        S.dma("sp", bc, bcol_d, ["bcol_d"], ["lr_bc"])
        S.dma("sp", lam, lam_d, ["lam_d"], ["lr_lam"])
        S.op("act", lambda e: e.activation(out=lam, in_=lam, func=AF.Exp, scale=-1.0), ["lr_lam"], ["lr_lam"])
        S.op("act", lambda e: e.activation(out=lam, in_=lam, func=AF.Ln, bias=C.onec, scale=1.0), ["lr_lam", "cst"], ["lr_lam"])
        S.op("dve", lambda e: e.tensor_scalar(out=nsp[:, :, :, 0], in0=lam, scalar1=-8.0, scalar2=None, op0=ALU.mult), ["lr_lam"], ["lr_nsp"])
        S.op("dve", lambda e: e.tensor_scalar(out=nsp[:, :, :, 1], in0=lam, scalar1=-16.0, scalar2=None, op0=ALU.mult), ["lr_lam"], ["lr_nsp"])
        tiles = [(0, 512), (512, 512), (1024, 512), (1536, 512), (2048, 256)]
        n = 0
        for c in range(8):
            S.dma("sp", W, wbd_d[c], ["wbd_d"], ["lr_W"])
            for s in range(NT // SEG):
                sc0 = s * SEG
                S.dma("sp", lx, lxT_d[c * 128:(c + 1) * 128, sc0:sc0 + SEG], ["lxT_d"], ["lr_x"])
                S.dma("sp", gt, gT_d[c * 128:(c + 1) * 128, sc0:sc0 + SEG], ["gT_d"], ["lr_g"])
                for d in range(2):
                    for gi, dst, dk in ((0, rr, "lr_r"), (1, ii, "lr_i")):
                        for (t0, Wd) in tiles:
                            pb = n % 2
                            n += 1
                            S.op("pe", lambda e, d=d, gi=gi, t0=t0, Wd=Wd, pb=pb: e.matmul(ps[pb][:, 0:Wd], lhsT=W[:, d * 2 + gi, :], rhs=lx[:, t0:t0 + Wd], start=True, stop=True),
                                 ["lr_W", "lr_x"], [f"ps{pb}"])
                            S.op("act", lambda e, dst=dst, d=d, gi=gi, t0=t0, Wd=Wd, pb=pb, c=c: e.activation(
                                out=dst[:, t0:t0 + Wd], in_=ps[pb][:, 0:Wd], func=AF.Sigmoid, bias=bc[:, c, d * 2 + gi:d * 2 + gi + 1], scale=1.0),
                                [f"ps{pb}", "lr_bc"], [dk])
                    S.op("act", lambda e, c=c, d=d: e.activation(out=aa, in_=rr, func=AF.Exp, scale=nsp[:, c, d, 0:1]), ["lr_r", "lr_nsp"], ["lr_a"])
                    S.op("act", lambda e, c=c, d=d: e.activation(out=bb, in_=rr, func=AF.Exp, scale=nsp[:, c, d, 1:2]), ["lr_r", "lr_nsp"], ["lr_b"])
                    S.op("dve", lambda e: e.tensor_scalar(out=bb, in0=bb, scalar1=-1.0, scalar2=1.0, op0=ALU.mult, op1=ALU.add), ["lr_b"], ["lr_b"])
                    S.op("act", lambda e: e.activation(out=bb, in_=bb, func=AF.Sqrt), ["lr_b"], ["lr_b"])
                    S.op("dve", lambda e: e.tensor_tensor(out=bb, in0=bb, in1=ii, op=ALU.mult), ["lr_b", "lr_i"], ["lr_b"])
                    S.op("dve", lambda e: e.tensor_tensor(out=bb, in0=bb, in1=lx, op=ALU.mult), ["lr_b", "lr_x"], ["lr_b"])
                    h_ = hh[d]
                    if d == 0:
                        S.op("dve", lambda e, h_=h_: e.tensor_tensor_scan(out=h_, data0=aa, data1=bb, initial=0.0, op0=ALU.mult, op1=ALU.add), ["lr_a", "lr_b"], ["lr_h0"])
                    else:
                        def rev(ap, a0, L):
                            return vw(ap, a0 + L - 1, [[-1, L]])
                        S.op("dve", lambda e, h_=h_: e.tensor_tensor_scan(out=rev(h_, 0, 256), data0=rev(aa, 0, 256), data1=rev(bb, 0, 256), initial=0.0,
                                                                         op0=ALU.mult, op1=ALU.add), ["lr_a", "lr_b"], ["lr_h1"])
                        S.op("dve", lambda e, h_=h_: e.tensor_tensor_scan(out=rev(h_, 256, 2048), data0=rev(aa, 256, 2048), data1=rev(bb, 256, 2048), initial=h_[:, 0:1],
                                                                         op0=ALU.mult, op1=ALU.add), ["lr_a", "lr_b", "lr_h1"], ["lr_h1"])
                S.op("dve", lambda e: e.tensor_tensor(out=hh[0], in0=hh[0], in1=hh[1], op=ALU.add), ["lr_h0", "lr_h1"], ["lr_h0"])
                S.op("dve", lambda e: e.tensor_tensor(out=yo, in0=hh[0], in1=gt, op=ALU.mult), ["lr_h0", "lr_g"], ["lr_yo"])
                S.dma("sp", ymixT_d[1024 + c * 128:1024 + (c + 1) * 128, sc0:sc0 + SEG], yo, ["lr_yo"], ["ymixT_d"])
    S.barrier()


NCST = 128 * 6 + 64 + 2
LAYERS = 4


def host_consts():
    c = np.zeros((128, NCST), np.float32)
    c[:, 0:128] = np.eye(128)
    c[:, 128:256] = np.arange(128)[None, :]
    c[:, 256:384] = 1.0 / 1024
    bd = np.zeros((128, 128), np.float32)
    bd[:64, :64] = 1.0 / 64
    bd[64:, 64:] = 1.0 / 64
    c[:, 384:512] = bd
    c[:, 512:640] = 1.0 / 256
    c[:, 640:768] = 1.0
    p = np.arange(128)
    kc = p % 64
    qc = np.arange(64)
    cs_ = np.clip(qc - 8, 0, 48)
    c[:, 768:832] = ((kc[:, None] >= cs_[None, :]) & (kc[:, None] < cs_[None, :] + 16)).astype(np.float32)
    c[:, 832] = EPS
    c[:, 833] = 1.0
    esel = np.zeros((64, 48, 128), np.float32)
    for r_ in range(48):
        esel[r_, r_, :] = 1.0
    s_ = np.arange(128)[:, None]
    l_ = np.arange(512)[None, :]
    negm = np.zeros((128, 8, 512), np.float32)
    for r_ in range(4):
        negm[:, r_, :] = np.where(l_ >= 128 * r_ + s_, 0.0, -1.0e6)
        negm[:, 4 + r_, :] = np.where(128 * r_ + s_ >= l_, 0.0, -1.0e6)
    return c, esel, negm


def build_program(NT=2 * SEG, layers=range(LAYERS), debug=False, stop_after=None, with_peer=True):
    nc = bass.Bass("TRN2", target_bir_lowering=False)
    C = Ctx()
    C.nc = nc
    C.S = S = Sched(nc)
    NS = NT // SEG

    def ein(name, shape, dt=F32):
        return nc.dram_tensor(name, list(shape), dt, kind="ExternalInput").ap()

    def scratch(name, shape, dt):
        return nc.dram_tensor(name, list(shape), dt, kind="ExternalOutput" if debug else "Internal").ap()

    x_in = ein("x_in", [1024, NT])
    cT_d = ein("cT", [128, 8, 3])
    cst_d = ein("cst", [128, NCST])
    esel_d = ein("esel", [64, 48, 128])
    negm_d = ein("negm", [128, 8, 512])
    adaw = ein("adaw", [LAYERS, 1024, 6144])
    adabT = ein("adabT", [LAYERS, 128, 48])
    ngT = ein("ngT", [LAYERS, 128, 2, 8])
    win = ein("win", [2, 1024, 5184])
    cw = ein("cw", [2, 128, 24, 4])
    cb = ein("cb", [2, 128, 24])
    dtb = ein("dtb", [2, 64, 1])
    alog = ein("alog", [2, 64, 1])
    dcol = ein("dcol", [2, 128, 8])
    sgc = ein("sgc", [2, 128, 8])
    wbd = ein("wbd", [2, 8, 128, 4, 128])
    bcol = ein("bcol", [2, 128, 8, 4])
    lamc = ein("lamc", [2, 128, 8, 2])
    wout = ein("wout", [2, 2048, 1024])
    wqkv = ein("wqkv", [2, 1024, 3072])
    qg2 = ein("qg2", [2, 128, 1])
    kg2 = ein("kg2", [2, 128, 1])
    rpbG = ein("rpbG", [2, 128, 16, 15, 64])
    wo = ein("wo", [2, 1024, 1024])
    if with_peer:
        pwq = ein("pwq", [LAYERS, 1024, 2048])
        pkeys = ein("pkeys", [LAYERS, 128, 16, 128])
        puT = ein("puT", [LAYERS, 128, 128, 8, 128])
        pv = ein("pv", [LAYERS, 16384, 1024])
    xT = nc.dram_tensor("xT", [1024, NT], F32, kind="ExternalOutput").ap()
    hT_d = scratch("hT_d", [1024, NT], BF16)
    rout_d = scratch("rout_d", [3, 128, NT], F32)
    xbcT_d = scratch("xbcT_d", [2048, NT], BF16)
    lxT_d = scratch("lxT_d", [1024, NT], F32)
    zT_d = scratch("zT_d", [1024, NT], F32)
    gT_d = scratch("gT_d", [1024, NT], F32)
    csT_d = scratch("csT_d", [64, NT], F32)
    cstok_d = scratch("cstok_d", [NT, 64], F32)
    xd_d = scratch("xd_d", [2, NT, 1024], BF16)
    ymixT_d = scratch("ymixT_d", [2048, NT], BF16)
    qkT_d = scratch("qkT_d", [16, 128, NT], BF16)
    v0_d = scratch("v0_d", [NT, 1024], BF16)
    v1_d = scratch("v1_d", [NS * 15 * 128, 1024], BF16)
    oT_d = scratch("oT_d", [8, 128, NT], BF16)

    C.ps = [nc.alloc_psum_tensor(f"psb{i}", [128, 512], F32).ap() for i in range(8)]
    cst = nc.alloc_sbuf_tensor("cst_sb", [128, NCST], F32).ap()
    C.ident, C.iota, C.onesm = cst[:, 0:128], cst[:, 128:256], cst[:, 256:384]
    C.bd64, C.ones256, C.mask01 = cst[:, 384:512], cst[:, 512:640], cst[:, 768:832]
    C.epsc, C.onec = cst[:, 832:833], cst[:, 833:834]
    C.identb = nc.alloc_sbuf_tensor("identb", [128, 128], BF16).ap()
    C.onesb = nc.alloc_sbuf_tensor("onesb", [128, 128], BF16).ap()
    C.scT = nc.alloc_sbuf_tensor("scT", [128, 8, 3], F32).ap()
    C.mods = [nc.alloc_sbuf_tensor(f"mods{l}", [128, 48, 3], F32).ap() for l in range(LAYERS)]
    C.modA = [nc.alloc_sbuf_tensor(f"modA{l}", [128, 2, 8, 3], F32).ap() for l in range(LAYERS)]
    S.dma("sp", cst, cst_d, ["cst_d"], ["cst"])
    S.op("act", lambda e: e.activation(out=C.identb, in_=C.ident, func=AF.Copy), ["cst"], ["cst"])
    S.op("act", lambda e: e.activation(out=C.onesb, in_=cst[:, 640:768], func=AF.Copy), ["cst"], ["cst"])
    S.dma("sp", C.scT, cT_d, ["cT_d"], ["scT"])
    S.op("act", lambda e: e.activation(out=C.scT, in_=C.scT, func=AF.Silu), ["scT"], ["scT"])
    for k in range(8):
        S.dma("sp", xT[k * 128:(k + 1) * 128, :], x_in[k * 128:(k + 1) * 128, :], ["x_in"], ["xT_d"])

    def done(tag):
        return stop_after is not None and tag == stop_after

    for l in layers:
        j = l // 2
        adaln(C, l, adaw[l], adabT[l], ngT[l])
        norm_mod(C, l, 0, xT, hT_d, NT)
        if done(f"nm{l}"):
            break
        if l % 2 == 0:
            even_inproj(C, hT_d, win[j], cw[j], cb[j], dtb[j], alog[j], xbcT_d, lxT_d, zT_d, gT_d, csT_d, cstok_d, xd_d, NT)
            if done(f"ei{l}"):
                break
            with contextlib.ExitStack() as st:
                C.esel = sb(C, st, "esel_sb", [64, 48, 128], F32)
                C.negm = sb(C, st, "negm_sb", [128, 8, 512], F32)
                S.dma("sp", C.esel, esel_d, ["esel_d"], ["cst"])
                S.dma("sp", C.negm, negm_d, ["negm_d"], ["cst"])
                ssd_phase(C, xbcT_d, csT_d, cstok_d, xd_d, zT_d, dcol[j], sgc[j], ymixT_d, NT)
            if done(f"ssd{l}"):
                break
            lru_phase(C, lxT_d, gT_d, wbd[j], bcol[j], lamc[j], ymixT_d, NT)
            if done(f"lru{l}"):
                break
            resid_linear(C, l, ymixT_d, 16, wout[j], 16, xT, NT)
        else:
            na_proj(C, hT_d, wqkv[j], qg2[j], kg2[j], qkT_d, v0_d, v1_d, NT)
            if done(f"np{l}"):
                break
            na_attn(C, qkT_d, v0_d, v1_d, rpbG[j], oT_d, NT)
            if done(f"na{l}"):
                break
            resid_linear(C, l, oT_d.rearrange("k p t -> (k p) t"), 8, wo[j], 16, xT, NT)
        if done(f"mix{l}"):
            break
        norm_mod(C, l, 1, xT, hT_d, NT)
        if not with_peer:
            continue
        peer_route(C, hT_d, pwq[l], pkeys[l], rout_d, NT)
        if done(f"pr{l}"):
            break
        peer_expert(C, hT_d, rout_d, puT[l], pv[l], xT, (lambda ch, m, l=l: C.mods[l][:, 40 + ch, m:m + 1]), NT, seg_m)
    n = S.finalize()
    return nc, n


def host_prep(inp, core, ncores=8):
    f = np.float32
    bs = slice(2 * core, 2 * core + 2)
    x, ctx, c, c_ctx = inp["x"][bs], inp["ctx"][bs], inp["c"][bs], inp["c_ctx"]
    seq = np.concatenate([ctx, x], axis=1)
    x_in = np.ascontiguousarray(seq.reshape(2 * SEG, 1024).T)
    cm = np.stack([c[0], c[1], c_ctx], axis=1)
    cT = np.ascontiguousarray(cm.reshape(8, 128, 3).transpose(1, 0, 2))
    return {"x_in": x_in.astype(f), "cT": cT.astype(f)}


def colT(a, nch):
    a = np.asarray(a, np.float32)
    lead = a.shape[:-1]
    return np.ascontiguousarray(np.moveaxis(a.reshape(*lead, nch, 128), -1, -2))


def host_shared(inp):
    f = np.float32
    g = lambda k: np.asarray(inp[k], f)
    cst, esel, negm = host_consts()
    d = {"cst": cst, "esel": esel, "negm": negm}
    d["adaw"] = g("ada_w")
    d["adabT"] = colT(g("ada_b"), 48)
    d["ngT"] = np.ascontiguousarray(np.stack([colT(g("norm1_g"), 8), colT(g("norm2_g"), 8)], axis=2))
    w = g("ev_w_in")
    z16 = np.zeros((2, 1024, 16), f)
    d["win"] = np.ascontiguousarray(np.concatenate(
        [w[:, :, 0:2048], w[:, :, 2080:3104], w[:, :, 3104:4128], w[:, :, 4128:5152], w[:, :, 2048:2064], z16, w[:, :, 2064:2080], z16], axis=2))
    cwx = np.concatenate([g("ev_conv_w"), g("ev_lru_conv_w")], axis=2)
    d["cw"] = np.ascontiguousarray(cwx.reshape(2, 4, 24, 128).transpose(0, 3, 2, 1))
    d["cb"] = colT(np.concatenate([g("ev_conv_b"), g("ev_lru_conv_b")], axis=1), 24)
    z16b = np.zeros((2, 16), f)
    dtb = g("ev_dt_bias")
    al = g("ev_a_log")
    d["dtb"] = np.ascontiguousarray(np.concatenate([dtb[:, 0], z16b, dtb[:, 1], z16b], axis=1)[:, :, None])
    d["alog"] = np.ascontiguousarray(np.concatenate([al[:, 0], z16b, al[:, 1], z16b], axis=1)[:, :, None])
    d["dcol"] = colT(np.repeat(g("ev_d"), 64, axis=1), 8)
    d["sgc"] = colT(g("ev_ssd_norm_g"), 8)
    wa, wx = g("ev_lru_wa"), g("ev_lru_wx")
    wbd = np.zeros((2, 8, 128, 4, 128), f)
    for dd_ in range(2):
        for gi, ww in enumerate((wa, wx)):
            for n_ in range(16):
                c_, o_ = n_ // 2, (n_ % 2) * 64
                wbd[:, c_, o_:o_ + 64, dd_ * 2 + gi, o_:o_ + 64] = ww[:, dd_, n_]
    d["wbd"] = wbd
    ba, bx = colT(g("ev_lru_ba"), 8), colT(g("ev_lru_bx"), 8)
    d["bcol"] = np.ascontiguousarray(np.stack([ba[:, 0], bx[:, 0], ba[:, 1], bx[:, 1]], axis=-1))
    d["lamc"] = np.ascontiguousarray(np.moveaxis(colT(g("ev_lru_lam"), 8), 1, -1))
    d["wout"] = g("ev_w_out")
    d["wqkv"] = g("od_w_qkv")
    d["qg2"] = np.ascontiguousarray(np.tile(g("od_q_norm_g"), (1, 2))[:, :, None])
    d["kg2"] = np.ascontiguousarray(np.tile(g("od_k_norm_g"), (1, 2))[:, :, None])
    rpb = g("od_rpb")
    p = np.arange(128)
    kc, up = p % 64, p // 64
    qc = np.arange(64)
    dc = np.clip(kc[:, None] - qc[None, :] + 15, 0, 30)
    dr = np.clip(np.arange(15)[None, :] + up[:, None], 0, 14)
    d["rpbG"] = np.ascontiguousarray(rpb[:, :, dr[:, :, None], dc[:, None, :]].transpose(0, 2, 1, 3, 4))
    d["wo"] = g("od_w_o")
    d["pwq"] = g("pe_w_q")
    d["pkeys"] = np.ascontiguousarray(g("pe_keys").reshape(4, 16, 128, 128).transpose(0, 3, 1, 2))
    u = g("pe_u")
    d["puT"] = np.ascontiguousarray(u.reshape(4, 128, 128, 8, 128).transpose(0, 2, 4, 3, 1))
    d["pv"] = g("pe_v")
    return d


_CACHE = {}


def kernel(**inputs):
    if "nc" not in _CACHE:
        _CACHE["nc"] = build_program()[0]
    nc = _CACHE["nc"]
    shared = host_shared(inputs)
    in_maps = []
    for core in range(8):
        m = dict(shared)
        m.update(host_prep(inputs, core))
        in_maps.append(m)
    res = run_bass_kernel_spmd(nc, in_maps, core_ids=list(range(8)))
    out = np.empty((16, 2048, 1024), np.float32)
    for core in range(8):
        xT = np.asarray(res.results[core]["xT"], np.float32)
        seq = xT.T.reshape(2, SEG, 1024)
        out[2 * core:2 * core + 2] = seq[:, 256:, :]
    return out
```

```python
import contextlib
import numpy as np
import ml_dtypes
import concourse.bass as bass
import concourse.mybir as mybir
from concourse.ap import AP
from concourse.bass_utils import run_bass_kernel_spmd

F32 = mybir.dt.float32
BF16 = mybir.dt.bfloat16
U32 = mybir.dt.uint32
AF = mybir.ActivationFunctionType
ALU = mybir.AluOpType
AX = mybir.AxisListType

ENGS = ("pe", "dve", "act", "pool", "sp")
NEG = -1.0e30


class Sched:
    NSLOT = 24
    ROT = 20000

    def __init__(self, nc):
        self.nc = nc
        self.ops = []
        self.eng = {"pe": nc.tensor, "dve": nc.vector, "act": nc.scalar,
                    "pool": nc.gpsimd, "sp": nc.sync}

    def op(self, eng, emit, reads=(), writes=(), dma=False):
        self.ops.append(dict(eng=eng, emit=emit, reads=tuple(reads),
                             writes=tuple(writes), dma=dma, bar=False))

    def dma(self, q, out, in_, reads, writes, **kw):
        self.op(q, lambda e: e.dma_start(out=out, in_=in_, **kw), reads, writes, dma=True)

    def barrier(self):
        self.ops.append(dict(bar=True))

    def finalize(self):
        nc = self.nc
        raw = self.ops
        ops = []
        last_w, readers = {}, {}
        slot_last = [None] * self.NSLOT
        nslot = 0
        last_on = {}
        pending_bar = {}
        for o in raw:
            if o["bar"]:
                extra = set(last_on.values()) | {s for s in slot_last if s is not None}
                for e in ENGS:
                    pending_bar[e] = set(extra) | pending_bar.get(e, set())
                continue
            i = len(ops)
            ops.append(o)
            d = set()
            for r in o["reads"]:
                if r in last_w:
                    d.add(last_w[r])
            for w in o["writes"]:
                if w in last_w:
                    d.add(last_w[w])
                d.update(readers.get(w, ()))
            if o["dma"]:
                s = nslot % self.NSLOT
                nslot += 1
                o["slot"] = s
                if slot_last[s] is not None:
                    d.add(slot_last[s])
                slot_last[s] = i
            if o["eng"] in pending_bar:
                d.update(pending_bar.pop(o["eng"]))
            d.discard(i)
            if o["eng"] == "pe" and not o["dma"]:
                d = {j for j in d if not (ops[j]["eng"] == "pe" and not ops[j]["dma"])}
            o["deps"] = d
            for w in o["writes"]:
                last_w[w] = i
                readers[w] = []
            for r in o["reads"]:
                if r not in o["writes"]:
                    readers.setdefault(r, []).append(i)
            last_on[o["eng"]] = i
        n = len(ops)
        signal = [False] * n
        for o in ops:
            for j in o["deps"]:
                signal[j] = True
        cnt = {e: 0 for e in ENGS}
        slot_cnt = [0] * self.NSLOT
        for i, o in enumerate(ops):
            if o["dma"]:
                slot_cnt[o["slot"]] += 1
                o["sig"] = ("slot", o["slot"], 16 * slot_cnt[o["slot"]])
            elif signal[i]:
                e = o["eng"]
                o["sig"] = (e, cnt[e] // self.ROT, cnt[e] % self.ROT + 1)
                cnt[e] += 1
            else:
                o["sig"] = None
        sems = {}
        for e in ENGS:
            for k in range(max((cnt[e] + self.ROT - 1) // self.ROT, 1)):
                sems[(e, k)] = nc.alloc_semaphore(f"s_{e}_{k}")
        for s in range(self.NSLOT):
            sems[("slot", s)] = nc.alloc_semaphore(f"s_dma_{s}")
        self.sems = sems
        waited = {e: {} for e in ENGS}
        for o in ops:
            e = o["eng"]
            eobj = self.eng[e]
            need = {}
            for j in o["deps"]:
                sg = ops[j]["sig"]
                key = (sg[0], sg[1])
                need[key] = max(need.get(key, 0), sg[2])
            for key, v in need.items():
                if waited[e].get(key, 0) >= v:
                    continue
                eobj.wait_ge(sems[key], v)
                waited[e][key] = v
            ins = o["emit"](eobj)
            sg = o["sig"]
            if sg is not None:
                if sg[0] == "slot":
                    ins.then_inc(sems[("slot", sg[1])], 16)
                else:
                    ins.then_inc(sems[(sg[0], sg[1])], 1)
        fin = {}
        for o in ops:
            if o["dma"]:
                fin[o["slot"]] = o["sig"][2]
        for s, v in fin.items():
            self.eng["sp"].wait_ge(sems[("slot", s)], v)
        self.ops = ops
        return n


def vw(base, off, dims):
    return AP(base.tensor, base.offset + off, [list(base.ap[0])] + [list(d) for d in dims])


class Ctx:
    pass


_UNIQ = [0]


def sb(C, st, name, shape, dt):
    _UNIQ[0] += 1
    return st.enter_context(C.nc.sbuf_tensor(f"{name}_{_UNIQ[0]}", shape, dt)).ap()


def peer_route(C, hT_d, wq_d, keysT_d, rout_d, NT):
    S, nc, ps = C.S, C.nc, C.ps
    S.barrier()
    with contextlib.ExitStack() as st:
        wq = sb(C, st, "pr_wq", [128, 8, 2048], BF16)
        wst = sb(C, st, "pr_wst", [128, 8, 512], F32)
        keys = sb(C, st, "pr_keys", [128, 16, 128], F32)
        hT = [sb(C, st, f"pr_hT{i}", [128, 8, 512], BF16) for i in range(2)]
        qTs = [sb(C, st, f"pr_qT{i}", [128, 16, 128], F32) for i in range(2)]
        scs = [sb(C, st, f"pr_sc{i}", [128, 2048], F32) for i in range(2)]
        tmp16 = sb(C, st, "pr_tmp16", [128, 16, 128], F32)
        tmp8 = sb(C, st, "pr_tmp8", [128, 8, 256], F32)
        stop = sb(C, st, "pr_stop", [128, 16, 16], F32)
        itop = sb(C, st, "pr_itop", [128, 16, 16], U32)
        itf = sb(C, st, "pr_itf", [128, 16, 16], F32)
        cand = sb(C, st, "pr_cand", [128, 8, 256], F32)
        best = sb(C, st, "pr_best", [128, 8, 16], F32)
        pos = sb(C, st, "pr_pos", [128, 8, 16], U32)
        posf = sb(C, st, "pr_posf", [128, 128], F32)
        av = sb(C, st, "pr_av", [128, 128], F32)
        bv = sb(C, st, "pr_bv", [128, 128], F32)
        eq = sb(C, st, "pr_eq", [128, 2048], F32)
        sel = sb(C, st, "pr_sel", [128, 3, 128], F32)
        zs = sb(C, st, "pr_zs", [128, 8], F32)
        selT = [sb(C, st, f"pr_selT{i}", [128, 3, 128], F32) for i in range(2)]
        thr = sb(C, st, "pr_thr", [128, 15], F32)
        wqv = wq_d.rearrange("(k p) f -> p k f", p=128)
        for c in range(4):
            S.dma("sp", wst, wqv[:, :, c * 512:(c + 1) * 512], ["wq_d"], ["pr_wst"])
            S.op("act", lambda e, c=c: e.activation(out=wq[:, :, c * 512:(c + 1) * 512], in_=wst, func=AF.Copy),
                 ["pr_wst"], ["pr_wq"])
        S.dma("sp", keys, keysT_d, ["keys_d"], ["pr_keys"])
        S.op("dve", lambda e: e.tensor_scalar(out=thr, in0=C.iota[:, 1:16], scalar1=16.0, scalar2=None, op0=ALU.mult), ["cst"], ["pr_thr"])
        hv = hT_d.rearrange("(k p) t -> p k t", p=128)
        rv = rout_d.rearrange("a s t -> s a t")
        STA = [f"pr_st{i}a" for i in range(16)]
        STB = [f"pr_st{i}b" for i in range(16)]
        ITA = [f"pr_it{i}a" for i in range(16)]
        ITB = [f"pr_it{i}b" for i in range(16)]
        BSA = [f"pr_bs{i}a" for i in range(8)]
        BSB = [f"pr_bs{i}b" for i in range(8)]
        PSA = [f"pr_ps{i}a" for i in range(8)]
        PSB = [f"pr_ps{i}b" for i in range(8)]
        nt = 0
        for blk in range(NT // 512):
            hT_ = hT[blk % 2]
            hk = f"pr_hT{blk % 2}"
            S.dma("sp", hT_, hv[:, :, blk * 512:(blk + 1) * 512], ["hT_d"], [hk])
            for tt in range(4):
                t0 = tt * 128
                qT, sc = qTs[nt % 2], scs[nt % 2]
                qk, sk = f"pr_qT{nt % 2}", f"pr_sc{nt % 2}"
                selT_ = selT[nt % 2]
                stk = f"pr_selT{nt % 2}"
                nt += 1
                for qc in range(16):
                    b = qc // 4
                    for k in range(8):
                        S.op("pe", lambda e, qc=qc, k=k, b=b, t0=t0, hT_=hT_: e.matmul(
                            ps[b][:, (qc % 4) * 128:(qc % 4 + 1) * 128], lhsT=wq[:, k, qc * 128:(qc + 1) * 128],
                            rhs=hT_[:, k, t0:t0 + 128], start=(k == 0), stop=(k == 7)),
                            ["pr_wq", hk], [f"ps{b}"])
                for b in range(4):
                    S.op("act", lambda e, b=b, qT=qT: e.activation(out=qT[:, 4 * b:4 * b + 4, :], in_=ps[b].rearrange("p (a t) -> p a t", a=4), func=AF.Copy),
                         [f"ps{b}"], [qk])
                for hz in range(16):
                    b = 4 + hz // 4
                    S.op("pe", lambda e, hz=hz, b=b, qT=qT: e.matmul(
                        ps[b][:, (hz % 4) * 128:(hz % 4 + 1) * 128], lhsT=qT[:, hz, :], rhs=keys[:, hz, :],
                        start=True, stop=True), [qk, "pr_keys"], [f"ps{b}"])
                for b in range(4):
                    S.op("act", lambda e, b=b, sc=sc: e.activation(out=sc[:, b * 512:(b + 1) * 512], in_=ps[4 + b], func=AF.Copy),
                         [f"ps{4 + b}"], [sk])
                G16 = range(16)
                for hz in G16:
                    S.op("dve", lambda e, hz=hz, sc=sc: e.max(out=stop[:, hz, 0:8], in_=sc[:, hz * 128:(hz + 1) * 128]), [sk], [STA[hz]])
                for hz in G16:
                    S.op("dve", lambda e, hz=hz, sc=sc: e.max_index(out=itop[:, hz, 0:8], in_max=stop[:, hz, 0:8], in_values=sc[:, hz * 128:(hz + 1) * 128]),
                         [sk, STA[hz]], [ITA[hz]])
                for hz in G16:
                    S.op("dve", lambda e, hz=hz, sc=sc: e.match_replace(out=tmp16[:, hz, :], in_to_replace=stop[:, hz, 0:8], in_values=sc[:, hz * 128:(hz + 1) * 128], imm_value=NEG),
                         [sk, STA[hz]], [f"pr_tm{hz}"])
                for hz in G16:
                    S.op("dve", lambda e, hz=hz: e.max(out=stop[:, hz, 8:16], in_=tmp16[:, hz, :]), [f"pr_tm{hz}"], [STB[hz]])
                for hz in G16:
                    S.op("dve", lambda e, hz=hz: e.max_index(out=itop[:, hz, 8:16], in_max=stop[:, hz, 8:16], in_values=tmp16[:, hz, :]),
                         [f"pr_tm{hz}", STB[hz]], [ITB[hz]])
                S.op("dve", lambda e: e.tensor_copy(out=itf, in_=itop), ITA + ITB, ["pr_itf"])
                in0 = vw(stop, 0, [[32, 8], [1, 16], [0, 16]])
                in1 = vw(stop, 16, [[32, 8], [0, 16], [1, 16]])
                S.op("dve", lambda e, in0=in0, in1=in1: e.tensor_tensor(out=cand.rearrange("p h (a b) -> p h a b", a=16), in0=in0, in1=in1, op=ALU.add),
                     STA + STB, ["pr_cand"])
                G8 = range(8)
                for h in G8:
                    S.op("dve", lambda e, h=h: e.max(out=best[:, h, 0:8], in_=cand[:, h, :]), ["pr_cand"], [BSA[h]])
                for h in G8:
                    S.op("dve", lambda e, h=h: e.max_index(out=pos[:, h, 0:8], in_max=best[:, h, 0:8], in_values=cand[:, h, :]), ["pr_cand", BSA[h]], [PSA[h]])
                for h in G8:
                    S.op("dve", lambda e, h=h: e.match_replace(out=tmp8[:, h, :], in_to_replace=best[:, h, 0:8], in_values=cand[:, h, :], imm_value=NEG),
                         ["pr_cand", BSA[h]], [f"pr_t8{h}"])
                for h in G8:
                    S.op("dve", lambda e, h=h: e.max(out=best[:, h, 8:16], in_=tmp8[:, h, :]), [f"pr_t8{h}"], [BSB[h]])
                for h in G8:
                    S.op("dve", lambda e, h=h: e.max_index(out=pos[:, h, 8:16], in_max=best[:, h, 8:16], in_values=tmp8[:, h, :]), [f"pr_t8{h}", BSB[h]], [PSB[h]])
                S.op("dve", lambda e: e.tensor_copy(out=posf, in_=pos.rearrange("p h k -> p (h k)")), PSA + PSB, ["pr_posf"])
                S.op("dve", lambda e: e.tensor_tensor(out=eq[:, 0:1920].rearrange("p (s m) -> p s m", m=15), in0=vw(posf, 0, [[1, 128], [0, 15]]),
                                                      in1=vw(thr, 0, [[0, 128], [1, 15]]), op=ALU.is_ge), ["pr_posf", "pr_thr"], ["pr_eq"])
                S.op("dve", lambda e: e.tensor_reduce(out=av, in_=eq[:, 0:1920].rearrange("p (s m) -> p s m", m=15), axis=AX.X, op=ALU.add), ["pr_eq"], ["pr_av"])
                S.op("dve", lambda e: e.scalar_tensor_tensor(out=bv, in0=av, scalar=-16.0, in1=posf, op0=ALU.mult, op1=ALU.add),
                     ["pr_posf", "pr_av"], ["pr_bv"])
                iota16 = vw(C.iota, 0, [[0, 8], [0, 16], [1, 16]])
                eq4 = eq.rearrange("p (h k a) -> p h k a", h=8, k=16)
                for z, src_ab in ((0, av), (1, bv)):
                    abb = vw(src_ab, 0, [[16, 8], [1, 16], [0, 16]])
                    itb = vw(itf, 16 * z, [[32, 8], [0, 16], [1, 16]])
                    S.op("dve", lambda e, abb=abb: e.tensor_tensor(out=eq4, in0=abb, in1=iota16, op=ALU.is_equal),
                         ["pr_av", "pr_bv"], ["pr_eq"])
                    S.op("dve", lambda e, itb=itb: e.tensor_tensor(out=eq4, in0=eq4, in1=itb, op=ALU.mult),
                         ["pr_eq", "pr_itf"], ["pr_eq"])
                    S.op("dve", lambda e, z=z: e.tensor_reduce(out=sel[:, z, :], in_=eq.rearrange("p (s a) -> p s a", a=16), axis=AX.X, op=ALU.add),
                         ["pr_eq"], ["pr_sel"])
                g3 = sel[:, 2, :].rearrange("p (h k) -> p h k", h=8)
                b0 = vw(best, 0, [[16, 8], [0, 16]])
                S.op("dve", lambda e: e.tensor_tensor(out=g3, in0=best, in1=b0, op=ALU.subtract), BSA + BSB, ["pr_sel"])
                S.op("act", lambda e: e.activation(out=sel[:, 2, :], in_=sel[:, 2, :], func=AF.Exp), ["pr_sel"], ["pr_sel"])
                S.op("dve", lambda e: e.tensor_reduce(out=zs, in_=g3, axis=AX.X, op=ALU.add), ["pr_sel"], ["pr_zs"])
                S.op("dve", lambda e: e.reciprocal(out=zs, in_=zs), ["pr_zs"], ["pr_zs"])
                zb = vw(zs, 0, [[1, 8], [0, 16]])
                S.op("dve", lambda e: e.tensor_tensor(out=g3, in0=g3, in1=zb, op=ALU.mult), ["pr_sel", "pr_zs"], ["pr_sel"])
                for a in range(3):
                    S.op("pe", lambda e, a=a: e.transpose(out=ps[0][:, a * 128:(a + 1) * 128], in_=sel[:, a, :], identity=C.ident),
                         ["pr_sel"], ["ps0"])
                S.op("act", lambda e, selT_=selT_: e.activation(out=selT_, in_=ps[0][:, 0:384].rearrange("p (a t) -> p a t", a=3), func=AF.Copy),
                     ["ps0"], [stk])
                c0 = blk * 512 + t0
                S.dma("sp", rv[:, :, c0:c0 + 128], selT_, [stk], ["rout_d"])
    S.barrier()


def peer_expert(C, hT_d, rout_d, uT_d, v_d, xT_d, g2cols, NT, seg_of_col):
    S, nc, ps = C.S, C.nc, C.ps
    TB = 512
    SBT = 16
    NST = 4
    S.barrier()
    with contextlib.ExitStack() as st:
        hT = sb(C, st, "px_hT", [128, 8, TB], BF16)
        rt = sb(C, st, "px_rt", [128, 3, TB], F32)
        A = [sb(C, st, f"px_A{i}", [128, SBT, 128], BF16) for i in range(2)]
        B = [sb(C, st, f"px_B{i}", [128, SBT, 128], BF16) for i in range(2)]
        G = sb(C, st, "px_G", [128, 128, TB], BF16)
        wt = [sb(C, st, f"px_w{i}", [128, 1024], BF16) for i in range(3)]
        stg = [sb(C, st, f"px_s{i}", [128, 1024], F32) for i in range(NST)]
        gl = [sb(C, st, f"px_gl{i}", [128, TB], BF16) for i in range(2)]
        xt = [sb(C, st, f"px_x{i}", [128, TB], F32) for i in range(2)]
        hv = hT_d.rearrange("(k p) t -> p k t", p=128)
        rv = rout_d.rearrange("a s t -> s a t")
        vv = v_d.rearrange("(i j) d -> j i d", j=128)
        xv = xT_d.rearrange("(k p) t -> k p t", p=128)
        nld = 0
        for blk in range(NT // TB):
            c0 = blk * TB
            S.dma("sp", hT, hv[:, :, c0:c0 + TB], ["hT_d"], ["px_hT"])
            S.dma("sp", rt, rv[:, :, c0:c0 + TB], ["rout_d"], ["px_rt"])
            def onehots(sbi):
                tb0 = sbi * SBT
                Ab, Bb = A[sbi % 2], B[sbi % 2]
                i1b = vw(rt, tb0, [[1, SBT], [0, 128]])
                i2b = vw(rt, TB + tb0, [[1, SBT], [0, 128]])
                gb = vw(rt, 2 * TB + tb0, [[1, SBT], [0, 128]])
                io = vw(C.iota, 0, [[0, SBT], [1, 128]])
                S.op("dve", lambda e, Ab=Ab, i1b=i1b, io=io: e.tensor_tensor(out=Ab, in0=i1b, in1=io, op=ALU.is_equal),
                     ["px_rt"], [f"px_A{sbi % 2}"])
                S.op("dve", lambda e, Bb=Bb, i2b=i2b, io=io: e.tensor_tensor(out=Bb, in0=i2b, in1=io, op=ALU.is_equal),
                     ["px_rt"], [f"px_B{sbi % 2}"])
                S.op("pool", lambda e, Bb=Bb, gb=gb: e.tensor_tensor(out=Bb, in0=Bb, in1=gb, op=ALU.mult),
                     ["px_rt", f"px_B{sbi % 2}"], [f"px_B{sbi % 2}"])

            nsb = TB // SBT
            onehots(0)
            for sbi in range(nsb):
                if sbi + 1 < nsb:
                    onehots(sbi + 1)
                tb0 = sbi * SBT
                Ab, Bb = A[sbi % 2], B[sbi % 2]
                for q4 in range(SBT // 4):
                    pb = 4 + (sbi * (SBT // 4) + q4) % 2
                    for t in range(4):
                        tl = q4 * 4 + t
                        S.op("pe", lambda e, Ab=Ab, Bb=Bb, tl=tl, t=t, pb=pb: e.matmul(
                            ps[pb][:, t * 128:(t + 1) * 128], lhsT=Ab[:, tl, :], rhs=Bb[:, tl, :], start=True, stop=True),
                            [f"px_A{sbi % 2}", f"px_B{sbi % 2}"], [f"ps{pb}"])
                    tg = tb0 + q4 * 4
                    outv = vw(G, tg, [[TB, 128], [1, 4]])
                    inv = ps[pb].rearrange("p (t j) -> p j t", t=4)
                    S.op("act", lambda e, outv=outv, inv=inv: e.activation(out=outv, in_=inv, func=AF.Copy), [f"ps{pb}"], ["px_G"])
            ubuf = {}
            for jx in range(128 + 2):
                if jx < 128:
                    j = jx
                    s_, w_ = stg[nld % NST], wt[nld % 3]
                    sk, wk = f"px_s{nld % NST}", f"px_w{nld % 3}"
                    nld += 1
                    S.dma("sp", s_, uT_d[j].rearrange("p k i -> p (k i)"), ["uT_d"], [sk])
                    if j % 2 == 0:
                        S.op("act", lambda e, s_=s_, w_=w_: e.activation(out=w_, in_=s_, func=AF.Copy), [sk], [wk])
                    else:
                        S.op("dve", lambda e, s_=s_, w_=w_: e.tensor_copy(out=w_, in_=s_), [sk], [wk])
                    ubuf[j] = (w_, wk)
                j = jx - 2
                if j < 0:
                    continue
                w_, wk = ubuf.pop(j)
                pb = 6 + j % 2
                for k in range(8):
                    S.op("pe", lambda e, w_=w_, k=k, pb=pb: e.matmul(ps[pb], lhsT=w_[:, k * 128:(k + 1) * 128], rhs=hT[:, k, :], start=(k == 0), stop=(k == 7)),
                         [wk, "px_hT"], [f"ps{pb}"])
                g_ = gl[j % 2]
                S.op("act", lambda e, g_=g_, pb=pb: e.activation(out=g_, in_=ps[pb], func=AF.Gelu), [f"ps{pb}"], [f"px_gl{j % 2}"])
                S.op("dve", lambda e, g_=g_, j=j: e.tensor_tensor(out=G[:, j, :], in0=G[:, j, :], in1=g_, op=ALU.mult),
                     [f"px_gl{j % 2}", "px_G"], ["px_G"])
            for j in range(128):
                s_, w_ = stg[nld % NST], wt[nld % 3]
                sk, wk = f"px_s{nld % NST}", f"px_w{nld % 3}"
                nld += 1
                S.dma("sp", s_, vv[j], ["v_d"], [sk])
                if j % 2 == 0:
                    S.op("act", lambda e, s_=s_, w_=w_: e.activation(out=w_, in_=s_, func=AF.Copy), [sk], [wk])
                else:
                    S.op("dve", lambda e, s_=s_, w_=w_: e.tensor_copy(out=w_, in_=s_), [sk], [wk])
                for c in range(8):
                    S.op("pe", lambda e, w_=w_, c=c, j=j: e.matmul(ps[c], lhsT=w_[:, c * 128:(c + 1) * 128], rhs=G[:, j, :],
                                                                  start=(j == 0), stop=(j == 127)),
                         [wk, "px_G"], [f"ps{c}"])
            for ch in range(8):
                x_ = xt[ch % 2]
                S.dma("sp", x_, xv[ch, :, c0:c0 + TB], ["xT_d"], [f"px_x{ch % 2}"])
                for (sl, m) in col_pieces(c0):
                    S.op("dve", lambda e, x_=x_, sl=sl, ch=ch, m=m: e.scalar_tensor_tensor(
                        out=x_[:, sl], in0=ps[ch][:, sl], scalar=g2cols(ch, m), in1=x_[:, sl], op0=ALU.mult, op1=ALU.add),
                        [f"ps{ch}", f"px_x{ch % 2}", "mods"], [f"px_x{ch % 2}"])
                S.dma("sp", xv[ch, :, c0:c0 + TB], x_, [f"px_x{ch % 2}"], ["xT_d"])
    S.barrier()


EPS = 1e-6
SEG = 2304


def seg_m(col):
    s, pos = divmod(col, SEG)
    return 2 if pos < 256 else s


def col_pieces(c0):
    m0, m1 = seg_m(c0), seg_m(c0 + 256)
    if m0 == m1:
        return [(slice(0, 512), m0)]
    return [(slice(0, 256), m0), (slice(256, 512), m1)]


def adaln(C, l, adaw_d, adabT_d, ngT_d):
    S, ps = C.S, C.ps
    S.barrier()
    mods, modA = C.mods[l], C.modA[l]
    with contextlib.ExitStack() as st:
        w = [sb(C, st, f"ad_w{i}", [128, 8, 512], F32) for i in range(2)]
        bias = sb(C, st, "ad_b", [128, 48], F32)
        ng = sb(C, st, "ad_ng", [128, 2, 8], F32)
        tmp = sb(C, st, "ad_tmp", [128, 8, 3], F32)
        S.dma("sp", bias, adabT_d, ["adab"], ["ad_b"])
        S.dma("sp", ng, ngT_d, ["ng"], ["ad_ng"])
        wv = adaw_d.rearrange("(k p) f -> p k f", p=128)
        for cb in range(12):
            w_ = w[cb % 2]
            S.dma("sp", w_, wv[:, :, cb * 512:(cb + 1) * 512], ["adaw"], [f"ad_w{cb % 2}"])
            for j in range(4):
                col = (cb * 4 + j) * 3
                for k in range(8):
                    S.op("pe", lambda e, w_=w_, j=j, k=k, col=col: e.matmul(
                        ps[0][:, col:col + 3], lhsT=w_[:, k, j * 128:(j + 1) * 128], rhs=C.scT[:, k, :],
                        start=(k == 0), stop=(k == 7)), [f"ad_w{cb % 2}", "scT"], ["ps0"])
        S.op("dve", lambda e: e.tensor_tensor(out=mods, in0=ps[0][:, 0:144].rearrange("p (j m) -> p j m", m=3),
                                              in1=vw(bias, 0, [[1, 48], [0, 3]]), op=ALU.add), ["ps0", "ad_b"], ["mods"])
        for n, base in ((0, 8), (1, 32)):
            S.op("dve", lambda e, base=base: e.tensor_scalar(out=tmp, in0=mods[:, base:base + 8, :], scalar1=1.0, scalar2=None, op0=ALU.add),
                 ["mods"], ["ad_tmp"])
            S.op("dve", lambda e, n=n: e.tensor_tensor(out=modA[:, n, :, :], in0=tmp, in1=vw(ng, n * 8, [[1, 8], [0, 3]]), op=ALU.mult),
                 ["ad_tmp", "ad_ng"], ["mods"])
    S.barrier()


def norm_mod(C, l, which, xT_d, hT_d, NT):
    S, ps = C.S, C.ps
    S.barrier()
    mods, modA = C.mods[l], C.modA[l]
    shb = 0 if which == 0 else 24
    with contextlib.ExitStack() as st:
        x = [sb(C, st, f"nm_x{i}", [128, 8, 512], F32) for i in range(2)]
        sq = sb(C, st, "nm_sq", [128, 8, 512], F32)
        r = sb(C, st, "nm_r", [128, 512], F32)
        t1 = sb(C, st, "nm_t1", [128, 8, 512], F32)
        h = [sb(C, st, f"nm_h{i}", [128, 8, 512], BF16) for i in range(2)]
        xv = xT_d.rearrange("(k p) t -> p k t", p=128)
        hv = hT_d.rearrange("(k p) t -> p k t", p=128)
        for ti in range(NT // 512):
            c0 = ti * 512
            x_, h_ = x[ti % 2], h[ti % 2]
            S.dma("sp", x_, xv[:, :, c0:c0 + 512], ["xT_d"], [f"nm_x{ti % 2}"])
            S.op("act", lambda e, x_=x_: e.activation(out=sq, in_=x_, func=AF.Square), [f"nm_x{ti % 2}"], ["nm_sq"])
            for k in range(8):
                S.op("pe", lambda e, k=k: e.matmul(ps[1], lhsT=C.onesm, rhs=sq[:, k, :], start=(k == 0), stop=(k == 7)),
                     ["nm_sq", "cst"], ["ps1"])
            S.op("act", lambda e: e.activation(out=r, in_=ps[1], func=AF.Sqrt, bias=C.epsc, scale=1.0), ["ps1", "cst"], ["nm_r"])
            S.op("dve", lambda e: e.reciprocal(out=r, in_=r), ["nm_r"], ["nm_r"])
            for k in range(8):
                for (sl, m) in col_pieces(c0):
                    S.op("dve", lambda e, x_=x_, k=k, sl=sl, m=m: e.scalar_tensor_tensor(
                        out=t1[:, k, sl], in0=x_[:, k, sl], scalar=modA[:, which, k, m:m + 1], in1=r[:, sl], op0=ALU.mult, op1=ALU.mult),
                        [f"nm_x{ti % 2}", "nm_r", "mods"], ["nm_t1"])
                    S.op("act", lambda e, h_=h_, k=k, sl=sl, m=m: e.activation(
                        out=h_[:, k, sl], in_=t1[:, k, sl], func=AF.Identity, bias=mods[:, shb + k, m:m + 1], scale=1.0),
                        ["nm_t1", "mods"], [f"nm_h{ti % 2}"])
            S.dma("sp", hv[:, :, c0:c0 + 512], h_, [f"nm_h{ti % 2}"], ["hT_d"])
    S.barrier()


def resid_linear(C, l, inT_d, KC, w_d, gbase, xT_d, NT):
    S, ps = C.S, C.ps
    S.barrier()
    mods = C.mods[l]
    with contextlib.ExitStack() as st:
        w = sb(C, st, "rl_w", [128, KC, 1024], BF16)
        wst = sb(C, st, "rl_wst", [128, KC, 256], F32)
        a = [sb(C, st, f"rl_a{i}", [128, KC, 512], BF16) for i in range(2)]
        x = [sb(C, st, f"rl_x{i}", [128, 512], F32) for i in range(2)]
        wv = w_d.rearrange("(k p) f -> p k f", p=128)
        for c in range(4):
            S.dma("sp", wst, wv[:, :, c * 256:(c + 1) * 256], ["w_d"], ["rl_wst"])
            S.op("act", lambda e, c=c: e.activation(out=w[:, :, c * 256:(c + 1) * 256], in_=wst, func=AF.Copy), ["rl_wst"], ["rl_w"])
        av = inT_d.rearrange("(k p) t -> p k t", p=128)
        xv = xT_d.rearrange("(k p) t -> k p t", p=128)
        n = 0
        for ti in range(NT // 512):
            c0 = ti * 512
            a_ = a[ti % 2]
            S.dma("sp", a_, av[:, :, c0:c0 + 512], ["inT_d"], [f"rl_a{ti % 2}"])
            for oc in range(8):
                pb = 2 + n % 2
                x_ = x[n % 2]
                for k in range(KC):
                    S.op("pe", lambda e, a_=a_, k=k, oc=oc, pb=pb: e.matmul(ps[pb], lhsT=w[:, k, oc * 128:(oc + 1) * 128], rhs=a_[:, k, :],
                                                                         start=(k == 0), stop=(k == KC - 1)),
                         ["rl_w", f"rl_a{ti % 2}"], [f"ps{pb}"])
                S.dma("sp", x_, xv[oc, :, c0:c0 + 512], ["xT_d"], [f"rl_x{n % 2}"])
                for (sl, m) in col_pieces(c0):
                    S.op("dve", lambda e, x_=x_, pb=pb, sl=sl, oc=oc, m=m: e.scalar_tensor_tensor(
                        out=x_[:, sl], in0=ps[pb][:, sl], scalar=mods[:, gbase + oc, m:m + 1], in1=x_[:, sl], op0=ALU.mult, op1=ALU.add),
                        [f"ps{pb}", f"rl_x{n % 2}", "mods"], [f"rl_x{n % 2}"])
                S.dma("sp", xv[oc, :, c0:c0 + 512], x_, [f"rl_x{n % 2}"], ["xT_d"])
                n += 1
    S.barrier()


def na_proj(C, hT_d, wqkv_d, qg2_d, kg2_d, qkT_d, v0_d, v1_d, NT):
    S, ps = C.S, C.ps
    NS = NT // SEG
    S.barrier()
    with contextlib.ExitStack() as st:
        hT = sb(C, st, "np_hT", [128, 8, NT], BF16)
        wst = [sb(C, st, f"np_wst{i}", [128, 8, 128], F32) for i in range(2)]
        wb = [sb(C, st, f"np_wb{i}", [128, 8, 128], BF16) for i in range(2)]
        raw = sb(C, st, "np_raw", [128, 512], F32)
        sq = sb(C, st, "np_sq", [128, 512], F32)
        r = sb(C, st, "np_r", [128, 512], F32)
        row = [sb(C, st, f"np_row{i}", [128, NT], BF16) for i in range(2)]
        gc = sb(C, st, "np_gc", [128, 2], F32)
        wv = sb(C, st, "np_wv", [128, 8, 1024], BF16)
        wvst = sb(C, st, "np_wvst", [128, 8, 512], F32)
        vo = [sb(C, st, f"np_vo{i}", [128, 1024], BF16) for i in range(2)]
        hv = hT_d.rearrange("(k p) t -> p k t", p=128)
        for ti in range(NT // 512):
            S.dma("sp", hT[:, :, ti * 512:(ti + 1) * 512], hv[:, :, ti * 512:(ti + 1) * 512], ["hT_d"], ["np_hT"])
        S.dma("sp", gc[:, 0:1], qg2_d, ["qg"], ["np_gc"])
        S.dma("sp", gc[:, 1:2], kg2_d, ["kg"], ["np_gc"])
        wqv = wqkv_d.rearrange("(k p) f -> p k f", p=128)
        for j in range(16):
            ws_, wb_, row_ = wst[j % 2], wb[j % 2], row[j % 2]
            S.dma("sp", ws_, wqv[:, :, j * 128:(j + 1) * 128], ["wqkv"], [f"np_wst{j % 2}"])
            S.op("act", lambda e, ws_=ws_, wb_=wb_: e.activation(out=wb_, in_=ws_, func=AF.Copy), [f"np_wst{j % 2}"], [f"np_wb{j % 2}"])
            for ti in range(NT // 512):
                c0 = ti * 512
                for k in range(8):
                    S.op("pe", lambda e, wb_=wb_, k=k, c0=c0: e.matmul(ps[0], lhsT=wb_[:, k, :], rhs=hT[:, k, c0:c0 + 512], start=(k == 0), stop=(k == 7)),
                         [f"np_wb{j % 2}", "np_hT"], ["ps0"])
                S.op("act", lambda e: e.activation(out=raw, in_=ps[0], func=AF.Copy), ["ps0"], ["np_raw"])
                S.op("act", lambda e: e.activation(out=sq, in_=ps[0], func=AF.Square), ["ps0"], ["np_sq"])
                S.op("pe", lambda e: e.matmul(ps[1], lhsT=C.bd64, rhs=sq, start=True, stop=True), ["np_sq", "cst"], ["ps1"])
                S.op("act", lambda e: e.activation(out=r, in_=ps[1], func=AF.Sqrt, bias=C.epsc, scale=1.0), ["ps1", "cst"], ["np_r"])
                S.op("dve", lambda e: e.reciprocal(out=r, in_=r), ["np_r"], ["np_r"])
                gi = 0 if j < 8 else 1
                S.op("dve", lambda e, row_=row_, c0=c0, gi=gi: e.scalar_tensor_tensor(
                    out=row_[:, c0:c0 + 512], in0=raw, scalar=gc[:, gi:gi + 1], in1=r, op0=ALU.mult, op1=ALU.mult),
                    ["np_raw", "np_r", "np_gc"], [f"np_row{j % 2}"])
            S.dma("sp", qkT_d[j], row_, [f"np_row{j % 2}"], ["qkT_d"])
        for c in range(2):
            S.dma("sp", wvst, wqv[:, :, 2048 + c * 512:2048 + (c + 1) * 512], ["wqkv"], ["np_wvst"])
            S.op("act", lambda e, c=c: e.activation(out=wv[:, :, c * 512:(c + 1) * 512], in_=wvst, func=AF.Copy), ["np_wvst"], ["np_wv"])
        jobs = [(v0_d, T * 128, T * 128) for T in range(NT // 128)]
        for s in range(NS):
            for T in range(15):
                jobs.append((v1_d, (s * 15 + T) * 128, s * SEG + 320 + 128 * T))
        for n, (dst, r0, c0) in enumerate(jobs):
            vo_ = vo[n % 2]
            for hf in range(2):
                pb = 2 + hf
                for k in range(8):
                    S.op("pe", lambda e, k=k, c0=c0, hf=hf, pb=pb: e.matmul(ps[pb], lhsT=hT[:, k, c0:c0 + 128], rhs=wv[:, k, hf * 512:(hf + 1) * 512],
                                                                         start=(k == 0), stop=(k == 7)), ["np_hT", "np_wv"], [f"ps{pb}"])
                S.op("act", lambda e, vo_=vo_, hf=hf, pb=pb: e.activation(out=vo_[:, hf * 512:(hf + 1) * 512], in_=ps[pb], func=AF.Copy),
                     [f"ps{pb}"], [f"np_vo{n % 2}"])
            S.dma("sp", dst[r0:r0 + 128, :], vo_, [f"np_vo{n % 2}"], ["v_d"])
    S.barrier()


def na_attn(C, qkT_d, v0_d, v1_d, rpbG_d, oT_d, NT):
    S, ps = C.S, C.ps
    NS = NT // SEG
    S.barrier()
    with contextlib.ExitStack() as st:
        EB = sb(C, st, "na_EB", [128, 16, 15, 64], BF16)
        rst = sb(C, st, "na_rst", [128, 15, 64], F32)
        q = [sb(C, st, f"na_q{i}", [128, SEG], BF16) for i in range(2)]
        kk = [sb(C, st, f"na_k{i}", [128, SEG], BF16) for i in range(2)]
        V0 = [sb(C, st, f"na_v0{i}", [128, 18, 128], BF16) for i in range(2)]
        V1 = [sb(C, st, f"na_v1{i}", [128, 15, 128], BF16) for i in range(2)]
        o = [sb(C, st, f"na_o{i}", [128, SEG], BF16) for i in range(2)]
        P = [sb(C, st, f"na_P{i}", [128, 512], BF16) for i in range(3)]
        rd = [sb(C, st, f"na_rd{i}", [128, 512], F32) for i in range(2)]
        for h in range(16):
            S.dma("sp", rst, rpbG_d[:, h], ["rpbG"], ["na_rst"])
            S.op("act", lambda e: e.activation(out=rst, in_=rst, func=AF.Exp), ["na_rst"], ["na_rst"])
            S.op("dve", lambda e, h=h: e.tensor_tensor(out=EB[:, h], in0=rst, in1=vw(C.mask01, 0, [[0, 15], [1, 64]]), op=ALU.mult),
                 ["na_rst", "cst"], ["na_EB"])
        v0v = v0_d.rearrange("(t p) f -> p t f", p=128)
        v1v = v1_d.rearrange("(t p) f -> p t f", p=128)
        it = 0
        npx = 0
        for s in range(NS):
            for j in range(8):
                b = it % 2
                it += 1
                q_, k_, V0_, V1_, o_ = q[b], kk[b], V0[b], V1[b], o[b]
                S.dma("sp", q_, qkT_d[j][:, s * SEG:(s + 1) * SEG], ["qkT_d"], [f"na_q{b}"])
                S.dma("sp", k_, qkT_d[8 + j][:, s * SEG:(s + 1) * SEG], ["qkT_d"], [f"na_k{b}"])
                S.dma("sp", V0_, v0v[:, s * 18:(s + 1) * 18, j * 128:(j + 1) * 128], ["v_d"], [f"na_v0{b}"])
                S.dma("sp", V1_, v1v[:, s * 15:(s + 1) * 15, j * 128:(j + 1) * 128], ["v_d"], [f"na_v1{b}"])
                rdeps = [f"na_q{b}", f"na_k{b}"]
                items = []
                for hh in range(2):
                    items.append((hh, "ctx", -1))
                    for r in range(32):
                        items.append((hh, "row", r))
                LAG = 1
                pend = {}

                def stage1(hh, kind, r, npx):
                    h = 2 * j + hh
                    lo, hi = 64 * hh, 64 * hh + 64
                    sbk = npx % 2
                    P_ = P[npx % 3]
                    pk = f"na_P{npx % 3}"
                    if kind == "ctx":
                        for c in range(2):
                            S.op("pe", lambda e, c=c, lo=lo, hi=hi, sbk=sbk, k_=k_, q_=q_: e.matmul(
                                ps[sbk][:, c * 256:(c + 1) * 256], lhsT=k_[lo:hi, c * 128:(c + 1) * 128], rhs=q_[lo:hi, 0:256], start=True, stop=True),
                                rdeps, [f"ps{sbk}"])
                        S.op("act", lambda e, P_=P_, sbk=sbk: e.activation(out=P_, in_=ps[sbk], func=AF.Exp, scale=0.125), [f"ps{sbk}"], [pk])
                        return (P_, pk, None)
                    rs = min(max(r - 4, 0), 24)
                    qc = 256 + 64 * r
                    kcols = [256 + 64 * (rs + 2 * c) for c in range(4)] + [0, 128]
                    for c in range(6):
                        S.op("pe", lambda e, c=c, lo=lo, hi=hi, sbk=sbk, kc=kcols[c], qc=qc, k_=k_, q_=q_: e.matmul(
                            ps[sbk][:, c * 64:(c + 1) * 64], lhsT=k_[lo:hi, kc:kc + 128], rhs=q_[lo:hi, qc:qc + 64], start=True, stop=True),
                            rdeps, [f"ps{sbk}"])
                    S.op("act", lambda e, P_=P_, sbk=sbk: e.activation(out=P_[:, 0:384], in_=ps[sbk][:, 0:384], func=AF.Exp, scale=0.125),
                         [f"ps{sbk}"], [pk])
                    d0 = rs - r + 7
                    ebv = vw(EB, (h * 15 + d0) * 64, [[128, 4], [1, 64]])
                    S.op("dve", lambda e, P_=P_, ebv=ebv: e.tensor_tensor(out=P_[:, 0:256].rearrange("p (c q) -> p c q", c=4),
                                                                          in0=P_[:, 0:256].rearrange("p (c q) -> p c q", c=4), in1=ebv, op=ALU.mult),
                         [pk, "na_EB"], [pk])
                    return (P_, pk, rs)

                def stage2(hh, kind, r, P_, pk, rs):
                    lo, hi = 64 * hh, 64 * hh + 64
                    if kind == "ctx":
                        for c in range(2):
                            S.op("pe", lambda e, c=c, lo=lo, hi=hi, P_=P_, V0_=V0_: e.matmul(
                                ps[2][lo:hi, 0:256], lhsT=V0_[:, c, lo:hi], rhs=P_[:, c * 256:(c + 1) * 256], start=(c == 0), stop=(c == 1)),
                                [f"na_v0{b}", pk], ["ps2"])
                        for c in range(2):
                            S.op("pe", lambda e, c=c, lo=lo, hi=hi, P_=P_: e.matmul(
                                ps[3][lo:hi, 0:256], lhsT=C.onesb[:, 0:64], rhs=P_[:, c * 256:(c + 1) * 256], start=(c == 0), stop=(c == 1)),
                                ["cst", pk], ["ps3"])
                        rd_ = rd[0]
                        S.op("dve", lambda e, rd_=rd_, lo=lo, hi=hi: e.reciprocal(out=rd_[lo:hi, 0:256], in_=ps[3][lo:hi, 0:256]), ["ps3"], ["na_rd0"])
                        S.op("dve", lambda e, rd_=rd_, lo=lo, hi=hi, o_=o_: e.tensor_tensor(out=o_[lo:hi, 0:256], in0=ps[2][lo:hi, 0:256], in1=rd_[lo:hi, 0:256], op=ALU.mult),
                             ["ps2", "na_rd0"], [f"na_o{b}"])
                        return
                    rg, rr = divmod(r, 8)
                    ob, db = 4 + 2 * (rg % 2), 5 + 2 * (rg % 2)
                    vts = []
                    for c in range(4):
                        row0 = rs + 2 * c
                        if rs % 2 == 0:
                            vts.append((V0_, 2 + row0 // 2, f"na_v0{b}"))
                        else:
                            vts.append((V1_, (row0 - 1) // 2, f"na_v1{b}"))
                    vts += [(V0_, 0, f"na_v0{b}"), (V0_, 1, f"na_v0{b}")]
                    for c in range(6):
                        Vt, ti, vk = vts[c]
                        S.op("pe", lambda e, Vt=Vt, ti=ti, P_=P_, c=c, lo=lo, hi=hi, ob=ob, rr=rr: e.matmul(
                            ps[ob][lo:hi, rr * 64:(rr + 1) * 64], lhsT=Vt[:, ti, lo:hi], rhs=P_[:, c * 64:(c + 1) * 64], start=(c == 0), stop=(c == 5)),
                            [vk, pk], [f"ps{ob}"])
                    for c in range(6):
                        S.op("pe", lambda e, P_=P_, c=c, lo=lo, hi=hi, db=db, rr=rr: e.matmul(
                            ps[db][lo:hi, rr * 64:(rr + 1) * 64], lhsT=C.onesb[:, 0:64], rhs=P_[:, c * 64:(c + 1) * 64], start=(c == 0), stop=(c == 5)),
                            ["cst", pk], [f"ps{db}"])
                    if rr == 7:
                        rd_ = rd[rg % 2]
                        oc0 = 256 + rg * 512
                        S.op("dve", lambda e, rd_=rd_, lo=lo, hi=hi, db=db: e.reciprocal(out=rd_[lo:hi, :], in_=ps[db][lo:hi, :]), [f"ps{db}"], [f"na_rd{rg % 2}"])
                        S.op("dve", lambda e, rd_=rd_, lo=lo, hi=hi, ob=ob, oc0=oc0, o_=o_: e.tensor_tensor(
                            out=o_[lo:hi, oc0:oc0 + 512], in0=ps[ob][lo:hi, :], in1=rd_[lo:hi, :], op=ALU.mult),
                            [f"ps{ob}", f"na_rd{rg % 2}"], [f"na_o{b}"])

                for ix in range(len(items) + LAG):
                    if ix < len(items):
                        pend[ix] = stage1(*items[ix], npx)
                        npx += 1
                    jx = ix - LAG
                    if jx >= 0:
                        stage2(*items[jx], *pend.pop(jx))
                S.dma("sp", oT_d[j][:, s * SEG:(s + 1) * SEG], o_, [f"na_o{b}"], ["oT_d"])
    S.barrier()


def conv_segments(NT):
    segs = []
    for s in range(NT // SEG):
        segs.append((s * SEG, 256))
        segs.append((s * SEG + 256, 2048))
    return segs


def even_inproj(C, hT_d, win_d, cw_d, cb_d, dtb_d, alog_d, xbcT_d, lxT_d, zT_d, gT_d, csT_d, cstok_d, xd_d, NT):
    S, ps = C.S, C.ps
    S.barrier()
    NTT = NT // 128
    with contextlib.ExitStack() as st:
        hT = sb(C, st, "ei_hT", [128, 8, NT], BF16)
        prow0 = sb(C, st, "ei_prow", [128, NT], F32)
        yrow = sb(C, st, "ei_yrow", [128, NT], F32)
        cw = sb(C, st, "ei_cw", [128, 24, 4], F32)
        cb = sb(C, st, "ei_cb", [128, 24], F32)
        dtb = sb(C, st, "ei_dtb", [64, 2], F32)
        st1 = contextlib.ExitStack()
        wst = [sb(C, st1, f"ei_wst{i}", [128, 8, 128], F32) for i in range(2)]
        wb = [sb(C, st1, f"ei_wb{i}", [128, 8, 128], BF16) for i in range(2)]
        prow1 = sb(C, st1, "ei_prow1", [128, NT], F32)
        orow = sb(C, st1, "ei_orow", [128, NT], BF16)
        prows = [prow0, prow1]
        hv = hT_d.rearrange("(k p) t -> p k t", p=128)
        for ti in range(NT // 512):
            S.dma("sp", hT[:, :, ti * 512:(ti + 1) * 512], hv[:, :, ti * 512:(ti + 1) * 512], ["hT_d"], ["ei_hT"])
        S.dma("sp", cw, cw_d, ["cw_d"], ["ei_cw"])
        S.dma("sp", cb, cb_d, ["cb_d"], ["ei_cb"])
        S.dma("sp", dtb[:, 0:1], dtb_d, ["dtb_d"], ["ei_dtb"])
        S.dma("sp", dtb[:, 1:2], alog_d, ["alog_d"], ["ei_dtb"])
        wv = win_d.rearrange("(k p) f -> p k f", p=128)
        n = 0
        for j in range(41):
            ws_, wb_ = wst[j % 2], wb[j % 2]
            prow = prows[j % 2]
            prk = f"ei_prow{j % 2}"
            fw = 128 if j < 40 else 64
            S.dma("sp", ws_[:, :, 0:fw], wv[:, :, j * 128:j * 128 + fw], ["win_d"], [f"ei_wst{j % 2}"])
            S.op("act", lambda e, ws_=ws_, wb_=wb_, fw=fw: e.activation(out=wb_[:, :, 0:fw], in_=ws_[:, :, 0:fw], func=AF.Copy),
                 [f"ei_wst{j % 2}"], [f"ei_wb{j % 2}"])
            for ti in range(NT // 512):
                c0 = ti * 512
                pb = n % 2
                n += 1
                for k in range(8):
                    S.op("pe", lambda e, wb_=wb_, k=k, c0=c0, pb=pb, fw=fw: e.matmul(ps[pb][0:fw, :], lhsT=wb_[:, k, 0:fw], rhs=hT[:, k, c0:c0 + 512],
                                                                                start=(k == 0), stop=(k == 7)), [f"ei_wb{j % 2}", "ei_hT"], [f"ps{pb}"])
                fn = AF.Copy if (j < 24 or j == 40) else (AF.Silu if j < 32 else AF.Gelu)
                S.op("act", lambda e, pb=pb, c0=c0, fn=fn, fw=fw, prow=prow: e.activation(out=prow[0:fw, c0:c0 + 512], in_=ps[pb][0:fw, :], func=fn),
                     [f"ps{pb}"], [prk])
            if j < 24:
                for (s0, L) in conv_segments(NT):
                    S.op("dve", lambda e, j=j, s0=s0, L=L, prow=prow: e.tensor_scalar(out=yrow[:, s0:s0 + L], in0=prow[:, s0:s0 + L], scalar1=cw[:, j, 2:3],
                                                                          scalar2=cb[:, j:j + 1], op0=ALU.mult, op1=ALU.add), [prk, "ei_cw", "ei_cb"], ["ei_yrow"])
                    for (kk, do, so, ln) in ((1, 1, 0, L - 1), (0, 2, 0, L - 2), (3, 0, 1, L - 1)):
                        S.op("dve", lambda e, j=j, s0=s0, kk=kk, do=do, so=so, ln=ln, prow=prow: e.scalar_tensor_tensor(
                            out=yrow[:, s0 + do:s0 + do + ln], in0=prow[:, s0 + so:s0 + so + ln], scalar=cw[:, j, kk:kk + 1],
                            in1=yrow[:, s0 + do:s0 + do + ln], op0=ALU.mult, op1=ALU.add), [prk, "ei_cw", "ei_yrow"], ["ei_yrow"])
                if j < 16:
                    S.op("act", lambda e: e.activation(out=orow, in_=yrow, func=AF.Silu), ["ei_yrow"], ["ei_orow"])
                    S.dma("sp", xbcT_d[j * 128:(j + 1) * 128, :], orow, ["ei_orow"], ["xbcT_d"])
                else:
                    S.dma("sp", lxT_d[(j - 16) * 128:(j - 15) * 128, :], yrow, ["ei_yrow"], ["lxT_d"])
            elif j < 32:
                S.dma("sp", zT_d[(j - 24) * 128:(j - 23) * 128, :], prow, [prk], ["zT_d"])
            elif j < 40:
                S.dma("sp", gT_d[(j - 32) * 128:(j - 31) * 128, :], prow, [prk], ["gT_d"])
        st1.close()
        S.barrier()
        prow = prow0
        xx = yrow[0:64, :]
        t2 = sb(C, st, "ei_t2", [64, NT], F32)
        ea = sb(C, st, "ei_ea", [64, 1], F32)
        S.op("dve", lambda e: e.tensor_scalar(out=xx, in0=prow[0:64, :], scalar1=dtb[:, 0:1], scalar2=None, op0=ALU.add), ["ei_prow0", "ei_dtb"], ["ei_yrow"])
        S.op("act", lambda e: e.activation(out=t2, in_=xx, func=AF.Abs), ["ei_yrow"], ["ei_t2"])
        S.op("act", lambda e: e.activation(out=t2, in_=t2, func=AF.Exp, scale=-1.0), ["ei_t2"], ["ei_t2"])
        S.op("act", lambda e: e.activation(out=t2, in_=t2, func=AF.Ln, bias=C.onec[0:64, :], scale=1.0), ["ei_t2", "cst"], ["ei_t2"])
        S.op("dve", lambda e: e.scalar_tensor_tensor(out=xx, in0=xx, scalar=0.0, in1=t2, op0=ALU.max, op1=ALU.add), ["ei_yrow", "ei_t2"], ["ei_yrow"])
        S.op("act", lambda e: e.activation(out=ea, in_=dtb[:, 1:2], func=AF.Exp), ["ei_dtb"], ["ei_ea"])
        la = prow[0:64, :]
        S.op("dve", lambda e: e.tensor_scalar(out=la, in0=xx, scalar1=ea[:, 0:1], scalar2=-1.0, op0=ALU.mult, op1=ALU.mult), ["ei_yrow", "ei_ea"], ["ei_prow0"])
        cs = t2
        for s in range(NT // SEG):
            c0 = s * SEG
            S.op("dve", lambda e, c0=c0: e.tensor_tensor_scan(out=cs[0:32, c0:c0 + SEG], data0=vw(C.onec[0:32, :], 0, [[0, SEG]]), data1=la[0:32, c0:c0 + SEG],
                                                               initial=0.0, op0=ALU.mult, op1=ALU.add), ["ei_prow0", "cst"], ["ei_t2"])

            def rev(ap, a0, L):
                return vw(ap[32:64, :], a0 + L - 1, [[-1, L]])
            S.op("dve", lambda e, c0=c0: e.tensor_tensor_scan(out=rev(cs, c0, 256), data0=vw(C.onec[32:64, :], 0, [[0, 256]]), data1=rev(la, c0, 256),
                                                               initial=0.0, op0=ALU.mult, op1=ALU.add), ["ei_prow0", "cst"], ["ei_t2"])
            S.op("dve", lambda e, c0=c0: e.tensor_tensor_scan(out=rev(cs, c0 + 256, 2048), data0=vw(C.onec[32:64, :], 0, [[0, 2048]]), data1=rev(la, c0 + 256, 2048),
                                                               initial=cs[32:64, c0:c0 + 1], op0=ALU.mult, op1=ALU.add), ["ei_prow0", "cst", "ei_t2"], ["ei_t2"])
        S.dma("sp", csT_d, cs, ["ei_t2"], ["csT_d"])
        cstok = sb(C, st, "ei_cstok", [128, NTT, 64], F32)
        dttok = sb(C, st, "ei_dttok", [128, NTT, 64], F32)
        for tt in range(NTT):
            pb = 2 + tt % 2
            S.op("pe", lambda e, tt=tt, pb=pb: e.transpose(out=ps[pb][:, 0:64], in_=cs[:, tt * 128:(tt + 1) * 128], identity=C.ident[0:64, 0:64]),
                 ["ei_t2", "cst"], [f"ps{pb}"])
            S.op("pe", lambda e, tt=tt, pb=pb: e.transpose(out=ps[pb][:, 64:128], in_=xx[:, tt * 128:(tt + 1) * 128], identity=C.ident[0:64, 0:64]),
                 ["ei_yrow", "cst"], [f"ps{pb}"])
            S.op("act", lambda e, tt=tt, pb=pb: e.activation(out=cstok[:, tt, :], in_=ps[pb][:, 0:64], func=AF.Copy), [f"ps{pb}"], ["ei_cstok"])
            S.op("act", lambda e, tt=tt, pb=pb: e.activation(out=dttok[:, tt, :], in_=ps[pb][:, 64:128], func=AF.Copy), [f"ps{pb}"], ["ei_dttok"])
        S.dma("sp", cstok_d.rearrange("(t p) f -> p t f", p=128), cstok, ["ei_cstok"], ["cstok_d"])
        xb = [sb(C, st, f"ei_xb{i}", [128, 8, 512], BF16) for i in range(2)]
        xdt = [sb(C, st, f"ei_xdt{i}", [128, 1024], BF16) for i in range(2)]
        xbv = xbcT_d[0:1024, :].rearrange("(k p) t -> p k t", p=128)
        m = 0
        for bi in range(NT // 512):
            xb_ = xb[bi % 2]
            S.dma("sp", xb_, xbv[:, :, bi * 512:(bi + 1) * 512], ["xbcT_d"], [f"ei_xb{bi % 2}"])
            for t4 in range(4):
                tt = bi * 4 + t4
                pb = 4 + tt % 2
                pbf = ps[pb].bitcast(BF16)
                for k in range(8):
                    S.op("pe", lambda e, xb_=xb_, k=k, t4=t4, pbf=pbf: e.transpose(out=pbf[:, k * 128:(k + 1) * 128], in_=xb_[:, k, t4 * 128:(t4 + 1) * 128], identity=C.identb),
                         [f"ei_xb{bi % 2}", "cst"], [f"ps{pb}"])
                for d in range(2):
                    xd_ = xdt[m % 2]
                    S.op("dve", lambda e, xd_=xd_, pbf=pbf, tt=tt, d=d: e.tensor_tensor(
                        out=xd_.rearrange("p (h q) -> p h q", h=16), in0=pbf.rearrange("p (h q) -> p h q", h=16),
                        in1=vw(dttok, tt * 64 + d * 32, [[1, 16], [0, 64]]), op=ALU.mult), [f"ps{pb}", "ei_dttok"], [f"ei_xdt{m % 2}"])
                    S.dma("sp", xd_d[d][tt * 128:(tt + 1) * 128, :], xd_, [f"ei_xdt{m % 2}"], ["xd_d"])
                    m += 1
    S.barrier()


def ssd_allowed(d, lt):
    if lt == 0:
        return [(0, 0), (1, 1)]
    base = 2 + 4 * (lt - 1)
    if d == 0:
        return [(stl, None) for stl in range(base)] + [(base + r, r) for r in range(4)]
    return [(0, None), (1, None)] + [(base + r, r) for r in range(4)] + [(stl, None) for stl in range(base + 4, 18)]


def ssd_phase(C, xbcT_d, csT_d, cstok_d, xd_d, zT_d, dcol_d, sg_d, ymixT_d, NT):
    S, ps = C.S, C.ps
    S.barrier()
    with contextlib.ExitStack() as st:
        csT = sb(C, st, "sd_csT", [64, SEG], F32)
        cstok = sb(C, st, "sd_cstok", [128, 18, 64], F32)
        ncstok = sb(C, st, "sd_ncstok", [128, 18, 64], F32)
        Bg = sb(C, st, "sd_B", [128, SEG], BF16)
        Cg = sb(C, st, "sd_C", [128, SEG], BF16)
        xdg = [sb(C, st, f"sd_xd{d}", [128, 18, 256], BF16) for d in range(2)]
        xg = sb(C, st, "sd_x", [128, 2, SEG], BF16)
        csb = [sb(C, st, f"sd_csb{i}", [128, 512], F32) for i in range(8)]
        Gs = [sb(C, st, f"sd_G{i}", [128, 512], BF16) for i in range(3)]
        dd = [sb(C, st, f"sd_dd{i}", [128, 512], F32) for i in range(3)]
        ee = [sb(C, st, f"sd_ee{i}", [128, 512], BF16) for i in range(3)]
        M = [sb(C, st, f"sd_M{i}", [128, 512], BF16) for i in range(3)]
        zt = sb(C, st, "sd_z", [128, 2, 512], F32)
        y = sb(C, st, "sd_y", [128, 2, 512], F32)
        sq = sb(C, st, "sd_sq", [128, 512], F32)
        r = sb(C, st, "sd_r", [128, 512], F32)
        yo = sb(C, st, "sd_yo", [128, 2, 512], BF16)
        dcol = sb(C, st, "sd_dcol", [128, 8], F32)
        sg = sb(C, st, "sd_sg", [128, 8], F32)
        S.dma("sp", dcol, dcol_d, ["dcol_d"], ["sd_dcol"])
        S.dma("sp", sg, sg_d, ["sg_d"], ["sd_sg"])
        ctv = cstok_d.rearrange("(t p) f -> p t f", p=128)
        xdv = [xd_d[d].rearrange("(t p) f -> p t f", p=128) for d in range(2)]
        xv = xbcT_d[0:1024, :].rearrange("(k p) t -> p k t", p=128)
        zv = zT_d.rearrange("(k p) t -> p k t", p=128)
        yv = ymixT_d[0:1024, :].rearrange("(k p) t -> p k t", p=128)
        ng = 0
        nh = 0
        for s in range(NT // SEG):
            sc0 = s * SEG
            S.dma("sp", csT, csT_d[:, sc0:sc0 + SEG], ["csT_d"], ["sd_csT"])
            S.dma("sp", cstok, ctv[:, s * 18:(s + 1) * 18, :], ["cstok_d"], ["sd_cstok"])
            S.op("dve", lambda e: e.tensor_scalar(out=ncstok, in0=cstok, scalar1=-1.0, scalar2=None, op0=ALU.mult), ["sd_cstok"], ["sd_ncstok"])
            for g in range(4):
                S.dma("sp", Bg, xbcT_d[1024 + g * 128:1024 + (g + 1) * 128, sc0:sc0 + SEG], ["xbcT_d"], ["sd_B"])
                S.dma("sp", Cg, xbcT_d[1536 + g * 128:1536 + (g + 1) * 128, sc0:sc0 + SEG], ["xbcT_d"], ["sd_C"])
                for d in range(2):
                    S.dma("sp", xdg[d], xdv[d][:, s * 18:(s + 1) * 18, g * 256:(g + 1) * 256], ["xd_d"], [f"sd_xd{d}"])
                S.dma("sp", xg, xv[:, 2 * g:2 * g + 2, sc0:sc0 + SEG], ["xbcT_d"], ["sd_x"])
                for lt in range(5):
                    l0, W = (0, 256) if lt == 0 else (256 + 512 * (lt - 1), 512)
                    S.dma("sp", zt[:, :, 0:W], zv[:, 2 * g:2 * g + 2, sc0 + l0:sc0 + l0 + W], ["zT_d"], ["sd_z"])
                    for d in range(2):
                        for hh in range(4):
                            row = d * 32 + 4 * g + hh
                            i8 = d * 4 + hh
                            S.op("pe", lambda e, row=row, l0=l0, W=W: e.matmul(ps[7][:, 0:W], lhsT=C.esel[:, row, :], rhs=csT[:, l0:l0 + W], start=True, stop=True),
                                 ["cst", "sd_csT"], ["ps7"])
                            S.op("act", lambda e, i8=i8, W=W: e.activation(out=csb[i8][:, 0:W], in_=ps[7][:, 0:W], func=AF.Copy), ["ps7"], [f"sd_csb{i8}"])
                    started = [False] * 4
                    pairs = [(d, stl, mi) for d in range(2) for (stl, mi) in ssd_allowed(d, lt)]
                    last_b = [p_ for p_ in pairs if p_[0] == 1][-1][1]
                    items = [(pi, d, stl, mi, hh) for pi, (d, stl, mi) in enumerate(pairs) for hh in range(4)]
                    LAG = 2
                    pend = {}
                    cur = {}
                    for ix in range(len(items) + LAG):
                        if ix < len(items):
                            pi, d, stl, mi, hh = items[ix]
                            if hh == 0:
                                gb = 5 + ng % 2
                                G_ = Gs[ng % 3]
                                gk = f"sd_G{ng % 3}"
                                ng += 1
                                S.op("pe", lambda e, stl=stl, l0=l0, W=W, gb=gb: e.matmul(ps[gb][:, 0:W], lhsT=Bg[:, stl * 128:(stl + 1) * 128], rhs=Cg[:, l0:l0 + W], start=True, stop=True),
                                     ["sd_B", "sd_C"], [f"ps{gb}"])
                                S.op("act", lambda e, G_=G_, gb=gb, W=W: e.activation(out=G_[:, 0:W], in_=ps[gb][:, 0:W], func=AF.Copy), [f"ps{gb}"], [gk])
                                cur = dict(G_=G_, gk=gk)
                            row = d * 32 + 4 * g + hh
                            i8 = d * 4 + hh
                            dd_, ee_, M_ = dd[nh % 3], ee[nh % 3], M[nh % 3]
                            dk, ek, mk = f"sd_dd{nh % 3}", f"sd_ee{nh % 3}", f"sd_M{nh % 3}"
                            nh += 1
                            col = cstok[:, stl, row:row + 1]
                            if mi is None:
                                ncol = ncstok[:, stl, row:row + 1]
                                S.op("act", lambda e, ee_=ee_, i8=i8, ncol=ncol, W=W: e.activation(out=ee_[:, 0:W], in_=csb[i8][:, 0:W], func=AF.Exp, bias=ncol, scale=1.0),
                                     [f"sd_csb{i8}", "sd_ncstok"], [ek])
                            else:
                                S.op("dve", lambda e, dd_=dd_, i8=i8, col=col, W=W, d=d, mi=mi: e.scalar_tensor_tensor(
                                    out=dd_[:, 0:W], in0=csb[i8][:, 0:W], scalar=col, in1=C.negm[:, d * 4 + mi, 0:W], op0=ALU.subtract, op1=ALU.add),
                                    [f"sd_csb{i8}", "sd_cstok", "cst"], [dk])
                                S.op("act", lambda e, dd_=dd_, ee_=ee_, W=W: e.activation(out=ee_[:, 0:W], in_=dd_[:, 0:W], func=AF.Exp), [dk], [ek])
                            pend[ix] = (d, stl, hh, ee_, ek, M_, mk, cur["G_"], cur["gk"])
                        jx = ix - LAG
                        if jx < 0:
                            continue
                        d, stl, hh, ee_, ek, M_, mk, G_, gk = pend.pop(jx)
                        S.op("dve", lambda e, ee_=ee_, M_=M_, G_=G_, W=W: e.tensor_tensor(out=M_[:, 0:W], in0=ee_[:, 0:W], in1=G_[:, 0:W], op=ALU.mult), [ek, gk], [mk])
                        is_last = (d == 1 and stl == last_b)
                        ab = hh // 2
                        lo = (hh % 2) * 64
                        S.op("pe", lambda e, d=d, stl=stl, hh=hh, M_=M_, W=W, ab=ab, lo=lo, st_=(not started[hh]), is_last=is_last: e.matmul(
                            ps[ab][lo:lo + 64, 0:W], lhsT=xdg[d][:, stl, hh * 64:(hh + 1) * 64], rhs=M_[:, 0:W], start=st_, stop=is_last),
                            [f"sd_xd{d}", mk], [f"ps{ab}"])
                        started[hh] = True
                    for pr in range(2):
                        ch = 2 * g + pr
                        S.op("dve", lambda e, pr=pr, ch=ch, l0=l0, W=W: e.scalar_tensor_tensor(out=y[:, pr, 0:W], in0=xg[:, pr, l0:l0 + W], scalar=dcol[:, ch:ch + 1],
                                                                                            in1=ps[pr][:, 0:W], op0=ALU.mult, op1=ALU.add),
                             ["sd_x", "sd_dcol", f"ps{pr}"], ["sd_y"])
                        S.op("dve", lambda e, pr=pr, W=W: e.tensor_tensor(out=y[:, pr, 0:W], in0=y[:, pr, 0:W], in1=zt[:, pr, 0:W], op=ALU.mult), ["sd_y", "sd_z"], ["sd_y"])
                        S.op("act", lambda e, pr=pr, W=W: e.activation(out=sq[:, 0:W], in_=y[:, pr, 0:W], func=AF.Square), ["sd_y"], ["sd_sq"])
                        S.op("pe", lambda e, pr=pr, W=W: e.matmul(ps[4][:, 0:W], lhsT=C.ones256, rhs=sq[:, 0:W], start=(pr == 0), stop=(pr == 1)), ["sd_sq", "cst"], ["ps4"])
                    S.op("act", lambda e, W=W: e.activation(out=r[:, 0:W], in_=ps[4][:, 0:W], func=AF.Sqrt, bias=C.epsc, scale=1.0), ["ps4", "cst"], ["sd_r"])
                    S.op("dve", lambda e, W=W: e.reciprocal(out=r[:, 0:W], in_=r[:, 0:W]), ["sd_r"], ["sd_r"])
                    for pr in range(2):
                        ch = 2 * g + pr
                        S.op("dve", lambda e, pr=pr, ch=ch, W=W: e.scalar_tensor_tensor(out=yo[:, pr, 0:W], in0=y[:, pr, 0:W], scalar=sg[:, ch:ch + 1], in1=r[:, 0:W],
                                                                                     op0=ALU.mult, op1=ALU.mult), ["sd_y", "sd_sg", "sd_r"], ["sd_yo"])
                    S.dma("sp", yv[:, 2 * g:2 * g + 2, sc0 + l0:sc0 + l0 + W], yo[:, :, 0:W], ["sd_yo"], ["ymixT_d"])
    S.barrier()


def lru_phase(C, lxT_d, gT_d, wbd_d, bcol_d, lam_d, ymixT_d, NT):
    S, ps = C.S, C.ps
    S.barrier()
    with contextlib.ExitStack() as st:
        lx = sb(C, st, "lr_x", [128, SEG], F32)
        gt = sb(C, st, "lr_g", [128, SEG], F32)
        W = sb(C, st, "lr_W", [128, 4, 128], F32)
        bc = sb(C, st, "lr_bc", [128, 8, 4], F32)
        nsp = sb(C, st, "lr_nsp", [128, 8, 2, 2], F32)
        lam = sb(C, st, "lr_lam", [128, 8, 2], F32)
        rr = sb(C, st, "lr_r", [128, SEG], F32)
        ii = sb(C, st, "lr_i", [128, SEG], F32)
        aa = sb(C, st, "lr_a", [128, SEG], F32)
        bb = sb(C, st, "lr_b", [128, SEG], F32)
        hh = [sb(C, st, f"lr_h{d}", [128, SEG], F32) for d in range(2)]
        yo = sb(C, st, "lr_yo", [128, SEG], BF16)
        S.dma("sp", bc, bcol_d, ["bcol_d"], ["lr_bc"])
        S.dma("sp", lam, lam_d, ["lam_d"], ["lr_lam"])
        S.op("act", lambda e: e.activation(out=lam, in_=lam, func=AF.Exp, scale=-1.0), ["lr_lam"], ["lr_lam"])
        S.op("act", lambda e: e.activation(out=lam, in_=lam, func=AF.Ln, bias=C.onec, scale=1.0), ["lr_lam", "cst"], ["lr_lam"])
        S.op("dve", lambda e: e.tensor_scalar(out=nsp[:, :, :, 0], in0=lam, scalar1=-8.0, scalar2=None, op0=ALU.mult), ["lr_lam"], ["lr_nsp"])
        S.op("dve", lambda e: e.tensor_scalar(out=nsp[:, :, :, 1], in0=lam, scalar1=-16.0, scalar2=None, op0=ALU.mult), ["lr_lam"], ["lr_nsp"])
        tiles = [(0, 512), (512, 512), (1024, 512), (1536, 512), (2048, 256)]
        n = 0
        for c in range(8):
            S.dma("sp", W, wbd_d[c], ["wbd_d"], ["lr_W"])
            for s in range(NT // SEG):
                sc0 = s * SEG
                S.dma("sp", lx, lxT_d[c * 128:(c + 1) * 128, sc0:sc0 + SEG], ["lxT_d"], ["lr_x"])
                S.dma("sp", gt, gT_d[c * 128:(c + 1) * 128, sc0:sc0 + SEG], ["gT_d"], ["lr_g"])
                for d in range(2):
                    for gi, dst, dk in ((0, rr, "lr_r"), (1, ii, "lr_i")):
                        for (t0, Wd) in tiles:
                            pb = n % 2
                            n += 1
                            S.op("pe", lambda e, d=d, gi=gi, t0=t0, Wd=Wd, pb=pb: e.matmul(ps[pb][:, 0:Wd], lhsT=W[:, d * 2 + gi, :], rhs=lx[:, t0:t0 + Wd], start=True, stop=True),
                                 ["lr_W", "lr_x"], [f"ps{pb}"])
                            S.op("act", lambda e, dst=dst, d=d, gi=gi, t0=t0, Wd=Wd, pb=pb, c=c: e.activation(
                                out=dst[:, t0:t0 + Wd], in_=ps[pb][:, 0:Wd], func=AF.Sigmoid, bias=bc[:, c, d * 2 + gi:d * 2 + gi + 1], scale=1.0),
                                [f"ps{pb}", "lr_bc"], [dk])
                    S.op("act", lambda e, c=c, d=d: e.activation(out=aa, in_=rr, func=AF.Exp, scale=nsp[:, c, d, 0:1]), ["lr_r", "lr_nsp"], ["lr_a"])
                    S.op("act", lambda e, c=c, d=d: e.activation(out=bb, in_=rr, func=AF.Exp, scale=nsp[:, c, d, 1:2]), ["lr_r", "lr_nsp"], ["lr_b"])
                    S.op("dve", lambda e: e.tensor_scalar(out=bb, in0=bb, scalar1=-1.0, scalar2=1.0, op0=ALU.mult, op1=ALU.add), ["lr_b"], ["lr_b"])
                    S.op("act", lambda e: e.activation(out=bb, in_=bb, func=AF.Sqrt), ["lr_b"], ["lr_b"])
                    S.op("dve", lambda e: e.tensor_tensor(out=bb, in0=bb, in1=ii, op=ALU.mult), ["lr_b", "lr_i"], ["lr_b"])
                    S.op("dve", lambda e: e.tensor_tensor(out=bb, in0=bb, in1=lx, op=ALU.mult), ["lr_b", "lr_x"], ["lr_b"])
                    h_ = hh[d]
                    if d == 0:
                        S.op("dve", lambda e, h_=h_: e.tensor_tensor_scan(out=h_, data0=aa, data1=bb, initial=0.0, op0=ALU.mult, op1=ALU.add), ["lr_a", "lr_b"], ["lr_h0"])
                    else:
                        def rev(ap, a0, L):
                            return vw(ap, a0 + L - 1, [[-1, L]])
                        S.op("dve", lambda e, h_=h_: e.tensor_tensor_scan(out=rev(h_, 0, 256), data0=rev(aa, 0, 256), data1=rev(bb, 0, 256), initial=0.0,
                                                                         op0=ALU.mult, op1=ALU.add), ["lr_a", "lr_b"], ["lr_h1"])
                        S.op("dve", lambda e, h_=h_: e.tensor_tensor_scan(out=rev(h_, 256, 2048), data0=rev(aa, 256, 2048), data1=rev(bb, 256, 2048), initial=h_[:, 0:1],
                                                                         op0=ALU.mult, op1=ALU.add), ["lr_a", "lr_b", "lr_h1"], ["lr_h1"])
                S.op("dve", lambda e: e.tensor_tensor(out=hh[0], in0=hh[0], in1=hh[1], op=ALU.add), ["lr_h0", "lr_h1"], ["lr_h0"])
                S.op("dve", lambda e: e.tensor_tensor(out=yo, in0=hh[0], in1=gt, op=ALU.mult), ["lr_h0", "lr_g"], ["lr_yo"])
                S.dma("sp", ymixT_d[1024 + c * 128:1024 + (c + 1) * 128, sc0:sc0 + SEG], yo, ["lr_yo"], ["ymixT_d"])
    S.barrier()


NCST = 128 * 6 + 64 + 2
LAYERS = 4


def host_consts():
    c = np.zeros((128, NCST), np.float32)
    c[:, 0:128] = np.eye(128)
    c[:, 128:256] = np.arange(128)[None, :]
    c[:, 256:384] = 1.0 / 1024
    bd = np.zeros((128, 128), np.float32)
    bd[:64, :64] = 1.0 / 64
    bd[64:, 64:] = 1.0 / 64
    c[:, 384:512] = bd
    c[:, 512:640] = 1.0 / 256
    c[:, 640:768] = 1.0
    p = np.arange(128)
    kc = p % 64
    qc = np.arange(64)
    cs_ = np.clip(qc - 8, 0, 48)
    c[:, 768:832] = ((kc[:, None] >= cs_[None, :]) & (kc[:, None] < cs_[None, :] + 16)).astype(np.float32)
    c[:, 832] = EPS
    c[:, 833] = 1.0
    esel = np.zeros((64, 48, 128), np.float32)
    for r_ in range(48):
        esel[r_, r_, :] = 1.0
    s_ = np.arange(128)[:, None]
    l_ = np.arange(512)[None, :]
    negm = np.zeros((128, 8, 512), np.float32)
    for r_ in range(4):
        negm[:, r_, :] = np.where(l_ >= 128 * r_ + s_, 0.0, -1.0e6)
        negm[:, 4 + r_, :] = np.where(128 * r_ + s_ >= l_, 0.0, -1.0e6)
    return c, esel, negm


def build_program(NT=2 * SEG, layers=range(LAYERS), debug=False, stop_after=None, with_peer=True):
    nc = bass.Bass("TRN2", target_bir_lowering=False)
    C = Ctx()
    C.nc = nc
    C.S = S = Sched(nc)
    NS = NT // SEG

    def ein(name, shape, dt=F32):
        return nc.dram_tensor(name, list(shape), dt, kind="ExternalInput").ap()

    def scratch(name, shape, dt):
        return nc.dram_tensor(name, list(shape), dt, kind="ExternalOutput" if debug else "Internal").ap()

    x_in = ein("x_in", [1024, NT])
    cT_d = ein("cT", [128, 8, 3])
    cst_d = ein("cst", [128, NCST])
    esel_d = ein("esel", [64, 48, 128])
    negm_d = ein("negm", [128, 8, 512])
    adaw = ein("adaw", [LAYERS, 1024, 6144])
    adabT = ein("adabT", [LAYERS, 128, 48])
    ngT = ein("ngT", [LAYERS, 128, 2, 8])
    win = ein("win", [2, 1024, 5184])
    cw = ein("cw", [2, 128, 24, 4])
    cb = ein("cb", [2, 128, 24])
    dtb = ein("dtb", [2, 64, 1])
    alog = ein("alog", [2, 64, 1])
    dcol = ein("dcol", [2, 128, 8])
    sgc = ein("sgc", [2, 128, 8])
    wbd = ein("wbd", [2, 8, 128, 4, 128])
    bcol = ein("bcol", [2, 128, 8, 4])
    lamc = ein("lamc", [2, 128, 8, 2])
    wout = ein("wout", [2, 2048, 1024])
    wqkv = ein("wqkv", [2, 1024, 3072])
    qg2 = ein("qg2", [2, 128, 1])
    kg2 = ein("kg2", [2, 128, 1])
    rpbG = ein("rpbG", [2, 128, 16, 15, 64])
    wo = ein("wo", [2, 1024, 1024])
    if with_peer:
        pwq = ein("pwq", [LAYERS, 1024, 2048])
        pkeys = ein("pkeys", [LAYERS, 128, 16, 128])
        puT = ein("puT", [LAYERS, 128, 128, 8, 128])
        pv = ein("pv", [LAYERS, 16384, 1024])
    xT = nc.dram_tensor("xT", [1024, NT], F32, kind="ExternalOutput").ap()
    hT_d = scratch("hT_d", [1024, NT], BF16)
    rout_d = scratch("rout_d", [3, 128, NT], F32)
    xbcT_d = scratch("xbcT_d", [2048, NT], BF16)
    lxT_d = scratch("lxT_d", [1024, NT], F32)
    zT_d = scratch("zT_d", [1024, NT], F32)
    gT_d = scratch("gT_d", [1024, NT], F32)
    csT_d = scratch("csT_d", [64, NT], F32)
    cstok_d = scratch("cstok_d", [NT, 64], F32)
    xd_d = scratch("xd_d", [2, NT, 1024], BF16)
    ymixT_d = scratch("ymixT_d", [2048, NT], BF16)
    qkT_d = scratch("qkT_d", [16, 128, NT], BF16)
    v0_d = scratch("v0_d", [NT, 1024], BF16)
    v1_d = scratch("v1_d", [NS * 15 * 128, 1024], BF16)
    oT_d = scratch("oT_d", [8, 128, NT], BF16)

    C.ps = [nc.alloc_psum_tensor(f"psb{i}", [128, 512], F32).ap() for i in range(8)]
    cst = nc.alloc_sbuf_tensor("cst_sb", [128, NCST], F32).ap()
    C.ident, C.iota, C.onesm = cst[:, 0:128], cst[:, 128:256], cst[:, 256:384]
    C.bd64, C.ones256, C.mask01 = cst[:, 384:512], cst[:, 512:640], cst[:, 768:832]
    C.epsc, C.onec = cst[:, 832:833], cst[:, 833:834]
    C.identb = nc.alloc_sbuf_tensor("identb", [128, 128], BF16).ap()
    C.onesb = nc.alloc_sbuf_tensor("onesb", [128, 128], BF16).ap()
    C.scT = nc.alloc_sbuf_tensor("scT", [128, 8, 3], F32).ap()
    C.mods = [nc.alloc_sbuf_tensor(f"mods{l}", [128, 48, 3], F32).ap() for l in range(LAYERS)]
    C.modA = [nc.alloc_sbuf_tensor(f"modA{l}", [128, 2, 8, 3], F32).ap() for l in range(LAYERS)]
    S.dma("sp", cst, cst_d, ["cst_d"], ["cst"])
    S.op("act", lambda e: e.activation(out=C.identb, in_=C.ident, func=AF.Copy), ["cst"], ["cst"])
    S.op("act", lambda e: e.activation(out=C.onesb, in_=cst[:, 640:768], func=AF.Copy), ["cst"], ["cst"])
    S.dma("sp", C.scT, cT_d, ["cT_d"], ["scT"])
    S.op("act", lambda e: e.activation(out=C.scT, in_=C.scT, func=AF.Silu), ["scT"], ["scT"])
    for k in range(8):
        S.dma("sp", xT[k * 128:(k + 1) * 128, :], x_in[k * 128:(k + 1) * 128, :], ["x_in"], ["xT_d"])

    def done(tag):
        return stop_after is not None and tag == stop_after

    for l in layers:
        j = l // 2
        adaln(C, l, adaw[l], adabT[l], ngT[l])
        norm_mod(C, l, 0, xT, hT_d, NT)
        if done(f"nm{l}"):
            break
        if l % 2 == 0:
            even_inproj(C, hT_d, win[j], cw[j], cb[j], dtb[j], alog[j], xbcT_d, lxT_d, zT_d, gT_d, csT_d, cstok_d, xd_d, NT)
            if done(f"ei{l}"):
                break
            with contextlib.ExitStack() as st:
                C.esel = sb(C, st, "esel_sb", [64, 48, 128], F32)
                C.negm = sb(C, st, "negm_sb", [128, 8, 512], F32)
                S.dma("sp", C.esel, esel_d, ["esel_d"], ["cst"])
                S.dma("sp", C.negm, negm_d, ["negm_d"], ["cst"])
                ssd_phase(C, xbcT_d, csT_d, cstok_d, xd_d, zT_d, dcol[j], sgc[j], ymixT_d, NT)
            if done(f"ssd{l}"):
                break
            lru_phase(C, lxT_d, gT_d, wbd[j], bcol[j], lamc[j], ymixT_d, NT)
            if done(f"lru{l}"):
                break
            resid_linear(C, l, ymixT_d, 16, wout[j], 16, xT, NT)
        else:
            na_proj(C, hT_d, wqkv[j], qg2[j], kg2[j], qkT_d, v0_d, v1_d, NT)
            if done(f"np{l}"):
                break
            na_attn(C, qkT_d, v0_d, v1_d, rpbG[j], oT_d, NT)
            if done(f"na{l}"):
                break
            resid_linear(C, l, oT_d.rearrange("k p t -> (k p) t"), 8, wo[j], 16, xT, NT)
        if done(f"mix{l}"):
            break
        norm_mod(C, l, 1, xT, hT_d, NT)
        if not with_peer:
            continue
        peer_route(C, hT_d, pwq[l], pkeys[l], rout_d, NT)
        if done(f"pr{l}"):
            break
        peer_expert(C, hT_d, rout_d, puT[l], pv[l], xT, (lambda ch, m, l=l: C.mods[l][:, 40 + ch, m:m + 1]), NT, seg_m)
    n = S.finalize()
    return nc, n


def host_prep(inp, core, ncores=8):
    f = np.float32
    bs = slice(2 * core, 2 * core + 2)
    x, ctx, c, c_ctx = inp["x"][bs], inp["ctx"][bs], inp["c"][bs], inp["c_ctx"]
    seq = np.concatenate([ctx, x], axis=1)
    x_in = np.ascontiguousarray(seq.reshape(2 * SEG, 1024).T)
    cm = np.stack([c[0], c[1], c_ctx], axis=1)
    cT = np.ascontiguousarray(cm.reshape(8, 128, 3).transpose(1, 0, 2))
    return {"x_in": x_in.astype(f), "cT": cT.astype(f)}


def colT(a, nch):
    a = np.asarray(a, np.float32)
    lead = a.shape[:-1]
    return np.ascontiguousarray(np.moveaxis(a.reshape(*lead, nch, 128), -1, -2))


def host_shared(inp):
    f = np.float32
    g = lambda k: np.asarray(inp[k], f)
    cst, esel, negm = host_consts()
    d = {"cst": cst, "esel": esel, "negm": negm}
    d["adaw"] = g("ada_w")
    d["adabT"] = colT(g("ada_b"), 48)
    d["ngT"] = np.ascontiguousarray(np.stack([colT(g("norm1_g"), 8), colT(g("norm2_g"), 8)], axis=2))
    w = g("ev_w_in")
    z16 = np.zeros((2, 1024, 16), f)
    d["win"] = np.ascontiguousarray(np.concatenate(
        [w[:, :, 0:2048], w[:, :, 2080:3104], w[:, :, 3104:4128], w[:, :, 4128:5152], w[:, :, 2048:2064], z16, w[:, :, 2064:2080], z16], axis=2))
    cwx = np.concatenate([g("ev_conv_w"), g("ev_lru_conv_w")], axis=2)
    d["cw"] = np.ascontiguousarray(cwx.reshape(2, 4, 24, 128).transpose(0, 3, 2, 1))
    d["cb"] = colT(np.concatenate([g("ev_conv_b"), g("ev_lru_conv_b")], axis=1), 24)
    z16b = np.zeros((2, 16), f)
    dtb = g("ev_dt_bias")
    al = g("ev_a_log")
    d["dtb"] = np.ascontiguousarray(np.concatenate([dtb[:, 0], z16b, dtb[:, 1], z16b], axis=1)[:, :, None])
    d["alog"] = np.ascontiguousarray(np.concatenate([al[:, 0], z16b, al[:, 1], z16b], axis=1)[:, :, None])
    d["dcol"] = colT(np.repeat(g("ev_d"), 64, axis=1), 8)
    d["sgc"] = colT(g("ev_ssd_norm_g"), 8)
    wa, wx = g("ev_lru_wa"), g("ev_lru_wx")
    wbd = np.zeros((2, 8, 128, 4, 128), f)
    for dd_ in range(2):
        for gi, ww in enumerate((wa, wx)):
            for n_ in range(16):
                c_, o_ = n_ // 2, (n_ % 2) * 64
                wbd[:, c_, o_:o_ + 64, dd_ * 2 + gi, o_:o_ + 64] = ww[:, dd_, n_]
    d["wbd"] = wbd
    ba, bx = colT(g("ev_lru_ba"), 8), colT(g("ev_lru_bx"), 8)
    d["bcol"] = np.ascontiguousarray(np.stack([ba[:, 0], bx[:, 0], ba[:, 1], bx[:, 1]], axis=-1))
    d["lamc"] = np.ascontiguousarray(np.moveaxis(colT(g("ev_lru_lam"), 8), 1, -1))
    d["wout"] = g("ev_w_out")
    d["wqkv"] = g("od_w_qkv")
    d["qg2"] = np.ascontiguousarray(np.tile(g("od_q_norm_g"), (1, 2))[:, :, None])
    d["kg2"] = np.ascontiguousarray(np.tile(g("od_k_norm_g"), (1, 2))[:, :, None])
    rpb = g("od_rpb")
    p = np.arange(128)
    kc, up = p % 64, p // 64
    qc = np.arange(64)
    dc = np.clip(kc[:, None] - qc[None, :] + 15, 0, 30)
    dr = np.clip(np.arange(15)[None, :] + up[:, None], 0, 14)
    d["rpbG"] = np.ascontiguousarray(rpb[:, :, dr[:, :, None], dc[:, None, :]].transpose(0, 2, 1, 3, 4))
    d["wo"] = g("od_w_o")
    d["pwq"] = g("pe_w_q")
    d["pkeys"] = np.ascontiguousarray(g("pe_keys").reshape(4, 16, 128, 128).transpose(0, 3, 1, 2))
    u = g("pe_u")
    d["puT"] = np.ascontiguousarray(u.reshape(4, 128, 128, 8, 128).transpose(0, 2, 4, 3, 1))
    d["pv"] = g("pe_v")
    return d


_CACHE = {}


def kernel(**inputs):
    if "nc" not in _CACHE:
        _CACHE["nc"] = build_program()[0]
    nc = _CACHE["nc"]
    shared = host_shared(inputs)
    in_maps = []
    for core in range(8):
        m = dict(shared)
        m.update(host_prep(inputs, core))
        in_maps.append(m)
    res = run_bass_kernel_spmd(nc, in_maps, core_ids=list(range(8)))
    out = np.empty((16, 2048, 1024), np.float32)
    for core in range(8):
        xT = np.asarray(res.results[core]["xT"], np.float32)
        seq = xT.T.reshape(2, SEG, 1024)
        out[2 * core:2 * core + 2] = seq[:, 256:, :]
    return out
```

```python
import contextlib
import numpy as np
import ml_dtypes
import concourse.bass as bass
import concourse.mybir as mybir
from concourse.ap import AP
from concourse.bass_utils import run_bass_kernel_spmd

F32 = mybir.dt.float32
BF16 = mybir.dt.bfloat16
U32 = mybir.dt.uint32
AF = mybir.ActivationFunctionType
ALU = mybir.AluOpType
AX = mybir.AxisListType

ENGS = ("pe", "dve", "act", "pool", "sp")
NEG = -1.0e30


class Sched:
    NSLOT = 24
    ROT = 20000

    def __init__(self, nc):
        self.nc = nc
        self.ops = []
        self.eng = {"pe": nc.tensor, "dve": nc.vector, "act": nc.scalar,
                    "pool": nc.gpsimd, "sp": nc.sync}

    def op(self, eng, emit, reads=(), writes=(), dma=False):
        self.ops.append(dict(eng=eng, emit=emit, reads=tuple(reads),
                             writes=tuple(writes), dma=dma, bar=False))

    def dma(self, q, out, in_, reads, writes, **kw):
        self.op(q, lambda e: e.dma_start(out=out, in_=in_, **kw), reads, writes, dma=True)

    def barrier(self):
        self.ops.append(dict(bar=True))

    def finalize(self):
        nc = self.nc
        raw = self.ops
        ops = []
        last_w, readers = {}, {}
        slot_last = [None] * self.NSLOT
        nslot = 0
        last_on = {}
        pending_bar = {}
        for o in raw:
            if o["bar"]:
                extra = set(last_on.values()) | {s for s in slot_last if s is not None}
                for e in ENGS:
                    pending_bar[e] = set(extra) | pending_bar.get(e, set())
                continue
            i = len(ops)
            ops.append(o)
            d = set()
            for r in o["reads"]:
                if r in last_w:
                    d.add(last_w[r])
            for w in o["writes"]:
                if w in last_w:
                    d.add(last_w[w])
                d.update(readers.get(w, ()))
            if o["dma"]:
                s = nslot % self.NSLOT
                nslot += 1
                o["slot"] = s
                if slot_last[s] is not None:
                    d.add(slot_last[s])
                slot_last[s] = i
            if o["eng"] in pending_bar:
                d.update(pending_bar.pop(o["eng"]))
            d.discard(i)
            if o["eng"] == "pe" and not o["dma"]:
                d = {j for j in d if not (ops[j]["eng"] == "pe" and not ops[j]["dma"])}
            o["deps"] = d
            for w in o["writes"]:
                last_w[w] = i
                readers[w] = []
            for r in o["reads"]:
                if r not in o["writes"]:
                    readers.setdefault(r, []).append(i)
            last_on[o["eng"]] = i
        n = len(ops)
        signal = [False] * n
        for o in ops:
            for j in o["deps"]:
                signal[j] = True
        cnt = {e: 0 for e in ENGS}
        slot_cnt = [0] * self.NSLOT
        for i, o in enumerate(ops):
            if o["dma"]:
                slot_cnt[o["slot"]] += 1
                o["sig"] = ("slot", o["slot"], 16 * slot_cnt[o["slot"]])
            elif signal[i]:
                e = o["eng"]
                o["sig"] = (e, cnt[e] // self.ROT, cnt[e] % self.ROT + 1)
                cnt[e] += 1
            else:
                o["sig"] = None
        sems = {}
        for e in ENGS:
            for k in range(max((cnt[e] + self.ROT - 1) // self.ROT, 1)):
                sems[(e, k)] = nc.alloc_semaphore(f"s_{e}_{k}")
        for s in range(self.NSLOT):
            sems[("slot", s)] = nc.alloc_semaphore(f"s_dma_{s}")
        self.sems = sems
        waited = {e: {} for e in ENGS}
        for o in ops:
            e = o["eng"]
            eobj = self.eng[e]
            need = {}
            for j in o["deps"]:
                sg = ops[j]["sig"]
                key = (sg[0], sg[1])
                need[key] = max(need.get(key, 0), sg[2])
            for key, v in need.items():
                if waited[e].get(key, 0) >= v:
                    continue
                eobj.wait_ge(sems[key], v)
                waited[e][key] = v
            ins = o["emit"](eobj)
            sg = o["sig"]
            if sg is not None:
                if sg[0] == "slot":
                    ins.then_inc(sems[("slot", sg[1])], 16)
                else:
                    ins.then_inc(sems[(sg[0], sg[1])], 1)
        fin = {}
        for o in ops:
            if o["dma"]:
                fin[o["slot"]] = o["sig"][2]
        for s, v in fin.items():
            self.eng["sp"].wait_ge(sems[("slot", s)], v)
        self.ops = ops
        return n


def vw(base, off, dims):
    return AP(base.tensor, base.offset + off, [list(base.ap[0])] + [list(d) for d in dims])


class Ctx:
    pass


_UNIQ = [0]


def sb(C, st, name, shape, dt):
    _UNIQ[0] += 1
    return st.enter_context(C.nc.sbuf_tensor(f"{name}_{_UNIQ[0]}", shape, dt)).ap()


def peer_route(C, hT_d, wq_d, keysT_d, rout_d, NT):
    S, nc, ps = C.S, C.nc, C.ps
    S.barrier()
    with contextlib.ExitStack() as st:
        wq = sb(C, st, "pr_wq", [128, 8, 2048], BF16)
        wst = sb(C, st, "pr_wst", [128, 8, 512], F32)
        keys = sb(C, st, "pr_keys", [128, 16, 128], F32)
        hT = [sb(C, st, f"pr_hT{i}", [128, 8, 512], BF16) for i in range(2)]
        qTs = [sb(C, st, f"pr_qT{i}", [128, 16, 128], F32) for i in range(2)]
        scs = [sb(C, st, f"pr_sc{i}", [128, 2048], F32) for i in range(2)]
        tmp16 = sb(C, st, "pr_tmp16", [128, 16, 128], F32)
        tmp8 = sb(C, st, "pr_tmp8", [128, 8, 256], F32)
        stop = sb(C, st, "pr_stop", [128, 16, 16], F32)
        itop = sb(C, st, "pr_itop", [128, 16, 16], U32)
        itf = sb(C, st, "pr_itf", [128, 16, 16], F32)
        cand = sb(C, st, "pr_cand", [128, 8, 256], F32)
        best = sb(C, st, "pr_best", [128, 8, 16], F32)
        pos = sb(C, st, "pr_pos", [128, 8, 16], U32)
        posf = sb(C, st, "pr_posf", [128, 128], F32)
        av = sb(C, st, "pr_av", [128, 128], F32)
        bv = sb(C, st, "pr_bv", [128, 128], F32)
        eq = sb(C, st, "pr_eq", [128, 2048], F32)
        sel = sb(C, st, "pr_sel", [128, 3, 128], F32)
        zs = sb(C, st, "pr_zs", [128, 8], F32)
        selT = [sb(C, st, f"pr_selT{i}", [128, 3, 128], F32) for i in range(2)]
        thr = sb(C, st, "pr_thr", [128, 15], F32)
        wqv = wq_d.rearrange("(k p) f -> p k f", p=128)
        for c in range(4):
            S.dma("sp", wst, wqv[:, :, c * 512:(c + 1) * 512], ["wq_d"], ["pr_wst"])
            S.op("act", lambda e, c=c: e.activation(out=wq[:, :, c * 512:(c + 1) * 512], in_=wst, func=AF.Copy),
                 ["pr_wst"], ["pr_wq"])
        S.dma("sp", keys, keysT_d, ["keys_d"], ["pr_keys"])
        S.op("dve", lambda e: e.tensor_scalar(out=thr, in0=C.iota[:, 1:16], scalar1=16.0, scalar2=None, op0=ALU.mult), ["cst"], ["pr_thr"])
        hv = hT_d.rearrange("(k p) t -> p k t", p=128)
        rv = rout_d.rearrange("a s t -> s a t")
        STA = [f"pr_st{i}a" for i in range(16)]
        STB = [f"pr_st{i}b" for i in range(16)]
        ITA = [f"pr_it{i}a" for i in range(16)]
        ITB = [f"pr_it{i}b" for i in range(16)]
        BSA = [f"pr_bs{i}a" for i in range(8)]
        BSB = [f"pr_bs{i}b" for i in range(8)]
        PSA = [f"pr_ps{i}a" for i in range(8)]
        PSB = [f"pr_ps{i}b" for i in range(8)]
        nt = 0
        for blk in range(NT // 512):
            hT_ = hT[blk % 2]
            hk = f"pr_hT{blk % 2}"
            S.dma("sp", hT_, hv[:, :, blk * 512:(blk + 1) * 512], ["hT_d"], [hk])
            for tt in range(4):
                t0 = tt * 128
                qT, sc = qTs[nt % 2], scs[nt % 2]
                qk, sk = f"pr_qT{nt % 2}", f"pr_sc{nt % 2}"
                selT_ = selT[nt % 2]
                stk = f"pr_selT{nt % 2}"
                nt += 1
                for qc in range(16):
                    b = qc // 4
                    for k in range(8):
                        S.op("pe", lambda e, qc=qc, k=k, b=b, t0=t0, hT_=hT_: e.matmul(
                            ps[b][:, (qc % 4) * 128:(qc % 4 + 1) * 128], lhsT=wq[:, k, qc * 128:(qc + 1) * 128],
                            rhs=hT_[:, k, t0:t0 + 128], start=(k == 0), stop=(k == 7)),
                            ["pr_wq", hk], [f"ps{b}"])
                for b in range(4):
                    S.op("act", lambda e, b=b, qT=qT: e.activation(out=qT[:, 4 * b:4 * b + 4, :], in_=ps[b].rearrange("p (a t) -> p a t", a=4), func=AF.Copy),
                         [f"ps{b}"], [qk])
                for hz in range(16):
                    b = 4 + hz // 4
                    S.op("pe", lambda e, hz=hz, b=b, qT=qT: e.matmul(
                        ps[b][:, (hz % 4) * 128:(hz % 4 + 1) * 128], lhsT=qT[:, hz, :], rhs=keys[:, hz, :],
                        start=True, stop=True), [qk, "pr_keys"], [f"ps{b}"])
                for b in range(4):
                    S.op("act", lambda e, b=b, sc=sc: e.activation(out=sc[:, b * 512:(b + 1) * 512], in_=ps[4 + b], func=AF.Copy),
                         [f"ps{4 + b}"], [sk])
                G16 = range(16)
                for hz in G16:
                    S.op("dve", lambda e, hz=hz, sc=sc: e.max(out=stop[:, hz, 0:8], in_=sc[:, hz * 128:(hz + 1) * 128]), [sk], [STA[hz]])
                for hz in G16:
                    S.op("dve", lambda e, hz=hz, sc=sc: e.max_index(out=itop[:, hz, 0:8], in_max=stop[:, hz, 0:8], in_values=sc[:, hz * 128:(hz + 1) * 128]),
                         [sk, STA[hz]], [ITA[hz]])
                for hz in G16:
                    S.op("dve", lambda e, hz=hz, sc=sc: e.match_replace(out=tmp16[:, hz, :], in_to_replace=stop[:, hz, 0:8], in_values=sc[:, hz * 128:(hz + 1) * 128], imm_value=NEG),
                         [sk, STA[hz]], [f"pr_tm{hz}"])
                for hz in G16:
                    S.op("dve", lambda e, hz=hz: e.max(out=stop[:, hz, 8:16], in_=tmp16[:, hz, :]), [f"pr_tm{hz}"], [STB[hz]])
                for hz in G16:
                    S.op("dve", lambda e, hz=hz: e.max_index(out=itop[:, hz, 8:16], in_max=stop[:, hz, 8:16], in_values=tmp16[:, hz, :]),
                         [f"pr_tm{hz}", STB[hz]], [ITB[hz]])
                S.op("dve", lambda e: e.tensor_copy(out=itf, in_=itop), ITA + ITB, ["pr_itf"])
                in0 = vw(stop, 0, [[32, 8], [1, 16], [0, 16]])
                in1 = vw(stop, 16, [[32, 8], [0, 16], [1, 16]])
                S.op("dve", lambda e, in0=in0, in1=in1: e.tensor_tensor(out=cand.rearrange("p h (a b) -> p h a b", a=16), in0=in0, in1=in1, op=ALU.add),
                     STA + STB, ["pr_cand"])
                G8 = range(8)
                for h in G8:
                    S.op("dve", lambda e, h=h: e.max(out=best[:, h, 0:8], in_=cand[:, h, :]), ["pr_cand"], [BSA[h]])
                for h in G8:
                    S.op("dve", lambda e, h=h: e.max_index(out=pos[:, h, 0:8], in_max=best[:, h, 0:8], in_values=cand[:, h, :]), ["pr_cand", BSA[h]], [PSA[h]])
                for h in G8:
                    S.op("dve", lambda e, h=h: e.match_replace(out=tmp8[:, h, :], in_to_replace=best[:, h, 0:8], in_values=cand[:, h, :], imm_value=NEG),
                         ["pr_cand", BSA[h]], [f"pr_t8{h}"])
                for h in G8:
                    S.op("dve", lambda e, h=h: e.max(out=best[:, h, 8:16], in_=tmp8[:, h, :]), [f"pr_t8{h}"], [BSB[h]])
                for h in G8:
                    S.op("dve", lambda e, h=h: e.max_index(out=pos[:, h, 8:16], in_max=best[:, h, 8:16], in_values=tmp8[:, h, :]), [f"pr_t8{h}", BSB[h]], [PSB[h]])
                S.op("dve", lambda e: e.tensor_copy(out=posf, in_=pos.rearrange("p h k -> p (h k)")), PSA + PSB, ["pr_posf"])
                S.op("dve", lambda e: e.tensor_tensor(out=eq[:, 0:1920].rearrange("p (s m) -> p s m", m=15), in0=vw(posf, 0, [[1, 128], [0, 15]]),
                                                      in1=vw(thr, 0, [[0, 128], [1, 15]]), op=ALU.is_ge), ["pr_posf", "pr_thr"], ["pr_eq"])
                S.op("dve", lambda e: e.tensor_reduce(out=av, in_=eq[:, 0:1920].rearrange("p (s m) -> p s m", m=15), axis=AX.X, op=ALU.add), ["pr_eq"], ["pr_av"])
                S.op("dve", lambda e: e.scalar_tensor_tensor(out=bv, in0=av, scalar=-16.0, in1=posf, op0=ALU.mult, op1=ALU.add),
                     ["pr_posf", "pr_av"], ["pr_bv"])
                iota16 = vw(C.iota, 0, [[0, 8], [0, 16], [1, 16]])
                eq4 = eq.rearrange("p (h k a) -> p h k a", h=8, k=16)
                for z, src_ab in ((0, av), (1, bv)):
                    abb = vw(src_ab, 0, [[16, 8], [1, 16], [0, 16]])
                    itb = vw(itf, 16 * z, [[32, 8], [0, 16], [1, 16]])
                    S.op("dve", lambda e, abb=abb: e.tensor_tensor(out=eq4, in0=abb, in1=iota16, op=ALU.is_equal),
                         ["pr_av", "pr_bv"], ["pr_eq"])
                    S.op("dve", lambda e, itb=itb: e.tensor_tensor(out=eq4, in0=eq4, in1=itb, op=ALU.mult),
                         ["pr_eq", "pr_itf"], ["pr_eq"])
                    S.op("dve", lambda e, z=z: e.tensor_reduce(out=sel[:, z, :], in_=eq.rearrange("p (s a) -> p s a", a=16), axis=AX.X, op=ALU.add),
                         ["pr_eq"], ["pr_sel"])
                g3 = sel[:, 2, :].rearrange("p (h k) -> p h k", h=8)
                b0 = vw(best, 0, [[16, 8], [0, 16]])
                S.op("dve", lambda e: e.tensor_tensor(out=g3, in0=best, in1=b0, op=ALU.subtract), BSA + BSB, ["pr_sel"])
                S.op("act", lambda e: e.activation(out=sel[:, 2, :], in_=sel[:, 2, :], func=AF.Exp), ["pr_sel"], ["pr_sel"])
                S.op("dve", lambda e: e.tensor_reduce(out=zs, in_=g3, axis=AX.X, op=ALU.add), ["pr_sel"], ["pr_zs"])
                S.op("dve", lambda e: e.reciprocal(out=zs, in_=zs), ["pr_zs"], ["pr_zs"])
                zb = vw(zs, 0, [[1, 8], [0, 16]])
                S.op("dve", lambda e: e.tensor_tensor(out=g3, in0=g3, in1=zb, op=ALU.mult), ["pr_sel", "pr_zs"], ["pr_sel"])
                for a in range(3):
                    S.op("pe", lambda e, a=a: e.transpose(out=ps[0][:, a * 128:(a + 1) * 128], in_=sel[:, a, :], identity=C.ident),
                         ["pr_sel"], ["ps0"])
                S.op("act", lambda e, selT_=selT_: e.activation(out=selT_, in_=ps[0][:, 0:384].rearrange("p (a t) -> p a t", a=3), func=AF.Copy),
                     ["ps0"], [stk])
                c0 = blk * 512 + t0
                S.dma("sp", rv[:, :, c0:c0 + 128], selT_, [stk], ["rout_d"])
    S.barrier()


def peer_expert(C, hT_d, rout_d, uT_d, v_d, xT_d, g2cols, NT, seg_of_col):
    S, nc, ps = C.S, C.nc, C.ps
    TB = 512
    SBT = 16
    NST = 4
    S.barrier()
    with contextlib.ExitStack() as st:
        hT = sb(C, st, "px_hT", [128, 8, TB], BF16)
        rt = sb(C, st, "px_rt", [128, 3, TB], F32)
        A = [sb(C, st, f"px_A{i}", [128, 128, SBT], BF16) for i in range(2)]
        B = [sb(C, st, f"px_B{i}", [128, 128, SBT], BF16) for i in range(2)]
        iotaR = sb(C, st, "px_iotaR", [128, 128, SBT], BF16)
        idxb = sb(C, st, "px_idxb", [128, 2, TB], BF16)
        S.op("dve", lambda e: e.tensor_copy(out=iotaR, in_=vw(C.iota, 0, [[1, 128], [0, SBT]])), ["cst"], ["px_iotaR"])
        G = sb(C, st, "px_G", [128, 128, TB], BF16)
        wt = [sb(C, st, f"px_w{i}", [128, 1024], BF16) for i in range(3)]
        stg = [sb(C, st, f"px_s{i}", [128, 1024], F32) for i in range(NST)]
        gl = [sb(C, st, f"px_gl{i}", [128, TB], BF16) for i in range(2)]
        xt = [sb(C, st, f"px_x{i}", [128, TB], F32) for i in range(2)]
        hv = hT_d.rearrange("(k p) t -> p k t", p=128)
        rv = rout_d.rearrange("a s t -> s a t")
        vv = v_d.rearrange("(i j) d -> j i d", j=128)
        xv = xT_d.rearrange("(k p) t -> k p t", p=128)
        nld = 0
        for blk in range(NT // TB):
            c0 = blk * TB
            S.dma("sp", hT, hv[:, :, c0:c0 + TB], ["hT_d"], ["px_hT"])
            S.dma("sp", rt, rv[:, :, c0:c0 + TB], ["rout_d"], ["px_rt"])
            S.op("dve", lambda e: e.tensor_copy(out=idxb, in_=rt[:, 0:2, :]), ["px_rt"], ["px_idxb"])

            def onehots(sbi):
                tb0 = sbi * SBT
                Ab, Bb = A[sbi % 2], B[sbi % 2]
                i1b = vw(idxb, tb0, [[0, 128], [1, SBT]])
                i2b = vw(idxb, TB + tb0, [[0, 128], [1, SBT]])
                gb = vw(rt, 2 * TB + tb0, [[0, 128], [1, SBT]])
                S.op("dve", lambda e, Ab=Ab, i1b=i1b: e.tensor_tensor(out=Ab, in0=i1b, in1=iotaR, op=ALU.is_equal),
                     ["px_idxb", "px_iotaR"], [f"px_A{sbi % 2}"])
                S.op("dve", lambda e, Bb=Bb, i2b=i2b: e.tensor_tensor(out=Bb, in0=i2b, in1=iotaR, op=ALU.is_equal),
                     ["px_idxb", "px_iotaR"], [f"px_B{sbi % 2}"])
                S.op("pool", lambda e, Bb=Bb, gb=gb: e.tensor_tensor(out=Bb, in0=Bb, in1=gb, op=ALU.mult),
                     ["px_rt", f"px_B{sbi % 2}"], [f"px_B{sbi % 2}"])

            nsb = TB // SBT
            onehots(0)
            for sbi in range(nsb):
                if sbi + 1 < nsb:
                    onehots(sbi + 1)
                tb0 = sbi * SBT
                Ab, Bb = A[sbi % 2], B[sbi % 2]
                for q4 in range(SBT // 4):
                    pb = 4 + (sbi * (SBT // 4) + q4) % 2
                    for t in range(4):
                        tl = q4 * 4 + t
                        S.op("pe", lambda e, Ab=Ab, Bb=Bb, tl=tl, t=t, pb=pb: e.matmul(
                            ps[pb][:, t * 128:(t + 1) * 128], lhsT=Ab[:, :, tl], rhs=Bb[:, :, tl], start=True, stop=True),
                            [f"px_A{sbi % 2}", f"px_B{sbi % 2}"], [f"ps{pb}"])
                    tg = tb0 + q4 * 4
                    outv = vw(G, tg, [[TB, 128], [1, 4]])
                    inv = ps[pb].rearrange("p (t j) -> p j t", t=4)
                    S.op("act", lambda e, outv=outv, inv=inv: e.activation(out=outv, in_=inv, func=AF.Copy), [f"ps{pb}"], ["px_G"])
            ubuf = {}
            for jx in range(128 + 2):
                if jx < 128:
                    j = jx
                    s_, w_ = stg[nld % NST], wt[nld % 3]
                    sk, wk = f"px_s{nld % NST}", f"px_w{nld % 3}"
                    nld += 1
                    S.dma("sp", s_, uT_d[j].rearrange("p k i -> p (k i)"), ["uT_d"], [sk])
                    if j % 2 == 0:
                        S.op("act", lambda e, s_=s_, w_=w_: e.activation(out=w_, in_=s_, func=AF.Copy), [sk], [wk])
                    else:
                        S.op("dve", lambda e, s_=s_, w_=w_: e.tensor_copy(out=w_, in_=s_), [sk], [wk])
                    ubuf[j] = (w_, wk)
                j = jx - 2
                if j < 0:
                    continue
                w_, wk = ubuf.pop(j)
                pb = 6 + j % 2
                for k in range(8):
                    S.op("pe", lambda e, w_=w_, k=k, pb=pb: e.matmul(ps[pb], lhsT=w_[:, k * 128:(k + 1) * 128], rhs=hT[:, k, :], start=(k == 0), stop=(k == 7)),
                         [wk, "px_hT"], [f"ps{pb}"])
                g_ = gl[j % 2]
                S.op("act", lambda e, g_=g_, pb=pb: e.activation(out=g_, in_=ps[pb], func=AF.Gelu), [f"ps{pb}"], [f"px_gl{j % 2}"])
                S.op("dve", lambda e, g_=g_, j=j: e.tensor_tensor(out=G[:, j, :], in0=G[:, j, :], in1=g_, op=ALU.mult),
                     [f"px_gl{j % 2}", "px_G"], ["px_G"])
            for j in range(128):
                s_, w_ = stg[nld % NST], wt[nld % 3]
                sk, wk = f"px_s{nld % NST}", f"px_w{nld % 3}"
                nld += 1
                S.dma("sp", s_, vv[j], ["v_d"], [sk])
                if j % 2 == 0:
                    S.op("act", lambda e, s_=s_, w_=w_: e.activation(out=w_, in_=s_, func=AF.Copy), [sk], [wk])
                else:
                    S.op("dve", lambda e, s_=s_, w_=w_: e.tensor_copy(out=w_, in_=s_), [sk], [wk])
                for c in range(8):
                    S.op("pe", lambda e, w_=w_, c=c, j=j: e.matmul(ps[c], lhsT=w_[:, c * 128:(c + 1) * 128], rhs=G[:, j, :],
                                                                  start=(j == 0), stop=(j == 127)),
                         [wk, "px_G"], [f"ps{c}"])
            for ch in range(8):
                x_ = xt[ch % 2]
                S.dma("sp", x_, xv[ch, :, c0:c0 + TB], ["xT_d"], [f"px_x{ch % 2}"])
                for (sl, m) in col_pieces(c0):
                    S.op("dve", lambda e, x_=x_, sl=sl, ch=ch, m=m: e.scalar_tensor_tensor(
                        out=x_[:, sl], in0=ps[ch][:, sl], scalar=g2cols(ch, m), in1=x_[:, sl], op0=ALU.mult, op1=ALU.add),
                        [f"ps{ch}", f"px_x{ch % 2}", "mods"], [f"px_x{ch % 2}"])
                S.dma("sp", xv[ch, :, c0:c0 + TB], x_, [f"px_x{ch % 2}"], ["xT_d"])
    S.barrier()


EPS = 1e-6
SEG = 2304


def seg_m(col):
    s, pos = divmod(col, SEG)
    return 2 if pos < 256 else s


def col_pieces(c0):
    m0, m1 = seg_m(c0), seg_m(c0 + 256)
    if m0 == m1:
        return [(slice(0, 512), m0)]
    return [(slice(0, 256), m0), (slice(256, 512), m1)]


def adaln(C, l, adaw_d, adabT_d, ngT_d):
    S, ps = C.S, C.ps
    S.barrier()
    mods, modA = C.mods[l], C.modA[l]
    with contextlib.ExitStack() as st:
        w = [sb(C, st, f"ad_w{i}", [128, 8, 512], F32) for i in range(2)]
        bias = sb(C, st, "ad_b", [128, 48], F32)
        ng = sb(C, st, "ad_ng", [128, 2, 8], F32)
        tmp = sb(C, st, "ad_tmp", [128, 8, 3], F32)
        S.dma("sp", bias, adabT_d, ["adab"], ["ad_b"])
        S.dma("sp", ng, ngT_d, ["ng"], ["ad_ng"])
        wv = adaw_d.rearrange("(k p) f -> p k f", p=128)
        for cb in range(12):
            w_ = w[cb % 2]
            S.dma("sp", w_, wv[:, :, cb * 512:(cb + 1) * 512], ["adaw"], [f"ad_w{cb % 2}"])
            for j in range(4):
                col = (cb * 4 + j) * 3
                for k in range(8):
                    S.op("pe", lambda e, w_=w_, j=j, k=k, col=col: e.matmul(
                        ps[0][:, col:col + 3], lhsT=w_[:, k, j * 128:(j + 1) * 128], rhs=C.scT[:, k, :],
                        start=(k == 0), stop=(k == 7)), [f"ad_w{cb % 2}", "scT"], ["ps0"])
        S.op("dve", lambda e: e.tensor_tensor(out=mods, in0=ps[0][:, 0:144].rearrange("p (j m) -> p j m", m=3),
                                              in1=vw(bias, 0, [[1, 48], [0, 3]]), op=ALU.add), ["ps0", "ad_b"], ["mods"])
        for n, base in ((0, 8), (1, 32)):
            S.op("dve", lambda e, base=base: e.tensor_scalar(out=tmp, in0=mods[:, base:base + 8, :], scalar1=1.0, scalar2=None, op0=ALU.add),
                 ["mods"], ["ad_tmp"])
            S.op("dve", lambda e, n=n: e.tensor_tensor(out=modA[:, n, :, :], in0=tmp, in1=vw(ng, n * 8, [[1, 8], [0, 3]]), op=ALU.mult),
                 ["ad_tmp", "ad_ng"], ["mods"])
    S.barrier()


def norm_mod(C, l, which, xT_d, hT_d, NT):
    S, ps = C.S, C.ps
    S.barrier()
    mods, modA = C.mods[l], C.modA[l]
    shb = 0 if which == 0 else 24
    with contextlib.ExitStack() as st:
        x = [sb(C, st, f"nm_x{i}", [128, 8, 512], F32) for i in range(2)]
        sq = [sb(C, st, f"nm_sq{i}", [128, 8, 512], F32) for i in range(2)]
        r = [sb(C, st, f"nm_r{i}", [128, 512], F32) for i in range(2)]
        t1 = sb(C, st, "nm_t1", [128, 8, 512], F32)
        h = [sb(C, st, f"nm_h{i}", [128, 8, 512], BF16) for i in range(2)]
        xv = xT_d.rearrange("(k p) t -> p k t", p=128)
        hv = hT_d.rearrange("(k p) t -> p k t", p=128)
        NTI = NT // 512

        def stage1(ti):
            c0 = ti * 512
            b = ti % 2
            x_, sq_, r_ = x[b], sq[b], r[b]
            S.dma("sp", x_, xv[:, :, c0:c0 + 512], ["xT_d"], [f"nm_x{b}"])
            S.op("act", lambda e, x_=x_, sq_=sq_: e.activation(out=sq_, in_=x_, func=AF.Square), [f"nm_x{b}"], [f"nm_sq{b}"])
            for k in range(8):
                S.op("pe", lambda e, k=k, sq_=sq_, b=b: e.matmul(ps[b], lhsT=C.onesm, rhs=sq_[:, k, :], start=(k == 0), stop=(k == 7)),
                     [f"nm_sq{b}", "cst"], [f"ps{b}"])
            S.op("act", lambda e, r_=r_, b=b: e.activation(out=r_, in_=ps[b], func=AF.Sqrt, bias=C.epsc, scale=1.0), [f"ps{b}", "cst"], [f"nm_r{b}"])
            S.op("dve", lambda e, r_=r_: e.reciprocal(out=r_, in_=r_), [f"nm_r{b}"], [f"nm_r{b}"])

        def stage2(ti):
            c0 = ti * 512
            b = ti % 2
            x_, r_, h_ = x[b], r[b], h[b]
            for k in range(8):
                for (sl, m) in col_pieces(c0):
                    S.op("dve", lambda e, x_=x_, r_=r_, k=k, sl=sl, m=m: e.scalar_tensor_tensor(
                        out=t1[:, k, sl], in0=x_[:, k, sl], scalar=modA[:, which, k, m:m + 1], in1=r_[:, sl], op0=ALU.mult, op1=ALU.mult),
                        [f"nm_x{b}", f"nm_r{b}", "mods"], [f"nm_t1{k}"])
                    S.op("act", lambda e, h_=h_, k=k, sl=sl, m=m: e.activation(
                        out=h_[:, k, sl], in_=t1[:, k, sl], func=AF.Identity, bias=mods[:, shb + k, m:m + 1], scale=1.0),
                        [f"nm_t1{k}", "mods"], [f"nm_h{b}"])
            S.dma("sp", hv[:, :, c0:c0 + 512], h_, [f"nm_h{b}"], ["hT_d"])

        stage1(0)
        for ti in range(NTI):
            if ti + 1 < NTI:
                stage1(ti + 1)
            stage2(ti)
    S.barrier()


def resid_linear(C, l, inT_d, KC, w_d, gbase, xT_d, NT):
    S, ps = C.S, C.ps
    S.barrier()
    mods = C.mods[l]
    with contextlib.ExitStack() as st:
        w = sb(C, st, "rl_w", [128, KC, 1024], BF16)
        wst = sb(C, st, "rl_wst", [128, KC, 256], F32)
        a = [sb(C, st, f"rl_a{i}", [128, KC, 512], BF16) for i in range(2)]
        x = [sb(C, st, f"rl_x{i}", [128, 512], F32) for i in range(2)]
        wv = w_d.rearrange("(k p) f -> p k f", p=128)
        for c in range(4):
            S.dma("sp", wst, wv[:, :, c * 256:(c + 1) * 256], ["w_d"], ["rl_wst"])
            S.op("act", lambda e, c=c: e.activation(out=w[:, :, c * 256:(c + 1) * 256], in_=wst, func=AF.Copy), ["rl_wst"], ["rl_w"])
        av = inT_d.rearrange("(k p) t -> p k t", p=128)
        xv = xT_d.rearrange("(k p) t -> k p t", p=128)
        n = 0
        for ti in range(NT // 512):
            c0 = ti * 512
            a_ = a[ti % 2]
            S.dma("sp", a_, av[:, :, c0:c0 + 512], ["inT_d"], [f"rl_a{ti % 2}"])
            for oc in range(8):
                pb = 2 + n % 2
                x_ = x[n % 2]
                for k in range(KC):
                    S.op("pe", lambda e, a_=a_, k=k, oc=oc, pb=pb: e.matmul(ps[pb], lhsT=w[:, k, oc * 128:(oc + 1) * 128], rhs=a_[:, k, :],
                                                                         start=(k == 0), stop=(k == KC - 1)),
                         ["rl_w", f"rl_a{ti % 2}"], [f"ps{pb}"])
                S.dma("sp", x_, xv[oc, :, c0:c0 + 512], ["xT_d"], [f"rl_x{n % 2}"])
                for (sl, m) in col_pieces(c0):
                    S.op("dve", lambda e, x_=x_, pb=pb, sl=sl, oc=oc, m=m: e.scalar_tensor_tensor(
                        out=x_[:, sl], in0=ps[pb][:, sl], scalar=mods[:, gbase + oc, m:m + 1], in1=x_[:, sl], op0=ALU.mult, op1=ALU.add),
                        [f"ps{pb}", f"rl_x{n % 2}", "mods"], [f"rl_x{n % 2}"])
                S.dma("sp", xv[oc, :, c0:c0 + 512], x_, [f"rl_x{n % 2}"], ["xT_d"])
                n += 1
    S.barrier()


def na_proj(C, hT_d, wqkv_d, qg2_d, kg2_d, qkT_d, v0_d, v1_d, NT):
    S, ps = C.S, C.ps
    NS = NT // SEG
    S.barrier()
    with contextlib.ExitStack() as st:
        hT = sb(C, st, "np_hT", [128, 8, NT], BF16)
        wst = [sb(C, st, f"np_wst{i}", [128, 8, 128], F32) for i in range(2)]
        wb = [sb(C, st, f"np_wb{i}", [128, 8, 128], BF16) for i in range(2)]
        raw = sb(C, st, "np_raw", [128, 512], F32)
        sq = sb(C, st, "np_sq", [128, 512], F32)
        r = sb(C, st, "np_r", [128, 512], F32)
        row = [sb(C, st, f"np_row{i}", [128, NT], BF16) for i in range(2)]
        gc = sb(C, st, "np_gc", [128, 2], F32)
        wv = sb(C, st, "np_wv", [128, 8, 1024], BF16)
        wvst = sb(C, st, "np_wvst", [128, 8, 512], F32)
        vo = [sb(C, st, f"np_vo{i}", [128, 1024], BF16) for i in range(2)]
        hv = hT_d.rearrange("(k p) t -> p k t", p=128)
        for ti in range(NT // 512):
            S.dma("sp", hT[:, :, ti * 512:(ti + 1) * 512], hv[:, :, ti * 512:(ti + 1) * 512], ["hT_d"], ["np_hT"])
        S.dma("sp", gc[:, 0:1], qg2_d, ["qg"], ["np_gc"])
        S.dma("sp", gc[:, 1:2], kg2_d, ["kg"], ["np_gc"])
        wqv = wqkv_d.rearrange("(k p) f -> p k f", p=128)
        for j in range(16):
            ws_, wb_, row_ = wst[j % 2], wb[j % 2], row[j % 2]
            S.dma("sp", ws_, wqv[:, :, j * 128:(j + 1) * 128], ["wqkv"], [f"np_wst{j % 2}"])
            S.op("act", lambda e, ws_=ws_, wb_=wb_: e.activation(out=wb_, in_=ws_, func=AF.Copy), [f"np_wst{j % 2}"], [f"np_wb{j % 2}"])
            for ti in range(NT // 512):
                c0 = ti * 512
                for k in range(8):
                    S.op("pe", lambda e, wb_=wb_, k=k, c0=c0: e.matmul(ps[0], lhsT=wb_[:, k, :], rhs=hT[:, k, c0:c0 + 512], start=(k == 0), stop=(k == 7)),
                         [f"np_wb{j % 2}", "np_hT"], ["ps0"])
                S.op("act", lambda e: e.activation(out=raw, in_=ps[0], func=AF.Copy), ["ps0"], ["np_raw"])
                S.op("act", lambda e: e.activation(out=sq, in_=ps[0], func=AF.Square), ["ps0"], ["np_sq"])
                S.op("pe", lambda e: e.matmul(ps[1], lhsT=C.bd64, rhs=sq, start=True, stop=True), ["np_sq", "cst"], ["ps1"])
                S.op("act", lambda e: e.activation(out=r, in_=ps[1], func=AF.Sqrt, bias=C.epsc, scale=1.0), ["ps1", "cst"], ["np_r"])
                S.op("dve", lambda e: e.reciprocal(out=r, in_=r), ["np_r"], ["np_r"])
                gi = 0 if j < 8 else 1
                S.op("dve", lambda e, row_=row_, c0=c0, gi=gi: e.scalar_tensor_tensor(
                    out=row_[:, c0:c0 + 512], in0=raw, scalar=gc[:, gi:gi + 1], in1=r, op0=ALU.mult, op1=ALU.mult),
                    ["np_raw", "np_r", "np_gc"], [f"np_row{j % 2}"])
            S.dma("sp", qkT_d[j], row_, [f"np_row{j % 2}"], ["qkT_d"])
        for c in range(2):
            S.dma("sp", wvst, wqv[:, :, 2048 + c * 512:2048 + (c + 1) * 512], ["wqkv"], ["np_wvst"])
            S.op("act", lambda e, c=c: e.activation(out=wv[:, :, c * 512:(c + 1) * 512], in_=wvst, func=AF.Copy), ["np_wvst"], ["np_wv"])
        jobs = [(v0_d, T * 128, T * 128) for T in range(NT // 128)]
        for s in range(NS):
            for T in range(15):
                jobs.append((v1_d, (s * 15 + T) * 128, s * SEG + 320 + 128 * T))
        for n, (dst, r0, c0) in enumerate(jobs):
            vo_ = vo[n % 2]
            for hf in range(2):
                pb = 2 + hf
                for k in range(8):
                    S.op("pe", lambda e, k=k, c0=c0, hf=hf, pb=pb: e.matmul(ps[pb], lhsT=hT[:, k, c0:c0 + 128], rhs=wv[:, k, hf * 512:(hf + 1) * 512],
                                                                         start=(k == 0), stop=(k == 7)), ["np_hT", "np_wv"], [f"ps{pb}"])
                S.op("act", lambda e, vo_=vo_, hf=hf, pb=pb: e.activation(out=vo_[:, hf * 512:(hf + 1) * 512], in_=ps[pb], func=AF.Copy),
                     [f"ps{pb}"], [f"np_vo{n % 2}"])
            S.dma("sp", dst[r0:r0 + 128, :], vo_, [f"np_vo{n % 2}"], ["v_d"])
    S.barrier()


def na_attn(C, qkT_d, v0_d, v1_d, rpbG_d, oT_d, NT):
    S, ps = C.S, C.ps
    NS = NT // SEG
    S.barrier()
    with contextlib.ExitStack() as st:
        EB = sb(C, st, "na_EB", [128, 16, 15, 64], BF16)
        rst = sb(C, st, "na_rst", [128, 15, 64], F32)
        q = [sb(C, st, f"na_q{i}", [128, SEG], BF16) for i in range(2)]
        kk = [sb(C, st, f"na_k{i}", [128, SEG], BF16) for i in range(2)]
        V0 = [sb(C, st, f"na_v0{i}", [128, 18, 128], BF16) for i in range(2)]
        V1 = [sb(C, st, f"na_v1{i}", [128, 15, 128], BF16) for i in range(2)]
        o = [sb(C, st, f"na_o{i}", [128, SEG], BF16) for i in range(2)]
        P = [sb(C, st, f"na_P{i}", [128, 512], BF16) for i in range(3)]
        rd = [sb(C, st, f"na_rd{i}", [128, 512], F32) for i in range(2)]
        for h in range(16):
            S.dma("sp", rst, rpbG_d[:, h], ["rpbG"], ["na_rst"])
            S.op("act", lambda e: e.activation(out=rst, in_=rst, func=AF.Exp), ["na_rst"], ["na_rst"])
            S.op("dve", lambda e, h=h: e.tensor_tensor(out=EB[:, h], in0=rst, in1=vw(C.mask01, 0, [[0, 15], [1, 64]]), op=ALU.mult),
                 ["na_rst", "cst"], ["na_EB"])
        v0v = v0_d.rearrange("(t p) f -> p t f", p=128)
        v1v = v1_d.rearrange("(t p) f -> p t f", p=128)
        it = 0
        npx = 0
        for s in range(NS):
            for j in range(8):
                b = it % 2
                it += 1
                q_, k_, V0_, V1_, o_ = q[b], kk[b], V0[b], V1[b], o[b]
                S.dma("sp", q_, qkT_d[j][:, s * SEG:(s + 1) * SEG], ["qkT_d"], [f"na_q{b}"])
                S.dma("sp", k_, qkT_d[8 + j][:, s * SEG:(s + 1) * SEG], ["qkT_d"], [f"na_k{b}"])
                S.dma("sp", V0_, v0v[:, s * 18:(s + 1) * 18, j * 128:(j + 1) * 128], ["v_d"], [f"na_v0{b}"])
                S.dma("sp", V1_, v1v[:, s * 15:(s + 1) * 15, j * 128:(j + 1) * 128], ["v_d"], [f"na_v1{b}"])
                rdeps = [f"na_q{b}", f"na_k{b}"]
                items = []
                for hh in range(2):
                    items.append((hh, "ctx", -1))
                    for r in range(32):
                        items.append((hh, "row", r))
                LAG = 1
                pend = {}

                def stage1(hh, kind, r, npx):
                    h = 2 * j + hh
                    lo, hi = 64 * hh, 64 * hh + 64
                    sbk = npx % 2
                    P_ = P[npx % 3]
                    pk = f"na_P{npx % 3}"
                    if kind == "ctx":
                        for c in range(2):
                            S.op("pe", lambda e, c=c, lo=lo, hi=hi, sbk=sbk, k_=k_, q_=q_: e.matmul(
                                ps[sbk][:, c * 256:(c + 1) * 256], lhsT=k_[lo:hi, c * 128:(c + 1) * 128], rhs=q_[lo:hi, 0:256], start=True, stop=True),
                                rdeps, [f"ps{sbk}"])
                        S.op("act", lambda e, P_=P_, sbk=sbk: e.activation(out=P_, in_=ps[sbk], func=AF.Exp, scale=0.125), [f"ps{sbk}"], [pk])
                        return (P_, pk, None)
                    rs = min(max(r - 4, 0), 24)
                    qc = 256 + 64 * r
                    kcols = [256 + 64 * (rs + 2 * c) for c in range(4)] + [0, 128]
                    for c in range(6):
                        S.op("pe", lambda e, c=c, lo=lo, hi=hi, sbk=sbk, kc=kcols[c], qc=qc, k_=k_, q_=q_: e.matmul(
                            ps[sbk][:, c * 64:(c + 1) * 64], lhsT=k_[lo:hi, kc:kc + 128], rhs=q_[lo:hi, qc:qc + 64], start=True, stop=True),
                            rdeps, [f"ps{sbk}"])
                    S.op("act", lambda e, P_=P_, sbk=sbk: e.activation(out=P_[:, 0:384], in_=ps[sbk][:, 0:384], func=AF.Exp, scale=0.125),
                         [f"ps{sbk}"], [pk])
                    d0 = rs - r + 7
                    ebv = vw(EB, (h * 15 + d0) * 64, [[128, 4], [1, 64]])
                    S.op("dve", lambda e, P_=P_, ebv=ebv: e.tensor_tensor(out=P_[:, 0:256].rearrange("p (c q) -> p c q", c=4),
                                                                          in0=P_[:, 0:256].rearrange("p (c q) -> p c q", c=4), in1=ebv, op=ALU.mult),
                         [pk, "na_EB"], [pk])
                    return (P_, pk, rs)

                def stage2(hh, kind, r, P_, pk, rs):
                    lo, hi = 64 * hh, 64 * hh + 64
                    if kind == "ctx":
                        for c in range(2):
                            S.op("pe", lambda e, c=c, lo=lo, hi=hi, P_=P_, V0_=V0_: e.matmul(
                                ps[2][lo:hi, 0:256], lhsT=V0_[:, c, lo:hi], rhs=P_[:, c * 256:(c + 1) * 256], start=(c == 0), stop=(c == 1)),
                                [f"na_v0{b}", pk], ["ps2"])
                        for c in range(2):
                            S.op("pe", lambda e, c=c, lo=lo, hi=hi, P_=P_: e.matmul(
                                ps[3][lo:hi, 0:256], lhsT=C.onesb[:, 0:64], rhs=P_[:, c * 256:(c + 1) * 256], start=(c == 0), stop=(c == 1)),
                                ["cst", pk], ["ps3"])
                        rd_ = rd[0]
                        S.op("dve", lambda e, rd_=rd_, lo=lo, hi=hi: e.reciprocal(out=rd_[lo:hi, 0:256], in_=ps[3][lo:hi, 0:256]), ["ps3"], ["na_rd0"])
                        S.op("dve", lambda e, rd_=rd_, lo=lo, hi=hi, o_=o_: e.tensor_tensor(out=o_[lo:hi, 0:256], in0=ps[2][lo:hi, 0:256], in1=rd_[lo:hi, 0:256], op=ALU.mult),
                             ["ps2", "na_rd0"], [f"na_o{b}"])
                        return
                    rg, rr = divmod(r, 8)
                    ob, db = 4 + 2 * (rg % 2), 5 + 2 * (rg % 2)
                    vts = []
                    for c in range(4):
                        row0 = rs + 2 * c
                        if rs % 2 == 0:
                            vts.append((V0_, 2 + row0 // 2, f"na_v0{b}"))
                        else:
                            vts.append((V1_, (row0 - 1) // 2, f"na_v1{b}"))
                    vts += [(V0_, 0, f"na_v0{b}"), (V0_, 1, f"na_v0{b}")]
                    for c in range(6):
                        Vt, ti, vk = vts[c]
                        S.op("pe", lambda e, Vt=Vt, ti=ti, P_=P_, c=c, lo=lo, hi=hi, ob=ob, rr=rr: e.matmul(
                            ps[ob][lo:hi, rr * 64:(rr + 1) * 64], lhsT=Vt[:, ti, lo:hi], rhs=P_[:, c * 64:(c + 1) * 64], start=(c == 0), stop=(c == 5)),
                            [vk, pk], [f"ps{ob}"])
                    for c in range(6):
                        S.op("pe", lambda e, P_=P_, c=c, lo=lo, hi=hi, db=db, rr=rr: e.matmul(
                            ps[db][lo:hi, rr * 64:(rr + 1) * 64], lhsT=C.onesb[:, 0:64], rhs=P_[:, c * 64:(c + 1) * 64], start=(c == 0), stop=(c == 5)),
                            ["cst", pk], [f"ps{db}"])
                    if rr == 7:
                        rd_ = rd[rg % 2]
                        oc0 = 256 + rg * 512
                        S.op("dve", lambda e, rd_=rd_, lo=lo, hi=hi, db=db: e.reciprocal(out=rd_[lo:hi, :], in_=ps[db][lo:hi, :]), [f"ps{db}"], [f"na_rd{rg % 2}"])
                        S.op("dve", lambda e, rd_=rd_, lo=lo, hi=hi, ob=ob, oc0=oc0, o_=o_: e.tensor_tensor(
                            out=o_[lo:hi, oc0:oc0 + 512], in0=ps[ob][lo:hi, :], in1=rd_[lo:hi, :], op=ALU.mult),
                            [f"ps{ob}", f"na_rd{rg % 2}"], [f"na_o{b}"])

                for ix in range(len(items) + LAG):
                    if ix < len(items):
                        pend[ix] = stage1(*items[ix], npx)
                        npx += 1
                    jx = ix - LAG
                    if jx >= 0:
                        stage2(*items[jx], *pend.pop(jx))
                S.dma("sp", oT_d[j][:, s * SEG:(s + 1) * SEG], o_, [f"na_o{b}"], ["oT_d"])
    S.barrier()


def conv_segments(NT):
    segs = []
    for s in range(NT // SEG):
        segs.append((s * SEG, 256))
        segs.append((s * SEG + 256, 2048))
    return segs


def even_inproj(C, hT_d, win_d, cw_d, cb_d, dtb_d, alog_d, xbcT_d, lxT_d, zT_d, gT_d, csT_d, cstok_d, xd_d, NT):
    S, ps = C.S, C.ps
    S.barrier()
    NTT = NT // 128
    with contextlib.ExitStack() as st:
        hT = sb(C, st, "ei_hT", [128, 8, NT], BF16)
        prow0 = sb(C, st, "ei_prow", [128, NT], F32)
        yrow = sb(C, st, "ei_yrow", [128, NT], F32)
        cw = sb(C, st, "ei_cw", [128, 24, 4], F32)
        cb = sb(C, st, "ei_cb", [128, 24], F32)
        dtb = sb(C, st, "ei_dtb", [64, 2], F32)
        st1 = contextlib.ExitStack()
        wst = [sb(C, st1, f"ei_wst{i}", [128, 8, 128], F32) for i in range(2)]
        wb = [sb(C, st1, f"ei_wb{i}", [128, 8, 128], BF16) for i in range(2)]
        prow1 = sb(C, st1, "ei_prow1", [128, NT], F32)
        orow = sb(C, st1, "ei_orow", [128, NT], BF16)
        prows = [prow0, prow1]
        hv = hT_d.rearrange("(k p) t -> p k t", p=128)
        for ti in range(NT // 512):
            S.dma("sp", hT[:, :, ti * 512:(ti + 1) * 512], hv[:, :, ti * 512:(ti + 1) * 512], ["hT_d"], ["ei_hT"])
        S.dma("sp", cw, cw_d, ["cw_d"], ["ei_cw"])
        S.dma("sp", cb, cb_d, ["cb_d"], ["ei_cb"])
        S.dma("sp", dtb[:, 0:1], dtb_d, ["dtb_d"], ["ei_dtb"])
        S.dma("sp", dtb[:, 1:2], alog_d, ["alog_d"], ["ei_dtb"])
        wv = win_d.rearrange("(k p) f -> p k f", p=128)
        n = 0
        for j in range(41):
            ws_, wb_ = wst[j % 2], wb[j % 2]
            prow = prows[j % 2]
            prk = f"ei_prow{j % 2}"
            fw = 128 if j < 40 else 64
            S.dma("sp", ws_[:, :, 0:fw], wv[:, :, j * 128:j * 128 + fw], ["win_d"], [f"ei_wst{j % 2}"])
            S.op("act", lambda e, ws_=ws_, wb_=wb_, fw=fw: e.activation(out=wb_[:, :, 0:fw], in_=ws_[:, :, 0:fw], func=AF.Copy),
                 [f"ei_wst{j % 2}"], [f"ei_wb{j % 2}"])
            for ti in range(NT // 512):
                c0 = ti * 512
                pb = n % 2
                n += 1
                for k in range(8):
                    S.op("pe", lambda e, wb_=wb_, k=k, c0=c0, pb=pb, fw=fw: e.matmul(ps[pb][0:fw, :], lhsT=wb_[:, k, 0:fw], rhs=hT[:, k, c0:c0 + 512],
                                                                                start=(k == 0), stop=(k == 7)), [f"ei_wb{j % 2}", "ei_hT"], [f"ps{pb}"])
                fn = AF.Copy if (j < 24 or j == 40) else (AF.Silu if j < 32 else AF.Gelu)
                S.op("act", lambda e, pb=pb, c0=c0, fn=fn, fw=fw, prow=prow: e.activation(out=prow[0:fw, c0:c0 + 512], in_=ps[pb][0:fw, :], func=fn),
                     [f"ps{pb}"], [prk])
            if j < 24:
                for (s0, L) in conv_segments(NT):
                    S.op("dve", lambda e, j=j, s0=s0, L=L, prow=prow: e.tensor_scalar(out=yrow[:, s0:s0 + L], in0=prow[:, s0:s0 + L], scalar1=cw[:, j, 2:3],
                                                                          scalar2=cb[:, j:j + 1], op0=ALU.mult, op1=ALU.add), [prk, "ei_cw", "ei_cb"], ["ei_yrow"])
                    for (kk, do, so, ln) in ((1, 1, 0, L - 1), (0, 2, 0, L - 2), (3, 0, 1, L - 1)):
                        S.op("dve", lambda e, j=j, s0=s0, kk=kk, do=do, so=so, ln=ln, prow=prow: e.scalar_tensor_tensor(
                            out=yrow[:, s0 + do:s0 + do + ln], in0=prow[:, s0 + so:s0 + so + ln], scalar=cw[:, j, kk:kk + 1],
                            in1=yrow[:, s0 + do:s0 + do + ln], op0=ALU.mult, op1=ALU.add), [prk, "ei_cw", "ei_yrow"], ["ei_yrow"])
                if j < 16:
                    S.op("act", lambda e: e.activation(out=orow, in_=yrow, func=AF.Silu), ["ei_yrow"], ["ei_orow"])
                    S.dma("sp", xbcT_d[j * 128:(j + 1) * 128, :], orow, ["ei_orow"], ["xbcT_d"])
                else:
                    S.dma("sp", lxT_d[(j - 16) * 128:(j - 15) * 128, :], yrow, ["ei_yrow"], ["lxT_d"])
            elif j < 32:
                S.dma("sp", zT_d[(j - 24) * 128:(j - 23) * 128, :], prow, [prk], ["zT_d"])
            elif j < 40:
                S.dma("sp", gT_d[(j - 32) * 128:(j - 31) * 128, :], prow, [prk], ["gT_d"])
        st1.close()
        S.barrier()
        prow = prow0
        xx = yrow[0:64, :]
        t2 = sb(C, st, "ei_t2", [64, NT], F32)
        ea = sb(C, st, "ei_ea", [64, 1], F32)
        S.op("dve", lambda e: e.tensor_scalar(out=xx, in0=prow[0:64, :], scalar1=dtb[:, 0:1], scalar2=None, op0=ALU.add), ["ei_prow0", "ei_dtb"], ["ei_yrow"])
        S.op("act", lambda e: e.activation(out=t2, in_=xx, func=AF.Abs), ["ei_yrow"], ["ei_t2"])
        S.op("act", lambda e: e.activation(out=t2, in_=t2, func=AF.Exp, scale=-1.0), ["ei_t2"], ["ei_t2"])
        S.op("act", lambda e: e.activation(out=t2, in_=t2, func=AF.Ln, bias=C.onec[0:64, :], scale=1.0), ["ei_t2", "cst"], ["ei_t2"])
        S.op("dve", lambda e: e.scalar_tensor_tensor(out=xx, in0=xx, scalar=0.0, in1=t2, op0=ALU.max, op1=ALU.add), ["ei_yrow", "ei_t2"], ["ei_yrow"])
        S.op("act", lambda e: e.activation(out=ea, in_=dtb[:, 1:2], func=AF.Exp), ["ei_dtb"], ["ei_ea"])
        la = prow[0:64, :]
        S.op("dve", lambda e: e.tensor_scalar(out=la, in0=xx, scalar1=ea[:, 0:1], scalar2=-1.0, op0=ALU.mult, op1=ALU.mult), ["ei_yrow", "ei_ea"], ["ei_prow0"])
        cs = t2
        for s in range(NT // SEG):
            c0 = s * SEG
            S.op("dve", lambda e, c0=c0: e.tensor_tensor_scan(out=cs[0:32, c0:c0 + SEG], data0=vw(C.onec[0:32, :], 0, [[0, SEG]]), data1=la[0:32, c0:c0 + SEG],
                                                               initial=0.0, op0=ALU.mult, op1=ALU.add), ["ei_prow0", "cst"], ["ei_t2"])

            def rev(ap, a0, L):
                return vw(ap[32:64, :], a0 + L - 1, [[-1, L]])
            S.op("dve", lambda e, c0=c0: e.tensor_tensor_scan(out=rev(cs, c0, 256), data0=vw(C.onec[32:64, :], 0, [[0, 256]]), data1=rev(la, c0, 256),
                                                               initial=0.0, op0=ALU.mult, op1=ALU.add), ["ei_prow0", "cst"], ["ei_t2"])
            S.op("dve", lambda e, c0=c0: e.tensor_tensor_scan(out=rev(cs, c0 + 256, 2048), data0=vw(C.onec[32:64, :], 0, [[0, 2048]]), data1=rev(la, c0 + 256, 2048),
                                                               initial=cs[32:64, c0:c0 + 1], op0=ALU.mult, op1=ALU.add), ["ei_prow0", "cst", "ei_t2"], ["ei_t2"])
        S.dma("sp", csT_d, cs, ["ei_t2"], ["csT_d"])
        cstok = sb(C, st, "ei_cstok", [128, NTT, 64], F32)
        dttok = sb(C, st, "ei_dttok", [128, NTT, 64], F32)
        for tt in range(NTT):
            pb = 2 + tt % 2
            S.op("pe", lambda e, tt=tt, pb=pb: e.transpose(out=ps[pb][:, 0:64], in_=cs[:, tt * 128:(tt + 1) * 128], identity=C.ident[0:64, 0:64]),
                 ["ei_t2", "cst"], [f"ps{pb}"])
            S.op("pe", lambda e, tt=tt, pb=pb: e.transpose(out=ps[pb][:, 64:128], in_=xx[:, tt * 128:(tt + 1) * 128], identity=C.ident[0:64, 0:64]),
                 ["ei_yrow", "cst"], [f"ps{pb}"])
            S.op("act", lambda e, tt=tt, pb=pb: e.activation(out=cstok[:, tt, :], in_=ps[pb][:, 0:64], func=AF.Copy), [f"ps{pb}"], ["ei_cstok"])
            S.op("act", lambda e, tt=tt, pb=pb: e.activation(out=dttok[:, tt, :], in_=ps[pb][:, 64:128], func=AF.Copy), [f"ps{pb}"], ["ei_dttok"])
        S.dma("sp", cstok_d.rearrange("(t p) f -> p t f", p=128), cstok, ["ei_cstok"], ["cstok_d"])
        xb = [sb(C, st, f"ei_xb{i}", [128, 8, 512], BF16) for i in range(2)]
        xdt = [sb(C, st, f"ei_xdt{i}", [128, 1024], BF16) for i in range(2)]
        xbv = xbcT_d[0:1024, :].rearrange("(k p) t -> p k t", p=128)
        m = 0
        for bi in range(NT // 512):
            xb_ = xb[bi % 2]
            S.dma("sp", xb_, xbv[:, :, bi * 512:(bi + 1) * 512], ["xbcT_d"], [f"ei_xb{bi % 2}"])
            for t4 in range(4):
                tt = bi * 4 + t4
                pb = 4 + tt % 2
                pbf = ps[pb].bitcast(BF16)
                for k in range(8):
                    S.op("pe", lambda e, xb_=xb_, k=k, t4=t4, pbf=pbf: e.transpose(out=pbf[:, k * 128:(k + 1) * 128], in_=xb_[:, k, t4 * 128:(t4 + 1) * 128], identity=C.identb),
                         [f"ei_xb{bi % 2}", "cst"], [f"ps{pb}"])
                for d in range(2):
                    xd_ = xdt[m % 2]
                    S.op("dve", lambda e, xd_=xd_, pbf=pbf, tt=tt, d=d: e.tensor_tensor(
                        out=xd_.rearrange("p (h q) -> p h q", h=16), in0=pbf.rearrange("p (h q) -> p h q", h=16),
                        in1=vw(dttok, tt * 64 + d * 32, [[1, 16], [0, 64]]), op=ALU.mult), [f"ps{pb}", "ei_dttok"], [f"ei_xdt{m % 2}"])
                    S.dma("sp", xd_d[d][tt * 128:(tt + 1) * 128, :], xd_, [f"ei_xdt{m % 2}"], ["xd_d"])
                    m += 1
    S.barrier()


def ssd_allowed(d, lt):
    if lt == 0:
        return [(0, 0), (1, 1)]
    base = 2 + 4 * (lt - 1)
    if d == 0:
        return [(stl, None) for stl in range(base)] + [(base + r, r) for r in range(4)]
    return [(0, None), (1, None)] + [(base + r, r) for r in range(4)] + [(stl, None) for stl in range(base + 4, 18)]


def ssd_phase(C, xbcT_d, csT_d, cstok_d, xd_d, zT_d, dcol_d, sg_d, ymixT_d, NT):
    S, ps = C.S, C.ps
    S.barrier()
    with contextlib.ExitStack() as st:
        csT = sb(C, st, "sd_csT", [64, SEG], F32)
        cstok = sb(C, st, "sd_cstok", [128, 18, 64], F32)
        ncstok = sb(C, st, "sd_ncstok", [128, 18, 64], F32)
        Bg = sb(C, st, "sd_B", [128, SEG], BF16)
        Cg = sb(C, st, "sd_C", [128, SEG], BF16)
        xdg = [sb(C, st, f"sd_xd{d}", [128, 18, 256], BF16) for d in range(2)]
        xg = sb(C, st, "sd_x", [128, 2, SEG], BF16)
        csb = [sb(C, st, f"sd_csb{i}", [128, 512], F32) for i in range(8)]
        Gs = [sb(C, st, f"sd_G{i}", [128, 512], BF16) for i in range(3)]
        dd = [sb(C, st, f"sd_dd{i}", [128, 512], F32) for i in range(3)]
        ee = [sb(C, st, f"sd_ee{i}", [128, 512], BF16) for i in range(3)]
        M = [sb(C, st, f"sd_M{i}", [128, 512], BF16) for i in range(3)]
        zt = sb(C, st, "sd_z", [128, 2, 512], F32)
        y = sb(C, st, "sd_y", [128, 2, 512], F32)
        sq = sb(C, st, "sd_sq", [128, 512], F32)
        r = sb(C, st, "sd_r", [128, 512], F32)
        yo = sb(C, st, "sd_yo", [128, 2, 512], BF16)
        dcol = sb(C, st, "sd_dcol", [128, 8], F32)
        sg = sb(C, st, "sd_sg", [128, 8], F32)
        S.dma("sp", dcol, dcol_d, ["dcol_d"], ["sd_dcol"])
        S.dma("sp", sg, sg_d, ["sg_d"], ["sd_sg"])
        ctv = cstok_d.rearrange("(t p) f -> p t f", p=128)
        xdv = [xd_d[d].rearrange("(t p) f -> p t f", p=128) for d in range(2)]
        xv = xbcT_d[0:1024, :].rearrange("(k p) t -> p k t", p=128)
        zv = zT_d.rearrange("(k p) t -> p k t", p=128)
        yv = ymixT_d[0:1024, :].rearrange("(k p) t -> p k t", p=128)
        ng = 0
        nh = 0
        for s in range(NT // SEG):
            sc0 = s * SEG
            S.dma("sp", csT, csT_d[:, sc0:sc0 + SEG], ["csT_d"], ["sd_csT"])
            S.dma("sp", cstok, ctv[:, s * 18:(s + 1) * 18, :], ["cstok_d"], ["sd_cstok"])
            S.op("dve", lambda e: e.tensor_scalar(out=ncstok, in0=cstok, scalar1=-1.0, scalar2=None, op0=ALU.mult), ["sd_cstok"], ["sd_ncstok"])
            for g in range(4):
                S.dma("sp", Bg, xbcT_d[1024 + g * 128:1024 + (g + 1) * 128, sc0:sc0 + SEG], ["xbcT_d"], ["sd_B"])
                S.dma("sp", Cg, xbcT_d[1536 + g * 128:1536 + (g + 1) * 128, sc0:sc0 + SEG], ["xbcT_d"], ["sd_C"])
                for d in range(2):
                    S.dma("sp", xdg[d], xdv[d][:, s * 18:(s + 1) * 18, g * 256:(g + 1) * 256], ["xd_d"], [f"sd_xd{d}"])
                S.dma("sp", xg, xv[:, 2 * g:2 * g + 2, sc0:sc0 + SEG], ["xbcT_d"], ["sd_x"])
                for lt in range(5):
                    l0, W = (0, 256) if lt == 0 else (256 + 512 * (lt - 1), 512)
                    S.dma("sp", zt[:, :, 0:W], zv[:, 2 * g:2 * g + 2, sc0 + l0:sc0 + l0 + W], ["zT_d"], ["sd_z"])
                    for d in range(2):
                        for hh in range(4):
                            row = d * 32 + 4 * g + hh
                            i8 = d * 4 + hh
                            S.op("pe", lambda e, row=row, l0=l0, W=W: e.matmul(ps[7][:, 0:W], lhsT=C.esel[:, row, :], rhs=csT[:, l0:l0 + W], start=True, stop=True),
                                 ["cst", "sd_csT"], ["ps7"])
                            S.op("act", lambda e, i8=i8, W=W: e.activation(out=csb[i8][:, 0:W], in_=ps[7][:, 0:W], func=AF.Copy), ["ps7"], [f"sd_csb{i8}"])
                    started = [False] * 4
                    pairs = [(d, stl, mi) for d in range(2) for (stl, mi) in ssd_allowed(d, lt)]
                    last_b = [p_ for p_ in pairs if p_[0] == 1][-1][1]
                    items = [(pi, d, stl, mi, hh) for pi, (d, stl, mi) in enumerate(pairs) for hh in range(4)]
                    LAG = 2
                    pend = {}
                    cur = {}
                    for ix in range(len(items) + LAG):
                        if ix < len(items):
                            pi, d, stl, mi, hh = items[ix]
                            if hh == 0:
                                gb = 5 + ng % 2
                                G_ = Gs[ng % 3]
                                gk = f"sd_G{ng % 3}"
                                ng += 1
                                S.op("pe", lambda e, stl=stl, l0=l0, W=W, gb=gb: e.matmul(ps[gb][:, 0:W], lhsT=Bg[:, stl * 128:(stl + 1) * 128], rhs=Cg[:, l0:l0 + W], start=True, stop=True),
                                     ["sd_B", "sd_C"], [f"ps{gb}"])
                                S.op("act", lambda e, G_=G_, gb=gb, W=W: e.activation(out=G_[:, 0:W], in_=ps[gb][:, 0:W], func=AF.Copy), [f"ps{gb}"], [gk])
                                cur = dict(G_=G_, gk=gk)
                            row = d * 32 + 4 * g + hh
                            i8 = d * 4 + hh
                            dd_, ee_, M_ = dd[nh % 3], ee[nh % 3], M[nh % 3]
                            dk, ek, mk = f"sd_dd{nh % 3}", f"sd_ee{nh % 3}", f"sd_M{nh % 3}"
                            nh += 1
                            col = cstok[:, stl, row:row + 1]
                            if mi is None:
                                ncol = ncstok[:, stl, row:row + 1]
                                S.op("act", lambda e, ee_=ee_, i8=i8, ncol=ncol, W=W: e.activation(out=ee_[:, 0:W], in_=csb[i8][:, 0:W], func=AF.Exp, bias=ncol, scale=1.0),
                                     [f"sd_csb{i8}", "sd_ncstok"], [ek])
                            else:
                                S.op("dve", lambda e, dd_=dd_, i8=i8, col=col, W=W, d=d, mi=mi: e.scalar_tensor_tensor(
                                    out=dd_[:, 0:W], in0=csb[i8][:, 0:W], scalar=col, in1=C.negm[:, d * 4 + mi, 0:W], op0=ALU.subtract, op1=ALU.add),
                                    [f"sd_csb{i8}", "sd_cstok", "cst"], [dk])
                                S.op("act", lambda e, dd_=dd_, ee_=ee_, W=W: e.activation(out=ee_[:, 0:W], in_=dd_[:, 0:W], func=AF.Exp), [dk], [ek])
                            pend[ix] = (d, stl, hh, ee_, ek, M_, mk, cur["G_"], cur["gk"])
                        jx = ix - LAG
                        if jx < 0:
                            continue
                        d, stl, hh, ee_, ek, M_, mk, G_, gk = pend.pop(jx)
                        S.op("dve", lambda e, ee_=ee_, M_=M_, G_=G_, W=W: e.tensor_tensor(out=M_[:, 0:W], in0=ee_[:, 0:W], in1=G_[:, 0:W], op=ALU.mult), [ek, gk], [mk])
                        is_last = (d == 1 and stl == last_b)
                        ab = hh // 2
                        lo = (hh % 2) * 64
                        S.op("pe", lambda e, d=d, stl=stl, hh=hh, M_=M_, W=W, ab=ab, lo=lo, st_=(not started[hh]), is_last=is_last: e.matmul(
                            ps[ab][lo:lo + 64, 0:W], lhsT=xdg[d][:, stl, hh * 64:(hh + 1) * 64], rhs=M_[:, 0:W], start=st_, stop=is_last),
                            [f"sd_xd{d}", mk], [f"ps{ab}"])
                        started[hh] = True
                    for pr in range(2):
                        ch = 2 * g + pr
                        S.op("dve", lambda e, pr=pr, ch=ch, l0=l0, W=W: e.scalar_tensor_tensor(out=y[:, pr, 0:W], in0=xg[:, pr, l0:l0 + W], scalar=dcol[:, ch:ch + 1],
                                                                                            in1=ps[pr][:, 0:W], op0=ALU.mult, op1=ALU.add),
                             ["sd_x", "sd_dcol", f"ps{pr}"], ["sd_y"])
                        S.op("dve", lambda e, pr=pr, W=W: e.tensor_tensor(out=y[:, pr, 0:W], in0=y[:, pr, 0:W], in1=zt[:, pr, 0:W], op=ALU.mult), ["sd_y", "sd_z"], ["sd_y"])
                        S.op("act", lambda e, pr=pr, W=W: e.activation(out=sq[:, 0:W], in_=y[:, pr, 0:W], func=AF.Square), ["sd_y"], ["sd_sq"])
                        S.op("pe", lambda e, pr=pr, W=W: e.matmul(ps[4][:, 0:W], lhsT=C.ones256, rhs=sq[:, 0:W], start=(pr == 0), stop=(pr == 1)), ["sd_sq", "cst"], ["ps4"])
                    S.op("act", lambda e, W=W: e.activation(out=r[:, 0:W], in_=ps[4][:, 0:W], func=AF.Sqrt, bias=C.epsc, scale=1.0), ["ps4", "cst"], ["sd_r"])
                    S.op("dve", lambda e, W=W: e.reciprocal(out=r[:, 0:W], in_=r[:, 0:W]), ["sd_r"], ["sd_r"])
                    for pr in range(2):
                        ch = 2 * g + pr
                        S.op("dve", lambda e, pr=pr, ch=ch, W=W: e.scalar_tensor_tensor(out=yo[:, pr, 0:W], in0=y[:, pr, 0:W], scalar=sg[:, ch:ch + 1], in1=r[:, 0:W],
                                                                                     op0=ALU.mult, op1=ALU.mult), ["sd_y", "sd_sg", "sd_r"], ["sd_yo"])
                    S.dma("sp", yv[:, 2 * g:2 * g + 2, sc0 + l0:sc0 + l0 + W], yo[:, :, 0:W], ["sd_yo"], ["ymixT_d"])
    S.barrier()


def lru_phase(C, lxT_d, gT_d, wbd_d, bcol_d, lam_d, ymixT_d, NT):
    S, ps = C.S, C.ps
    S.barrier()
    with contextlib.ExitStack() as st:
        lxs = [sb(C, st, f"lr_x{i}", [128, SEG], F32) for i in range(2)]
        gts = [sb(C, st, f"lr_g{i}", [128, SEG], F32) for i in range(2)]
        Ws = [sb(C, st, f"lr_W{i}", [128, 4, 128], F32) for i in range(2)]
        bc = sb(C, st, "lr_bc", [128, 8, 4], F32)
        nsp = sb(C, st, "lr_nsp", [128, 8, 2, 2], F32)
        lam = sb(C, st, "lr_lam", [128, 8, 2], F32)
        rrs = [sb(C, st, f"lr_r{d}", [128, SEG], F32) for d in range(2)]
        iis = [sb(C, st, f"lr_i{d}", [128, SEG], F32) for d in range(2)]
        aas = [sb(C, st, f"lr_a{d}", [128, SEG], F32) for d in range(2)]
        bbs = [sb(C, st, f"lr_b{d}", [128, SEG], F32) for d in range(2)]
        hhs = [[sb(C, st, f"lr_h{i}{d}", [128, SEG], F32) for d in range(2)] for i in range(2)]
        yos = [sb(C, st, f"lr_yo{i}", [128, SEG], BF16) for i in range(2)]
        S.dma("sp", bc, bcol_d, ["bcol_d"], ["lr_bc"])
        S.dma("sp", lam, lam_d, ["lam_d"], ["lr_lam"])
        S.op("act", lambda e: e.activation(out=lam, in_=lam, func=AF.Exp, scale=-1.0), ["lr_lam"], ["lr_lam"])
        S.op("act", lambda e: e.activation(out=lam, in_=lam, func=AF.Ln, bias=C.onec, scale=1.0), ["lr_lam", "cst"], ["lr_lam"])
        S.op("dve", lambda e: e.tensor_scalar(out=nsp[:, :, :, 0], in0=lam, scalar1=-8.0, scalar2=None, op0=ALU.mult), ["lr_lam"], ["lr_nsp"])
        S.op("dve", lambda e: e.tensor_scalar(out=nsp[:, :, :, 1], in0=lam, scalar1=-16.0, scalar2=None, op0=ALU.mult), ["lr_lam"], ["lr_nsp"])
        tiles = [(0, 512), (512, 512), (1024, 512), (1536, 512), (2048, 256)]
        n = 0
        it = 0
        for c in range(8):
            W = Ws[c % 2]
            wk = f"lr_W{c % 2}"
            S.dma("sp", W, wbd_d[c], ["wbd_d"], [wk])
            for s_ in range(NT // SEG):
                sc0 = s_ * SEG
                ip = it % 2
                it += 1
                lx, gt, yo, hh = lxs[ip], gts[ip], yos[ip], hhs[ip]
                xk, gk, yk = f"lr_x{ip}", f"lr_g{ip}", f"lr_yo{ip}"
                S.dma("sp", lx, lxT_d[c * 128:(c + 1) * 128, sc0:sc0 + SEG], ["lxT_d"], [xk])
                S.dma("sp", gt, gT_d[c * 128:(c + 1) * 128, sc0:sc0 + SEG], ["gT_d"], [gk])
                for d in range(2):
                    rr, ii, aa, bb = rrs[d], iis[d], aas[d], bbs[d]
                    rk, ik, ak, bk, hk = f"lr_r{d}", f"lr_i{d}", f"lr_a{d}", f"lr_b{d}", f"lr_h{ip}{d}"
                    for gi, dst, dk in ((0, rr, rk), (1, ii, ik)):
                        for (t0, Wd) in tiles:
                            pb = n % 4
                            n += 1
                            S.op("pe", lambda e, d=d, gi=gi, t0=t0, Wd=Wd, pb=pb, W=W, lx=lx: e.matmul(ps[pb][:, 0:Wd], lhsT=W[:, d * 2 + gi, :], rhs=lx[:, t0:t0 + Wd], start=True, stop=True),
                                 [wk, xk], [f"ps{pb}"])
                            S.op("act", lambda e, dst=dst, d=d, gi=gi, t0=t0, Wd=Wd, pb=pb, c=c: e.activation(
                                out=dst[:, t0:t0 + Wd], in_=ps[pb][:, 0:Wd], func=AF.Sigmoid, bias=bc[:, c, d * 2 + gi:d * 2 + gi + 1], scale=1.0),
                                [f"ps{pb}", "lr_bc"], [dk])
                    S.op("act", lambda e, c=c, d=d, aa=aa, rr=rr: e.activation(out=aa, in_=rr, func=AF.Exp, scale=nsp[:, c, d, 0:1]), [rk, "lr_nsp"], [ak])
                    S.op("act", lambda e, c=c, d=d, bb=bb, rr=rr: e.activation(out=bb, in_=rr, func=AF.Exp, scale=nsp[:, c, d, 1:2]), [rk, "lr_nsp"], [bk])
                    S.op("dve", lambda e, bb=bb: e.tensor_scalar(out=bb, in0=bb, scalar1=-1.0, scalar2=1.0, op0=ALU.mult, op1=ALU.add), [bk], [bk])
                    S.op("act", lambda e, bb=bb: e.activation(out=bb, in_=bb, func=AF.Sqrt), [bk], [bk])
                    S.op("dve", lambda e, bb=bb, ii=ii: e.tensor_tensor(out=bb, in0=bb, in1=ii, op=ALU.mult), [bk, ik], [bk])
                    S.op("dve", lambda e, bb=bb, lx=lx: e.tensor_tensor(out=bb, in0=bb, in1=lx, op=ALU.mult), [bk, xk], [bk])
                    h_ = hh[d]
                    if d == 0:
                        S.op("dve", lambda e, h_=h_, aa=aa, bb=bb: e.tensor_tensor_scan(out=h_, data0=aa, data1=bb, initial=0.0, op0=ALU.mult, op1=ALU.add), [ak, bk], [hk])
                    else:
                        def rev(ap, a0, L):
                            return vw(ap, a0 + L - 1, [[-1, L]])
                        S.op("dve", lambda e, h_=h_, aa=aa, bb=bb: e.tensor_tensor_scan(out=rev(h_, 0, 256), data0=rev(aa, 0, 256), data1=rev(bb, 0, 256), initial=0.0,
                                                                                       op0=ALU.mult, op1=ALU.add), [ak, bk], [hk])
                        S.op("dve", lambda e, h_=h_, aa=aa, bb=bb: e.tensor_tensor_scan(out=rev(h_, 256, 2048), data0=rev(aa, 256, 2048), data1=rev(bb, 256, 2048), initial=h_[:, 0:1],
                                                                                       op0=ALU.mult, op1=ALU.add), [ak, bk, hk], [hk])
                S.op("dve", lambda e, hh=hh: e.tensor_tensor(out=hh[0], in0=hh[0], in1=hh[1], op=ALU.add), [f"lr_h{ip}0", f"lr_h{ip}1"], [f"lr_h{ip}0"])
                S.op("dve", lambda e, hh=hh, yo=yo, gt=gt: e.tensor_tensor(out=yo, in0=hh[0], in1=gt, op=ALU.mult), [f"lr_h{ip}0", gk], [yk])
                S.dma("sp", ymixT_d[1024 + c * 128:1024 + (c + 1) * 128, sc0:sc0 + SEG], yo, [yk], ["ymixT_d"])
    S.barrier()


NCST = 128 * 6 + 64 + 2
LAYERS = 4


def host_consts():
    c = np.zeros((128, NCST), np.float32)
    c[:, 0:128] = np.eye(128)
    c[:, 128:256] = np.arange(128)[None, :]
    c[:, 256:384] = 1.0 / 1024
    bd = np.zeros((128, 128), np.float32)
    bd[:64, :64] = 1.0 / 64
    bd[64:, 64:] = 1.0 / 64
    c[:, 384:512] = bd
    c[:, 512:640] = 1.0 / 256
    c[:, 640:768] = 1.0
    p = np.arange(128)
    kc = p % 64
    qc = np.arange(64)
    cs_ = np.clip(qc - 8, 0, 48)
    c[:, 768:832] = ((kc[:, None] >= cs_[None, :]) & (kc[:, None] < cs_[None, :] + 16)).astype(np.float32)
    c[:, 832] = EPS
    c[:, 833] = 1.0
    esel = np.zeros((64, 48, 128), np.float32)
    for r_ in range(48):
        esel[r_, r_, :] = 1.0
    s_ = np.arange(128)[:, None]
    l_ = np.arange(512)[None, :]
    negm = np.zeros((128, 8, 512), np.float32)
    for r_ in range(4):
        negm[:, r_, :] = np.where(l_ >= 128 * r_ + s_, 0.0, -1.0e6)
        negm[:, 4 + r_, :] = np.where(128 * r_ + s_ >= l_, 0.0, -1.0e6)
    return c, esel, negm


def build_program(NT=2 * SEG, layers=range(LAYERS), debug=False, stop_after=None, with_peer=True):
    nc = bass.Bass("TRN2", target_bir_lowering=False)
    C = Ctx()
    C.nc = nc
    C.S = S = Sched(nc)
    NS = NT // SEG

    def ein(name, shape, dt=F32):
        return nc.dram_tensor(name, list(shape), dt, kind="ExternalInput").ap()

    def scratch(name, shape, dt):
        return nc.dram_tensor(name, list(shape), dt, kind="ExternalOutput" if debug else "Internal").ap()

    x_in = ein("x_in", [1024, NT])
    cT_d = ein("cT", [128, 8, 3])
    cst_d = ein("cst", [128, NCST])
    esel_d = ein("esel", [64, 48, 128])
    negm_d = ein("negm", [128, 8, 512])
    adaw = ein("adaw", [LAYERS, 1024, 6144])
    adabT = ein("adabT", [LAYERS, 128, 48])
    ngT = ein("ngT", [LAYERS, 128, 2, 8])
    win = ein("win", [2, 1024, 5184])
    cw = ein("cw", [2, 128, 24, 4])
    cb = ein("cb", [2, 128, 24])
    dtb = ein("dtb", [2, 64, 1])
    alog = ein("alog", [2, 64, 1])
    dcol = ein("dcol", [2, 128, 8])
    sgc = ein("sgc", [2, 128, 8])
    wbd = ein("wbd", [2, 8, 128, 4, 128])
    bcol = ein("bcol", [2, 128, 8, 4])
    lamc = ein("lamc", [2, 128, 8, 2])
    wout = ein("wout", [2, 2048, 1024])
    wqkv = ein("wqkv", [2, 1024, 3072])
    qg2 = ein("qg2", [2, 128, 1])
    kg2 = ein("kg2", [2, 128, 1])
    rpbG = ein("rpbG", [2, 128, 16, 15, 64])
    wo = ein("wo", [2, 1024, 1024])
    if with_peer:
        pwq = ein("pwq", [LAYERS, 1024, 2048])
        pkeys = ein("pkeys", [LAYERS, 128, 16, 128])
        puT = ein("puT", [LAYERS, 128, 128, 8, 128])
        pv = ein("pv", [LAYERS, 16384, 1024])
    xT = nc.dram_tensor("xT", [1024, NT], F32, kind="ExternalOutput").ap()
    hT_d = scratch("hT_d", [1024, NT], BF16)
    rout_d = scratch("rout_d", [3, 128, NT], F32)
    xbcT_d = scratch("xbcT_d", [2048, NT], BF16)
    lxT_d = scratch("lxT_d", [1024, NT], F32)
    zT_d = scratch("zT_d", [1024, NT], F32)
    gT_d = scratch("gT_d", [1024, NT], F32)
    csT_d = scratch("csT_d", [64, NT], F32)
    cstok_d = scratch("cstok_d", [NT, 64], F32)
    xd_d = scratch("xd_d", [2, NT, 1024], BF16)
    ymixT_d = scratch("ymixT_d", [2048, NT], BF16)
    qkT_d = scratch("qkT_d", [16, 128, NT], BF16)
    v0_d = scratch("v0_d", [NT, 1024], BF16)
    v1_d = scratch("v1_d", [NS * 15 * 128, 1024], BF16)
    oT_d = scratch("oT_d", [8, 128, NT], BF16)

    C.ps = [nc.alloc_psum_tensor(f"psb{i}", [128, 512], F32).ap() for i in range(8)]
    cst = nc.alloc_sbuf_tensor("cst_sb", [128, NCST], F32).ap()
    C.ident, C.iota, C.onesm = cst[:, 0:128], cst[:, 128:256], cst[:, 256:384]
    C.bd64, C.ones256, C.mask01 = cst[:, 384:512], cst[:, 512:640], cst[:, 768:832]
    C.epsc, C.onec = cst[:, 832:833], cst[:, 833:834]
    C.identb = nc.alloc_sbuf_tensor("identb", [128, 128], BF16).ap()
    C.onesb = nc.alloc_sbuf_tensor("onesb", [128, 128], BF16).ap()
    C.scT = nc.alloc_sbuf_tensor("scT", [128, 8, 3], F32).ap()
    C.mods = [nc.alloc_sbuf_tensor(f"mods{l}", [128, 48, 3], F32).ap() for l in range(LAYERS)]
    C.modA = [nc.alloc_sbuf_tensor(f"modA{l}", [128, 2, 8, 3], F32).ap() for l in range(LAYERS)]
    S.dma("sp", cst, cst_d, ["cst_d"], ["cst"])
    S.op("act", lambda e: e.activation(out=C.identb, in_=C.ident, func=AF.Copy), ["cst"], ["cst"])
    S.op("act", lambda e: e.activation(out=C.onesb, in_=cst[:, 640:768], func=AF.Copy), ["cst"], ["cst"])
    S.dma("sp", C.scT, cT_d, ["cT_d"], ["scT"])
    S.op("act", lambda e: e.activation(out=C.scT, in_=C.scT, func=AF.Silu), ["scT"], ["scT"])
    for k in range(8):
        S.dma("sp", xT[k * 128:(k + 1) * 128, :], x_in[k * 128:(k + 1) * 128, :], ["x_in"], ["xT_d"])

    def done(tag):
        return stop_after is not None and tag == stop_after

    for l in layers:
        j = l // 2
        adaln(C, l, adaw[l], adabT[l], ngT[l])
        norm_mod(C, l, 0, xT, hT_d, NT)
        if done(f"nm{l}"):
            break
        if l % 2 == 0:
            even_inproj(C, hT_d, win[j], cw[j], cb[j], dtb[j], alog[j], xbcT_d, lxT_d, zT_d, gT_d, csT_d, cstok_d, xd_d, NT)
            if done(f"ei{l}"):
                break
            with contextlib.ExitStack() as st:
                C.esel = sb(C, st, "esel_sb", [64, 48, 128], F32)
                C.negm = sb(C, st, "negm_sb", [128, 8, 512], F32)
                S.dma("sp", C.esel, esel_d, ["esel_d"], ["cst"])
                S.dma("sp", C.negm, negm_d, ["negm_d"], ["cst"])
                ssd_phase(C, xbcT_d, csT_d, cstok_d, xd_d, zT_d, dcol[j], sgc[j], ymixT_d, NT)
            if done(f"ssd{l}"):
                break
            lru_phase(C, lxT_d, gT_d, wbd[j], bcol[j], lamc[j], ymixT_d, NT)
            if done(f"lru{l}"):
                break
            resid_linear(C, l, ymixT_d, 16, wout[j], 16, xT, NT)
        else:
            na_proj(C, hT_d, wqkv[j], qg2[j], kg2[j], qkT_d, v0_d, v1_d, NT)
            if done(f"np{l}"):
                break
            na_attn(C, qkT_d, v0_d, v1_d, rpbG[j], oT_d, NT)
            if done(f"na{l}"):
                break
            resid_linear(C, l, oT_d.rearrange("k p t -> (k p) t"), 8, wo[j], 16, xT, NT)
        if done(f"mix{l}"):
            break
        norm_mod(C, l, 1, xT, hT_d, NT)
        if not with_peer:
            continue
        peer_route(C, hT_d, pwq[l], pkeys[l], rout_d, NT)
        if done(f"pr{l}"):
            break
        peer_expert(C, hT_d, rout_d, puT[l], pv[l], xT, (lambda ch, m, l=l: C.mods[l][:, 40 + ch, m:m + 1]), NT, seg_m)
    n = S.finalize()
    return nc, n


def host_prep(inp, core, ncores=8):
    f = np.float32
    bs = slice(2 * core, 2 * core + 2)
    x, ctx, c, c_ctx = inp["x"][bs], inp["ctx"][bs], inp["c"][bs], inp["c_ctx"]
    seq = np.concatenate([ctx, x], axis=1)
    x_in = np.ascontiguousarray(seq.reshape(2 * SEG, 1024).T)
    cm = np.stack([c[0], c[1], c_ctx], axis=1)
    cT = np.ascontiguousarray(cm.reshape(8, 128, 3).transpose(1, 0, 2))
    return {"x_in": x_in.astype(f), "cT": cT.astype(f)}


def colT(a, nch):
    a = np.asarray(a, np.float32)
    lead = a.shape[:-1]
    return np.ascontiguousarray(np.moveaxis(a.reshape(*lead, nch, 128), -1, -2))


def host_shared(inp):
    f = np.float32
    g = lambda k: np.asarray(inp[k], f)
    cst, esel, negm = host_consts()
    d = {"cst": cst, "esel": esel, "negm": negm}
    d["adaw"] = g("ada_w")
    d["adabT"] = colT(g("ada_b"), 48)
    d["ngT"] = np.ascontiguousarray(np.stack([colT(g("norm1_g"), 8), colT(g("norm2_g"), 8)], axis=2))
    w = g("ev_w_in")
    z16 = np.zeros((2, 1024, 16), f)
    d["win"] = np.ascontiguousarray(np.concatenate(
        [w[:, :, 0:2048], w[:, :, 2080:3104], w[:, :, 3104:4128], w[:, :, 4128:5152], w[:, :, 2048:2064], z16, w[:, :, 2064:2080], z16], axis=2))
    cwx = np.concatenate([g("ev_conv_w"), g("ev_lru_conv_w")], axis=2)
    d["cw"] = np.ascontiguousarray(cwx.reshape(2, 4, 24, 128).transpose(0, 3, 2, 1))
    d["cb"] = colT(np.concatenate([g("ev_conv_b"), g("ev_lru_conv_b")], axis=1), 24)
    z16b = np.zeros((2, 16), f)
    dtb = g("ev_dt_bias")
    al = g("ev_a_log")
    d["dtb"] = np.ascontiguousarray(np.concatenate([dtb[:, 0], z16b, dtb[:, 1], z16b], axis=1)[:, :, None])
    d["alog"] = np.ascontiguousarray(np.concatenate([al[:, 0], z16b, al[:, 1], z16b], axis=1)[:, :, None])
    d["dcol"] = colT(np.repeat(g("ev_d"), 64, axis=1), 8)
    d["sgc"] = colT(g("ev_ssd_norm_g"), 8)
    wa, wx = g("ev_lru_wa"), g("ev_lru_wx")
    wbd = np.zeros((2, 8, 128, 4, 128), f)
    for dd_ in range(2):
        for gi, ww in enumerate((wa, wx)):
            for n_ in range(16):
                c_, o_ = n_ // 2, (n_ % 2) * 64
                wbd[:, c_, o_:o_ + 64, dd_ * 2 + gi, o_:o_ + 64] = ww[:, dd_, n_]
    d["wbd"] = wbd
    ba, bx = colT(g("ev_lru_ba"), 8), colT(g("ev_lru_bx"), 8)
    d["bcol"] = np.ascontiguousarray(np.stack([ba[:, 0], bx[:, 0], ba[:, 1], bx[:, 1]], axis=-1))
    d["lamc"] = np.ascontiguousarray(np.moveaxis(colT(g("ev_lru_lam"), 8), 1, -1))
    d["wout"] = g("ev_w_out")
    d["wqkv"] = g("od_w_qkv")
    d["qg2"] = np.ascontiguousarray(np.tile(g("od_q_norm_g"), (1, 2))[:, :, None])
    d["kg2"] = np.ascontiguousarray(np.tile(g("od_k_norm_g"), (1, 2))[:, :, None])
    rpb = g("od_rpb")
    p = np.arange(128)
    kc, up = p % 64, p // 64
    qc = np.arange(64)
    dc = np.clip(kc[:, None] - qc[None, :] + 15, 0, 30)
    dr = np.clip(np.arange(15)[None, :] + up[:, None], 0, 14)
    d["rpbG"] = np.ascontiguousarray(rpb[:, :, dr[:, :, None], dc[:, None, :]].transpose(0, 2, 1, 3, 4))
    d["wo"] = g("od_w_o")
    d["pwq"] = g("pe_w_q")
    d["pkeys"] = np.ascontiguousarray(g("pe_keys").reshape(4, 16, 128, 128).transpose(0, 3, 1, 2))
    u = g("pe_u")
    d["puT"] = np.ascontiguousarray(u.reshape(4, 128, 128, 8, 128).transpose(0, 2, 4, 3, 1))
    d["pv"] = g("pe_v")
    return d


_CACHE = {}


def kernel(**inputs):
    if "nc" not in _CACHE:
        _CACHE["nc"] = build_program()[0]
    nc = _CACHE["nc"]
    shared = host_shared(inputs)
    in_maps = []
    for core in range(8):
        m = dict(shared)
        m.update(host_prep(inputs, core))
        in_maps.append(m)
    res = run_bass_kernel_spmd(nc, in_maps, core_ids=list(range(8)))
    out = np.empty((16, 2048, 1024), np.float32)
    for core in range(8):
        xT = np.asarray(res.results[core]["xT"], np.float32)
        seq = xT.T.reshape(2, SEG, 1024)
        out[2 * core:2 * core + 2] = seq[:, 256:, :]
    return out
```

```python
import contextlib
import numpy as np
import ml_dtypes
import concourse.bass as bass
import concourse.mybir as mybir
from concourse.ap import AP
from concourse.bass_utils import run_bass_kernel_spmd

F32 = mybir.dt.float32
BF16 = mybir.dt.bfloat16
U32 = mybir.dt.uint32
AF = mybir.ActivationFunctionType
ALU = mybir.AluOpType
AX = mybir.AxisListType

ENGS = ("pe", "dve", "act", "pool", "sp")
NEG = -1.0e30


class Sched:
    NSLOT = 24
    ROT = 20000

    def __init__(self, nc):
        self.nc = nc
        self.ops = []
        self.eng = {"pe": nc.tensor, "dve": nc.vector, "act": nc.scalar,
                    "pool": nc.gpsimd, "sp": nc.sync}

    def op(self, eng, emit, reads=(), writes=(), dma=False):
        self.ops.append(dict(eng=eng, emit=emit, reads=tuple(reads),
                             writes=tuple(writes), dma=dma, bar=False))

    def dma(self, q, out, in_, reads, writes, **kw):
        self.op(q, lambda e: e.dma_start(out=out, in_=in_, **kw), reads, writes, dma=True)

    def barrier(self):
        self.ops.append(dict(bar=True))

    def finalize(self):
        nc = self.nc
        raw = self.ops
        ops = []
        last_w, readers = {}, {}
        slot_last = [None] * self.NSLOT
        nslot = 0
        last_on = {}
        pending_bar = {}
        for o in raw:
            if o["bar"]:
                extra = set(last_on.values()) | {s for s in slot_last if s is not None}
                for e in ENGS:
                    pending_bar[e] = set(extra) | pending_bar.get(e, set())
                continue
            i = len(ops)
            ops.append(o)
            d = set()
            for r in o["reads"]:
                if r in last_w:
                    d.add(last_w[r])
            for w in o["writes"]:
                if w in last_w:
                    d.add(last_w[w])
                d.update(readers.get(w, ()))
            if o["dma"]:
                s = nslot % self.NSLOT
                nslot += 1
                o["slot"] = s
                if slot_last[s] is not None:
                    d.add(slot_last[s])
                slot_last[s] = i
            if o["eng"] in pending_bar:
                d.update(pending_bar.pop(o["eng"]))
            d.discard(i)
            if o["eng"] == "pe" and not o["dma"]:
                d = {j for j in d if not (ops[j]["eng"] == "pe" and not ops[j]["dma"])}
            o["deps"] = d
            for w in o["writes"]:
                last_w[w] = i
                readers[w] = []
            for r in o["reads"]:
                if r not in o["writes"]:
                    readers.setdefault(r, []).append(i)
            last_on[o["eng"]] = i
        n = len(ops)
        signal = [False] * n
        for o in ops:
            for j in o["deps"]:
                signal[j] = True
        cnt = {e: 0 for e in ENGS}
        slot_cnt = [0] * self.NSLOT
        for i, o in enumerate(ops):
            if o["dma"]:
                slot_cnt[o["slot"]] += 1
                o["sig"] = ("slot", o["slot"], 16 * slot_cnt[o["slot"]])
            elif signal[i]:
                e = o["eng"]
                o["sig"] = (e, cnt[e] // self.ROT, cnt[e] % self.ROT + 1)
                cnt[e] += 1
            else:
                o["sig"] = None
        sems = {}
        for e in ENGS:
            for k in range(max((cnt[e] + self.ROT - 1) // self.ROT, 1)):
                sems[(e, k)] = nc.alloc_semaphore(f"s_{e}_{k}")
        for s in range(self.NSLOT):
            sems[("slot", s)] = nc.alloc_semaphore(f"s_dma_{s}")
        self.sems = sems
        waited = {e: {} for e in ENGS}
        for o in ops:
            e = o["eng"]
            eobj = self.eng[e]
            need = {}
            for j in o["deps"]:
                sg = ops[j]["sig"]
                key = (sg[0], sg[1])
                need[key] = max(need.get(key, 0), sg[2])
            for key, v in need.items():
                if waited[e].get(key, 0) >= v:
                    continue
                eobj.wait_ge(sems[key], v)
                waited[e][key] = v
            ins = o["emit"](eobj)
            sg = o["sig"]
            if sg is not None:
                if sg[0] == "slot":
                    ins.then_inc(sems[("slot", sg[1])], 16)
                else:
                    ins.then_inc(sems[(sg[0], sg[1])], 1)
        fin = {}
        for o in ops:
            if o["dma"]:
                fin[o["slot"]] = o["sig"][2]
        for s, v in fin.items():
            self.eng["sp"].wait_ge(sems[("slot", s)], v)
        self.ops = ops
        return n


def vw(base, off, dims):
    return AP(base.tensor, base.offset + off, [list(base.ap[0])] + [list(d) for d in dims])


class Ctx:
    pass


_UNIQ = [0]


def sb(C, st, name, shape, dt):
    _UNIQ[0] += 1
    return st.enter_context(C.nc.sbuf_tensor(f"{name}_{_UNIQ[0]}", shape, dt)).ap()


def peer_route(C, hT_d, wq_d, keysT_d, rout_d, NT):
    S, nc, ps = C.S, C.nc, C.ps
    S.barrier()
    with contextlib.ExitStack() as st:
        wq = sb(C, st, "pr_wq", [128, 8, 2048], BF16)
        wst = sb(C, st, "pr_wst", [128, 8, 512], F32)
        keys = sb(C, st, "pr_keys", [128, 16, 128], F32)
        hT = [sb(C, st, f"pr_hT{i}", [128, 8, 512], BF16) for i in range(2)]
        qTs = [sb(C, st, f"pr_qT{i}", [128, 16, 128], F32) for i in range(2)]
        scs = [sb(C, st, f"pr_sc{i}", [128, 2048], F32) for i in range(2)]
        tmp16 = sb(C, st, "pr_tmp16", [128, 16, 128], F32)
        tmp8 = sb(C, st, "pr_tmp8", [128, 8, 256], F32)
        stop = sb(C, st, "pr_stop", [128, 16, 16], F32)
        itop = sb(C, st, "pr_itop", [128, 16, 16], U32)
        itf = sb(C, st, "pr_itf", [128, 16, 16], F32)
        cand = sb(C, st, "pr_cand", [128, 8, 256], F32)
        best = sb(C, st, "pr_best", [128, 8, 16], F32)
        pos = sb(C, st, "pr_pos", [128, 8, 16], U32)
        posf = sb(C, st, "pr_posf", [128, 128], F32)
        av = sb(C, st, "pr_av", [128, 128], F32)
        bv = sb(C, st, "pr_bv", [128, 128], F32)
        eq = sb(C, st, "pr_eq", [128, 2048], F32)
        sel = sb(C, st, "pr_sel", [128, 3, 128], F32)
        zs = sb(C, st, "pr_zs", [128, 8], F32)
        selT = [sb(C, st, f"pr_selT{i}", [128, 3, 128], F32) for i in range(2)]
        thr = sb(C, st, "pr_thr", [128, 15], F32)
        wqv = wq_d.rearrange("(k p) f -> p k f", p=128)
        for c in range(4):
            S.dma("sp", wst, wqv[:, :, c * 512:(c + 1) * 512], ["wq_d"], ["pr_wst"])
            S.op("act", lambda e, c=c: e.activation(out=wq[:, :, c * 512:(c + 1) * 512], in_=wst, func=AF.Copy),
                 ["pr_wst"], ["pr_wq"])
        S.dma("sp", keys, keysT_d, ["keys_d"], ["pr_keys"])
        S.op("dve", lambda e: e.tensor_scalar(out=thr, in0=C.iota[:, 1:16], scalar1=16.0, scalar2=None, op0=ALU.mult), ["cst"], ["pr_thr"])
        hv = hT_d.rearrange("(k p) t -> p k t", p=128)
        rv = rout_d.rearrange("a s t -> s a t")
        STA = [f"pr_st{i}a" for i in range(16)]
        STB = [f"pr_st{i}b" for i in range(16)]
        ITA = [f"pr_it{i}a" for i in range(16)]
        ITB = [f"pr_it{i}b" for i in range(16)]
        BSA = [f"pr_bs{i}a" for i in range(8)]
        BSB = [f"pr_bs{i}b" for i in range(8)]
        PSA = [f"pr_ps{i}a" for i in range(8)]
        PSB = [f"pr_ps{i}b" for i in range(8)]
        nt = 0
        for blk in range(NT // 512):
            hT_ = hT[blk % 2]
            hk = f"pr_hT{blk % 2}"
            S.dma("sp", hT_, hv[:, :, blk * 512:(blk + 1) * 512], ["hT_d"], [hk])
            for tt in range(4):
                t0 = tt * 128
                qT, sc = qTs[nt % 2], scs[nt % 2]
                qk, sk = f"pr_qT{nt % 2}", f"pr_sc{nt % 2}"
                selT_ = selT[nt % 2]
                stk = f"pr_selT{nt % 2}"
                nt += 1
                for qc in range(16):
                    b = qc // 4
                    for k in range(8):
                        S.op("pe", lambda e, qc=qc, k=k, b=b, t0=t0, hT_=hT_: e.matmul(
                            ps[b][:, (qc % 4) * 128:(qc % 4 + 1) * 128], lhsT=wq[:, k, qc * 128:(qc + 1) * 128],
                            rhs=hT_[:, k, t0:t0 + 128], start=(k == 0), stop=(k == 7)),
                            ["pr_wq", hk], [f"ps{b}"])
                for b in range(4):
                    S.op("act", lambda e, b=b, qT=qT: e.activation(out=qT[:, 4 * b:4 * b + 4, :], in_=ps[b].rearrange("p (a t) -> p a t", a=4), func=AF.Copy),
                         [f"ps{b}"], [qk])
                for hz in range(16):
                    b = 4 + hz // 4
                    S.op("pe", lambda e, hz=hz, b=b, qT=qT: e.matmul(
                        ps[b][:, (hz % 4) * 128:(hz % 4 + 1) * 128], lhsT=qT[:, hz, :], rhs=keys[:, hz, :],
                        start=True, stop=True), [qk, "pr_keys"], [f"ps{b}"])
                for b in range(4):
                    S.op("act", lambda e, b=b, sc=sc: e.activation(out=sc[:, b * 512:(b + 1) * 512], in_=ps[4 + b], func=AF.Copy),
                         [f"ps{4 + b}"], [sk])
                G16 = range(16)
                for hz in G16:
                    S.op("dve", lambda e, hz=hz, sc=sc: e.max(out=stop[:, hz, 0:8], in_=sc[:, hz * 128:(hz + 1) * 128]), [sk], [STA[hz]])
                for hz in G16:
                    S.op("dve", lambda e, hz=hz, sc=sc: e.max_index(out=itop[:, hz, 0:8], in_max=stop[:, hz, 0:8], in_values=sc[:, hz * 128:(hz + 1) * 128]),
                         [sk, STA[hz]], [ITA[hz]])
                for hz in G16:
                    S.op("dve", lambda e, hz=hz, sc=sc: e.match_replace(out=tmp16[:, hz, :], in_to_replace=stop[:, hz, 0:8], in_values=sc[:, hz * 128:(hz + 1) * 128], imm_value=NEG),
                         [sk, STA[hz]], [f"pr_tm{hz}"])
                for hz in G16:
                    S.op("dve", lambda e, hz=hz: e.max(out=stop[:, hz, 8:16], in_=tmp16[:, hz, :]), [f"pr_tm{hz}"], [STB[hz]])
                for hz in G16:
                    S.op("dve", lambda e, hz=hz: e.max_index(out=itop[:, hz, 8:16], in_max=stop[:, hz, 8:16], in_values=tmp16[:, hz, :]),
                         [f"pr_tm{hz}", STB[hz]], [ITB[hz]])
                S.op("dve", lambda e: e.tensor_copy(out=itf, in_=itop), ITA + ITB, ["pr_itf"])
                in0 = vw(stop, 0, [[32, 8], [1, 16], [0, 16]])
                in1 = vw(stop, 16, [[32, 8], [0, 16], [1, 16]])
                S.op("dve", lambda e, in0=in0, in1=in1: e.tensor_tensor(out=cand.rearrange("p h (a b) -> p h a b", a=16), in0=in0, in1=in1, op=ALU.add),
                     STA + STB, ["pr_cand"])
                G8 = range(8)
                for h in G8:
                    S.op("dve", lambda e, h=h: e.max(out=best[:, h, 0:8], in_=cand[:, h, :]), ["pr_cand"], [BSA[h]])
                for h in G8:
                    S.op("dve", lambda e, h=h: e.max_index(out=pos[:, h, 0:8], in_max=best[:, h, 0:8], in_values=cand[:, h, :]), ["pr_cand", BSA[h]], [PSA[h]])
                for h in G8:
                    S.op("dve", lambda e, h=h: e.match_replace(out=tmp8[:, h, :], in_to_replace=best[:, h, 0:8], in_values=cand[:, h, :], imm_value=NEG),
                         ["pr_cand", BSA[h]], [f"pr_t8{h}"])
                for h in G8:
                    S.op("dve", lambda e, h=h: e.max(out=best[:, h, 8:16], in_=tmp8[:, h, :]), [f"pr_t8{h}"], [BSB[h]])
                for h in G8:
                    S.op("dve", lambda e, h=h: e.max_index(out=pos[:, h, 8:16], in_max=best[:, h, 8:16], in_values=tmp8[:, h, :]), [f"pr_t8{h}", BSB[h]], [PSB[h]])
                S.op("dve", lambda e: e.tensor_copy(out=posf, in_=pos.rearrange("p h k -> p (h k)")), PSA + PSB, ["pr_posf"])
                S.op("dve", lambda e: e.tensor_tensor(out=eq[:, 0:1920].rearrange("p (s m) -> p s m", m=15), in0=vw(posf, 0, [[1, 128], [0, 15]]),
                                                      in1=vw(thr, 0, [[0, 128], [1, 15]]), op=ALU.is_ge), ["pr_posf", "pr_thr"], ["pr_eq"])
                S.op("dve", lambda e: e.tensor_reduce(out=av, in_=eq[:, 0:1920].rearrange("p (s m) -> p s m", m=15), axis=AX.X, op=ALU.add), ["pr_eq"], ["pr_av"])
                S.op("dve", lambda e: e.scalar_tensor_tensor(out=bv, in0=av, scalar=-16.0, in1=posf, op0=ALU.mult, op1=ALU.add),
                     ["pr_posf", "pr_av"], ["pr_bv"])
                iota16 = vw(C.iota, 0, [[0, 8], [0, 16], [1, 16]])
                eq4 = eq.rearrange("p (h k a) -> p h k a", h=8, k=16)
                for z, src_ab in ((0, av), (1, bv)):
                    abb = vw(src_ab, 0, [[16, 8], [1, 16], [0, 16]])
                    itb = vw(itf, 16 * z, [[32, 8], [0, 16], [1, 16]])
                    S.op("dve", lambda e, abb=abb: e.tensor_tensor(out=eq4, in0=abb, in1=iota16, op=ALU.is_equal),
                         ["pr_av", "pr_bv"], ["pr_eq"])
                    S.op("dve", lambda e, itb=itb: e.tensor_tensor(out=eq4, in0=eq4, in1=itb, op=ALU.mult),
                         ["pr_eq", "pr_itf"], ["pr_eq"])
                    S.op("dve", lambda e, z=z: e.tensor_reduce(out=sel[:, z, :], in_=eq.rearrange("p (s a) -> p s a", a=16), axis=AX.X, op=ALU.add),
                         ["pr_eq"], ["pr_sel"])
                g3 = sel[:, 2, :].rearrange("p (h k) -> p h k", h=8)
                b0 = vw(best, 0, [[16, 8], [0, 16]])
                S.op("dve", lambda e: e.tensor_tensor(out=g3, in0=best, in1=b0, op=ALU.subtract), BSA + BSB, ["pr_sel"])
                S.op("act", lambda e: e.activation(out=sel[:, 2, :], in_=sel[:, 2, :], func=AF.Exp), ["pr_sel"], ["pr_sel"])
                S.op("dve", lambda e: e.tensor_reduce(out=zs, in_=g3, axis=AX.X, op=ALU.add), ["pr_sel"], ["pr_zs"])
                S.op("dve", lambda e: e.reciprocal(out=zs, in_=zs), ["pr_zs"], ["pr_zs"])
                zb = vw(zs, 0, [[1, 8], [0, 16]])
                S.op("dve", lambda e: e.tensor_tensor(out=g3, in0=g3, in1=zb, op=ALU.mult), ["pr_sel", "pr_zs"], ["pr_sel"])
                for a in range(3):
                    S.op("pe", lambda e, a=a: e.transpose(out=ps[0][:, a * 128:(a + 1) * 128], in_=sel[:, a, :], identity=C.ident),
                         ["pr_sel"], ["ps0"])
                S.op("act", lambda e, selT_=selT_: e.activation(out=selT_, in_=ps[0][:, 0:384].rearrange("p (a t) -> p a t", a=3), func=AF.Copy),
                     ["ps0"], [stk])
                c0 = blk * 512 + t0
                S.dma("sp", rv[:, :, c0:c0 + 128], selT_, [stk], ["rout_d"])
    S.barrier()


def peer_expert(C, hT_d, rout_d, uT_d, v_d, xT_d, g2cols, NT, seg_of_col):
    S, nc, ps = C.S, C.nc, C.ps
    TB = 512
    SBT = 16
    NST = 4
    S.barrier()
    with contextlib.ExitStack() as st:
        hT = sb(C, st, "px_hT", [128, 8, TB], BF16)
        rt = sb(C, st, "px_rt", [128, 3, TB], F32)
        A = [sb(C, st, f"px_A{i}", [128, 128, SBT], BF16) for i in range(2)]
        B = [sb(C, st, f"px_B{i}", [128, 128, SBT], BF16) for i in range(2)]
        iotaR = sb(C, st, "px_iotaR", [128, 128, SBT], BF16)
        idxb = sb(C, st, "px_idxb", [128, 2, TB], BF16)
        S.op("dve", lambda e: e.tensor_copy(out=iotaR, in_=vw(C.iota, 0, [[1, 128], [0, SBT]])), ["cst"], ["px_iotaR"])
        G = sb(C, st, "px_G", [128, 128, TB], BF16)
        wt = [sb(C, st, f"px_w{i}", [128, 1024], BF16) for i in range(3)]
        stg = [sb(C, st, f"px_s{i}", [128, 1024], F32) for i in range(NST)]
        gl = [sb(C, st, f"px_gl{i}", [128, TB], BF16) for i in range(2)]
        xt = [sb(C, st, f"px_x{i}", [128, TB], F32) for i in range(2)]
        hv = hT_d.rearrange("(k p) t -> p k t", p=128)
        rv = rout_d.rearrange("a s t -> s a t")
        vv = v_d.rearrange("(i j) d -> j i d", j=128)
        xv = xT_d.rearrange("(k p) t -> k p t", p=128)
        nld = 0
        for blk in range(NT // TB):
            c0 = blk * TB
            S.dma("sp", hT, hv[:, :, c0:c0 + TB], ["hT_d"], ["px_hT"])
            S.dma("sp", rt, rv[:, :, c0:c0 + TB], ["rout_d"], ["px_rt"])
            S.op("dve", lambda e: e.tensor_copy(out=idxb, in_=rt[:, 0:2, :]), ["px_rt"], ["px_idxb"])

            def onehots(sbi):
                tb0 = sbi * SBT
                Ab, Bb = A[sbi % 2], B[sbi % 2]
                i1b = vw(idxb, tb0, [[0, 128], [1, SBT]])
                i2b = vw(idxb, TB + tb0, [[0, 128], [1, SBT]])
                gb = vw(rt, 2 * TB + tb0, [[0, 128], [1, SBT]])
                S.op("dve", lambda e, Ab=Ab, i1b=i1b: e.tensor_tensor(out=Ab, in0=i1b, in1=iotaR, op=ALU.is_equal),
                     ["px_idxb", "px_iotaR"], [f"px_A{sbi % 2}"])
                S.op("dve", lambda e, Bb=Bb, i2b=i2b: e.tensor_tensor(out=Bb, in0=i2b, in1=iotaR, op=ALU.is_equal),
                     ["px_idxb", "px_iotaR"], [f"px_B{sbi % 2}"])
                S.op("pool", lambda e, Bb=Bb, gb=gb: e.tensor_tensor(out=Bb, in0=Bb, in1=gb, op=ALU.mult),
                     ["px_rt", f"px_B{sbi % 2}"], [f"px_B{sbi % 2}"])

            nsb = TB // SBT
            onehots(0)
            for sbi in range(nsb):
                if sbi + 1 < nsb:
                    onehots(sbi + 1)
                tb0 = sbi * SBT
                Ab, Bb = A[sbi % 2], B[sbi % 2]
                for q4 in range(SBT // 4):
                    pb = 4 + (sbi * (SBT // 4) + q4) % 2
                    for t in range(4):
                        tl = q4 * 4 + t
                        S.op("pe", lambda e, Ab=Ab, Bb=Bb, tl=tl, t=t, pb=pb: e.matmul(
                            ps[pb][:, t * 128:(t + 1) * 128], lhsT=Ab[:, :, tl], rhs=Bb[:, :, tl], start=True, stop=True),
                            [f"px_A{sbi % 2}", f"px_B{sbi % 2}"], [f"ps{pb}"])
                    tg = tb0 + q4 * 4
                    outv = vw(G, tg, [[TB, 128], [1, 4]])
                    inv = ps[pb].rearrange("p (t j) -> p j t", t=4)
                    S.op("act", lambda e, outv=outv, inv=inv: e.activation(out=outv, in_=inv, func=AF.Copy), [f"ps{pb}"], ["px_G"])
            ubuf = {}
            for jx in range(128 + 2):
                if jx < 128:
                    j = jx
                    s_, w_ = stg[nld % NST], wt[nld % 3]
                    sk, wk = f"px_s{nld % NST}", f"px_w{nld % 3}"
                    nld += 1
                    S.dma("sp", s_, uT_d[j].rearrange("p k i -> p (k i)"), ["uT_d"], [sk])
                    if j % 2 == 0:
                        S.op("act", lambda e, s_=s_, w_=w_: e.activation(out=w_, in_=s_, func=AF.Copy), [sk], [wk])
                    else:
                        S.op("dve", lambda e, s_=s_, w_=w_: e.tensor_copy(out=w_, in_=s_), [sk], [wk])
                    ubuf[j] = (w_, wk)
                j = jx - 2
                if j < 0:
                    continue
                w_, wk = ubuf.pop(j)
                pb = 6 + j % 2
                for k in range(8):
                    S.op("pe", lambda e, w_=w_, k=k, pb=pb: e.matmul(ps[pb], lhsT=w_[:, k * 128:(k + 1) * 128], rhs=hT[:, k, :], start=(k == 0), stop=(k == 7)),
                         [wk, "px_hT"], [f"ps{pb}"])
                g_ = gl[j % 2]
                S.op("act", lambda e, g_=g_, pb=pb: e.activation(out=g_, in_=ps[pb], func=AF.Gelu), [f"ps{pb}"], [f"px_gl{j % 2}"])
                S.op("dve", lambda e, g_=g_, j=j: e.tensor_tensor(out=G[:, j, :], in0=G[:, j, :], in1=g_, op=ALU.mult),
                     [f"px_gl{j % 2}", "px_G"], ["px_G"])
            for j in range(128):
                s_, w_ = stg[nld % NST], wt[nld % 3]
                sk, wk = f"px_s{nld % NST}", f"px_w{nld % 3}"
                nld += 1
                S.dma("sp", s_, vv[j], ["v_d"], [sk])
                if j % 2 == 0:
                    S.op("act", lambda e, s_=s_, w_=w_: e.activation(out=w_, in_=s_, func=AF.Copy), [sk], [wk])
                else:
                    S.op("dve", lambda e, s_=s_, w_=w_: e.tensor_copy(out=w_, in_=s_), [sk], [wk])
                for c in range(8):
                    S.op("pe", lambda e, w_=w_, c=c, j=j: e.matmul(ps[c], lhsT=w_[:, c * 128:(c + 1) * 128], rhs=G[:, j, :],
                                                                  start=(j == 0), stop=(j == 127)),
                         [wk, "px_G"], [f"ps{c}"])
            for ch in range(8):
                x_ = xt[ch % 2]
                S.dma("sp", x_, xv[ch, :, c0:c0 + TB], ["xT_d"], [f"px_x{ch % 2}"])
                for (sl, m) in col_pieces(c0):
                    S.op("dve", lambda e, x_=x_, sl=sl, ch=ch, m=m: e.scalar_tensor_tensor(
                        out=x_[:, sl], in0=ps[ch][:, sl], scalar=g2cols(ch, m), in1=x_[:, sl], op0=ALU.mult, op1=ALU.add),
                        [f"ps{ch}", f"px_x{ch % 2}", "mods"], [f"px_x{ch % 2}"])
                S.dma("sp", xv[ch, :, c0:c0 + TB], x_, [f"px_x{ch % 2}"], ["xT_d"])
    S.barrier()


EPS = 1e-6
SEG = 2304


def seg_m(col):
    s, pos = divmod(col, SEG)
    return 2 if pos < 256 else s


def col_pieces(c0):
    m0, m1 = seg_m(c0), seg_m(c0 + 256)
    if m0 == m1:
        return [(slice(0, 512), m0)]
    return [(slice(0, 256), m0), (slice(256, 512), m1)]


def adaln(C, l, adaw_d, adabT_d, ngT_d):
    S, ps = C.S, C.ps
    S.barrier()
    mods, modA = C.mods[l], C.modA[l]
    with contextlib.ExitStack() as st:
        w = [sb(C, st, f"ad_w{i}", [128, 8, 512], F32) for i in range(2)]
        bias = sb(C, st, "ad_b", [128, 48], F32)
        ng = sb(C, st, "ad_ng", [128, 2, 8], F32)
        tmp = sb(C, st, "ad_tmp", [128, 8, 3], F32)
        S.dma("sp", bias, adabT_d, ["adab"], ["ad_b"])
        S.dma("sp", ng, ngT_d, ["ng"], ["ad_ng"])
        wv = adaw_d.rearrange("(k p) f -> p k f", p=128)
        for cb in range(12):
            w_ = w[cb % 2]
            S.dma("sp", w_, wv[:, :, cb * 512:(cb + 1) * 512], ["adaw"], [f"ad_w{cb % 2}"])
            for j in range(4):
                col = (cb * 4 + j) * 3
                for k in range(8):
                    S.op("pe", lambda e, w_=w_, j=j, k=k, col=col: e.matmul(
                        ps[0][:, col:col + 3], lhsT=w_[:, k, j * 128:(j + 1) * 128], rhs=C.scT[:, k, :],
                        start=(k == 0), stop=(k == 7)), [f"ad_w{cb % 2}", "scT"], ["ps0"])
        S.op("dve", lambda e: e.tensor_tensor(out=mods, in0=ps[0][:, 0:144].rearrange("p (j m) -> p j m", m=3),
                                              in1=vw(bias, 0, [[1, 48], [0, 3]]), op=ALU.add), ["ps0", "ad_b"], ["mods"])
        for n, base in ((0, 8), (1, 32)):
            S.op("dve", lambda e, base=base: e.tensor_scalar(out=tmp, in0=mods[:, base:base + 8, :], scalar1=1.0, scalar2=None, op0=ALU.add),
                 ["mods"], ["ad_tmp"])
            S.op("dve", lambda e, n=n: e.tensor_tensor(out=modA[:, n, :, :], in0=tmp, in1=vw(ng, n * 8, [[1, 8], [0, 3]]), op=ALU.mult),
                 ["ad_tmp", "ad_ng"], ["mods"])
    S.barrier()


def norm_mod(C, l, which, xT_d, hT_d, NT):
    S, ps = C.S, C.ps
    S.barrier()
    mods, modA = C.mods[l], C.modA[l]
    shb = 0 if which == 0 else 24
    with contextlib.ExitStack() as st:
        x = [sb(C, st, f"nm_x{i}", [128, 8, 512], F32) for i in range(2)]
        sq = [sb(C, st, f"nm_sq{i}", [128, 8, 512], F32) for i in range(2)]
        r = [sb(C, st, f"nm_r{i}", [128, 512], F32) for i in range(2)]
        t1 = sb(C, st, "nm_t1", [128, 8, 512], F32)
        h = [sb(C, st, f"nm_h{i}", [128, 8, 512], BF16) for i in range(2)]
        xv = xT_d.rearrange("(k p) t -> p k t", p=128)
        hv = hT_d.rearrange("(k p) t -> p k t", p=128)
        NTI = NT // 512

        def stage1(ti):
            c0 = ti * 512
            b = ti % 2
            x_, sq_, r_ = x[b], sq[b], r[b]
            S.dma("sp", x_, xv[:, :, c0:c0 + 512], ["xT_d"], [f"nm_x{b}"])
            S.op("act", lambda e, x_=x_, sq_=sq_: e.activation(out=sq_, in_=x_, func=AF.Square), [f"nm_x{b}"], [f"nm_sq{b}"])
            for k in range(8):
                S.op("pe", lambda e, k=k, sq_=sq_, b=b: e.matmul(ps[b], lhsT=C.onesm, rhs=sq_[:, k, :], start=(k == 0), stop=(k == 7)),
                     [f"nm_sq{b}", "cst"], [f"ps{b}"])
            S.op("act", lambda e, r_=r_, b=b: e.activation(out=r_, in_=ps[b], func=AF.Sqrt, bias=C.epsc, scale=1.0), [f"ps{b}", "cst"], [f"nm_r{b}"])
            S.op("dve", lambda e, r_=r_: e.reciprocal(out=r_, in_=r_), [f"nm_r{b}"], [f"nm_r{b}"])

        def stage2(ti):
            c0 = ti * 512
            b = ti % 2
            x_, r_, h_ = x[b], r[b], h[b]
            for k in range(8):
                for (sl, m) in col_pieces(c0):
                    S.op("dve", lambda e, x_=x_, r_=r_, k=k, sl=sl, m=m: e.scalar_tensor_tensor(
                        out=t1[:, k, sl], in0=x_[:, k, sl], scalar=modA[:, which, k, m:m + 1], in1=r_[:, sl], op0=ALU.mult, op1=ALU.mult),
                        [f"nm_x{b}", f"nm_r{b}", "mods"], [f"nm_t1{k}"])
                    S.op("act", lambda e, h_=h_, k=k, sl=sl, m=m: e.activation(
                        out=h_[:, k, sl], in_=t1[:, k, sl], func=AF.Identity, bias=mods[:, shb + k, m:m + 1], scale=1.0),
                        [f"nm_t1{k}", "mods"], [f"nm_h{b}"])
            S.dma("sp", hv[:, :, c0:c0 + 512], h_, [f"nm_h{b}"], ["hT_d"])

        stage1(0)
        for ti in range(NTI):
            if ti + 1 < NTI:
                stage1(ti + 1)
            stage2(ti)
    S.barrier()


def resid_linear(C, l, inT_d, KC, w_d, gbase, xT_d, NT):
    S, ps = C.S, C.ps
    S.barrier()
    mods = C.mods[l]
    with contextlib.ExitStack() as st:
        w = sb(C, st, "rl_w", [128, KC, 1024], BF16)
        wst = sb(C, st, "rl_wst", [128, KC, 256], F32)
        a = [sb(C, st, f"rl_a{i}", [128, KC, 512], BF16) for i in range(2)]
        x = [sb(C, st, f"rl_x{i}", [128, 512], F32) for i in range(2)]
        wv = w_d.rearrange("(k p) f -> p k f", p=128)
        for c in range(4):
            S.dma("sp", wst, wv[:, :, c * 256:(c + 1) * 256], ["w_d"], ["rl_wst"])
            S.op("act", lambda e, c=c: e.activation(out=w[:, :, c * 256:(c + 1) * 256], in_=wst, func=AF.Copy), ["rl_wst"], ["rl_w"])
        av = inT_d.rearrange("(k p) t -> p k t", p=128)
        xv = xT_d.rearrange("(k p) t -> k p t", p=128)
        n = 0
        for ti in range(NT // 512):
            c0 = ti * 512
            a_ = a[ti % 2]
            S.dma("sp", a_, av[:, :, c0:c0 + 512], ["inT_d"], [f"rl_a{ti % 2}"])
            for oc in range(8):
                pb = 2 + n % 2
                x_ = x[n % 2]
                for k in range(KC):
                    S.op("pe", lambda e, a_=a_, k=k, oc=oc, pb=pb: e.matmul(ps[pb], lhsT=w[:, k, oc * 128:(oc + 1) * 128], rhs=a_[:, k, :],
                                                                         start=(k == 0), stop=(k == KC - 1)),
                         ["rl_w", f"rl_a{ti % 2}"], [f"ps{pb}"])
                S.dma("sp", x_, xv[oc, :, c0:c0 + 512], ["xT_d"], [f"rl_x{n % 2}"])
                for (sl, m) in col_pieces(c0):
                    S.op("dve", lambda e, x_=x_, pb=pb, sl=sl, oc=oc, m=m: e.scalar_tensor_tensor(
                        out=x_[:, sl], in0=ps[pb][:, sl], scalar=mods[:, gbase + oc, m:m + 1], in1=x_[:, sl], op0=ALU.mult, op1=ALU.add),
                        [f"ps{pb}", f"rl_x{n % 2}", "mods"], [f"rl_x{n % 2}"])
                S.dma("sp", xv[oc, :, c0:c0 + 512], x_, [f"rl_x{n % 2}"], ["xT_d"])
                n += 1
    S.barrier()


def na_proj(C, hT_d, wqkv_d, qg2_d, kg2_d, qkT_d, v0_d, v1_d, NT):
    S, ps = C.S, C.ps
    NS = NT // SEG
    S.barrier()
    with contextlib.ExitStack() as st:
        hT = sb(C, st, "np_hT", [128, 8, NT], BF16)
        wst = [sb(C, st, f"np_wst{i}", [128, 8, 128], F32) for i in range(2)]
        wb = [sb(C, st, f"np_wb{i}", [128, 8, 128], BF16) for i in range(2)]
        raw = sb(C, st, "np_raw", [128, 512], F32)
        sq = sb(C, st, "np_sq", [128, 512], F32)
        r = sb(C, st, "np_r", [128, 512], F32)
        row = [sb(C, st, f"np_row{i}", [128, NT], BF16) for i in range(2)]
        gc = sb(C, st, "np_gc", [128, 2], F32)
        wv = sb(C, st, "np_wv", [128, 8, 1024], BF16)
        wvst = sb(C, st, "np_wvst", [128, 8, 512], F32)
        vo = [sb(C, st, f"np_vo{i}", [128, 1024], BF16) for i in range(2)]
        hv = hT_d.rearrange("(k p) t -> p k t", p=128)
        for ti in range(NT // 512):
            S.dma("sp", hT[:, :, ti * 512:(ti + 1) * 512], hv[:, :, ti * 512:(ti + 1) * 512], ["hT_d"], ["np_hT"])
        S.dma("sp", gc[:, 0:1], qg2_d, ["qg"], ["np_gc"])
        S.dma("sp", gc[:, 1:2], kg2_d, ["kg"], ["np_gc"])
        wqv = wqkv_d.rearrange("(k p) f -> p k f", p=128)
        for j in range(16):
            ws_, wb_, row_ = wst[j % 2], wb[j % 2], row[j % 2]
            S.dma("sp", ws_, wqv[:, :, j * 128:(j + 1) * 128], ["wqkv"], [f"np_wst{j % 2}"])
            S.op("act", lambda e, ws_=ws_, wb_=wb_: e.activation(out=wb_, in_=ws_, func=AF.Copy), [f"np_wst{j % 2}"], [f"np_wb{j % 2}"])
            for ti in range(NT // 512):
                c0 = ti * 512
                for k in range(8):
                    S.op("pe", lambda e, wb_=wb_, k=k, c0=c0: e.matmul(ps[0], lhsT=wb_[:, k, :], rhs=hT[:, k, c0:c0 + 512], start=(k == 0), stop=(k == 7)),
                         [f"np_wb{j % 2}", "np_hT"], ["ps0"])
                S.op("act", lambda e: e.activation(out=raw, in_=ps[0], func=AF.Copy), ["ps0"], ["np_raw"])
                S.op("act", lambda e: e.activation(out=sq, in_=ps[0], func=AF.Square), ["ps0"], ["np_sq"])
                S.op("pe", lambda e: e.matmul(ps[1], lhsT=C.bd64, rhs=sq, start=True, stop=True), ["np_sq", "cst"], ["ps1"])
                S.op("act", lambda e: e.activation(out=r, in_=ps[1], func=AF.Sqrt, bias=C.epsc, scale=1.0), ["ps1", "cst"], ["np_r"])
                S.op("dve", lambda e: e.reciprocal(out=r, in_=r), ["np_r"], ["np_r"])
                gi = 0 if j < 8 else 1
                S.op("dve", lambda e, row_=row_, c0=c0, gi=gi: e.scalar_tensor_tensor(
                    out=row_[:, c0:c0 + 512], in0=raw, scalar=gc[:, gi:gi + 1], in1=r, op0=ALU.mult, op1=ALU.mult),
                    ["np_raw", "np_r", "np_gc"], [f"np_row{j % 2}"])
            S.dma("sp", qkT_d[j], row_, [f"np_row{j % 2}"], ["qkT_d"])
        for c in range(2):
            S.dma("sp", wvst, wqv[:, :, 2048 + c * 512:2048 + (c + 1) * 512], ["wqkv"], ["np_wvst"])
            S.op("act", lambda e, c=c: e.activation(out=wv[:, :, c * 512:(c + 1) * 512], in_=wvst, func=AF.Copy), ["np_wvst"], ["np_wv"])
        jobs = [(v0_d, T * 128, T * 128) for T in range(NT // 128)]
        for s in range(NS):
            for T in range(15):
                jobs.append((v1_d, (s * 15 + T) * 128, s * SEG + 320 + 128 * T))
        for n, (dst, r0, c0) in enumerate(jobs):
            vo_ = vo[n % 2]
            for hf in range(2):
                pb = 2 + hf
                for k in range(8):
                    S.op("pe", lambda e, k=k, c0=c0, hf=hf, pb=pb: e.matmul(ps[pb], lhsT=hT[:, k, c0:c0 + 128], rhs=wv[:, k, hf * 512:(hf + 1) * 512],
                                                                         start=(k == 0), stop=(k == 7)), ["np_hT", "np_wv"], [f"ps{pb}"])
                S.op("act", lambda e, vo_=vo_, hf=hf, pb=pb: e.activation(out=vo_[:, hf * 512:(hf + 1) * 512], in_=ps[pb], func=AF.Copy),
                     [f"ps{pb}"], [f"np_vo{n % 2}"])
            S.dma("sp", dst[r0:r0 + 128, :], vo_, [f"np_vo{n % 2}"], ["v_d"])
    S.barrier()


def na_attn(C, qkT_d, v0_d, v1_d, rpbG_d, oT_d, NT):
    S, ps = C.S, C.ps
    NS = NT // SEG
    S.barrier()
    with contextlib.ExitStack() as st:
        EB = sb(C, st, "na_EB", [128, 16, 15, 64], BF16)
        rst = sb(C, st, "na_rst", [128, 15, 64], F32)
        q = [sb(C, st, f"na_q{i}", [128, SEG], BF16) for i in range(2)]
        kk = [sb(C, st, f"na_k{i}", [128, SEG], BF16) for i in range(2)]
        V0 = [sb(C, st, f"na_v0{i}", [128, 18, 128], BF16) for i in range(2)]
        V1 = [sb(C, st, f"na_v1{i}", [128, 15, 128], BF16) for i in range(2)]
        o = [sb(C, st, f"na_o{i}", [128, SEG], BF16) for i in range(2)]
        P = [sb(C, st, f"na_P{i}", [128, 512], BF16) for i in range(3)]
        rd = [sb(C, st, f"na_rd{i}", [128, 512], F32) for i in range(2)]
        for h in range(16):
            S.dma("sp", rst, rpbG_d[:, h], ["rpbG"], ["na_rst"])
            S.op("act", lambda e: e.activation(out=rst, in_=rst, func=AF.Exp), ["na_rst"], ["na_rst"])
            S.op("dve", lambda e, h=h: e.tensor_tensor(out=EB[:, h], in0=rst, in1=vw(C.mask01, 0, [[0, 15], [1, 64]]), op=ALU.mult),
                 ["na_rst", "cst"], ["na_EB"])
        v0v = v0_d.rearrange("(t p) f -> p t f", p=128)
        v1v = v1_d.rearrange("(t p) f -> p t f", p=128)
        it = 0
        npx = 0
        for s in range(NS):
            for j in range(8):
                b = it % 2
                it += 1
                q_, k_, V0_, V1_, o_ = q[b], kk[b], V0[b], V1[b], o[b]
                S.dma("sp", q_, qkT_d[j][:, s * SEG:(s + 1) * SEG], ["qkT_d"], [f"na_q{b}"])
                S.dma("sp", k_, qkT_d[8 + j][:, s * SEG:(s + 1) * SEG], ["qkT_d"], [f"na_k{b}"])
                S.dma("sp", V0_, v0v[:, s * 18:(s + 1) * 18, j * 128:(j + 1) * 128], ["v_d"], [f"na_v0{b}"])
                S.dma("sp", V1_, v1v[:, s * 15:(s + 1) * 15, j * 128:(j + 1) * 128], ["v_d"], [f"na_v1{b}"])
                rdeps = [f"na_q{b}", f"na_k{b}"]
                items = []
                for hh in range(2):
                    items.append((hh, "ctx", -1))
                    for r in range(32):
                        items.append((hh, "row", r))
                LAG = 1
                pend = {}

                def stage1(hh, kind, r, npx):
                    h = 2 * j + hh
                    lo, hi = 64 * hh, 64 * hh + 64
                    sbk = npx % 2
                    P_ = P[npx % 3]
                    pk = f"na_P{npx % 3}"
                    if kind == "ctx":
                        for c in range(2):
                            S.op("pe", lambda e, c=c, lo=lo, hi=hi, sbk=sbk, k_=k_, q_=q_: e.matmul(
                                ps[sbk][:, c * 256:(c + 1) * 256], lhsT=k_[lo:hi, c * 128:(c + 1) * 128], rhs=q_[lo:hi, 0:256], start=True, stop=True),
                                rdeps, [f"ps{sbk}"])
                        S.op("act", lambda e, P_=P_, sbk=sbk: e.activation(out=P_, in_=ps[sbk], func=AF.Exp, scale=0.125), [f"ps{sbk}"], [pk])
                        return (P_, pk, None)
                    rs = min(max(r - 4, 0), 24)
                    qc = 256 + 64 * r
                    kcols = [256 + 64 * (rs + 2 * c) for c in range(4)] + [0, 128]
                    for c in range(6):
                        S.op("pe", lambda e, c=c, lo=lo, hi=hi, sbk=sbk, kc=kcols[c], qc=qc, k_=k_, q_=q_: e.matmul(
                            ps[sbk][:, c * 64:(c + 1) * 64], lhsT=k_[lo:hi, kc:kc + 128], rhs=q_[lo:hi, qc:qc + 64], start=True, stop=True),
                            rdeps, [f"ps{sbk}"])
                    S.op("act", lambda e, P_=P_, sbk=sbk: e.activation(out=P_[:, 0:384], in_=ps[sbk][:, 0:384], func=AF.Exp, scale=0.125),
                         [f"ps{sbk}"], [pk])
                    d0 = rs - r + 7
                    ebv = vw(EB, (h * 15 + d0) * 64, [[128, 4], [1, 64]])
                    S.op("dve", lambda e, P_=P_, ebv=ebv: e.tensor_tensor(out=P_[:, 0:256].rearrange("p (c q) -> p c q", c=4),
                                                                          in0=P_[:, 0:256].rearrange("p (c q) -> p c q", c=4), in1=ebv, op=ALU.mult),
                         [pk, "na_EB"], [pk])
                    return (P_, pk, rs)

                def stage2(hh, kind, r, P_, pk, rs):
                    lo, hi = 64 * hh, 64 * hh + 64
                    if kind == "ctx":
                        for c in range(2):
                            S.op("pe", lambda e, c=c, lo=lo, hi=hi, P_=P_, V0_=V0_: e.matmul(
                                ps[2][lo:hi, 0:256], lhsT=V0_[:, c, lo:hi], rhs=P_[:, c * 256:(c + 1) * 256], start=(c == 0), stop=(c == 1)),
                                [f"na_v0{b}", pk], ["ps2"])
                        for c in range(2):
                            S.op("pe", lambda e, c=c, lo=lo, hi=hi, P_=P_: e.matmul(
                                ps[3][lo:hi, 0:256], lhsT=C.onesb[:, 0:64], rhs=P_[:, c * 256:(c + 1) * 256], start=(c == 0), stop=(c == 1)),
                                ["cst", pk], ["ps3"])
                        rd_ = rd[0]
                        S.op("dve", lambda e, rd_=rd_, lo=lo, hi=hi: e.reciprocal(out=rd_[lo:hi, 0:256], in_=ps[3][lo:hi, 0:256]), ["ps3"], ["na_rd0"])
                        S.op("dve", lambda e, rd_=rd_, lo=lo, hi=hi, o_=o_: e.tensor_tensor(out=o_[lo:hi, 0:256], in0=ps[2][lo:hi, 0:256], in1=rd_[lo:hi, 0:256], op=ALU.mult),
                             ["ps2", "na_rd0"], [f"na_o{b}"])
                        return
                    rg, rr = divmod(r, 8)
                    ob, db = 4 + 2 * (rg % 2), 5 + 2 * (rg % 2)
                    vts = []
                    for c in range(4):
                        row0 = rs + 2 * c
                        if rs % 2 == 0:
                            vts.append((V0_, 2 + row0 // 2, f"na_v0{b}"))
                        else:
                            vts.append((V1_, (row0 - 1) // 2, f"na_v1{b}"))
                    vts += [(V0_, 0, f"na_v0{b}"), (V0_, 1, f"na_v0{b}")]
                    for c in range(6):
                        Vt, ti, vk = vts[c]
                        S.op("pe", lambda e, Vt=Vt, ti=ti, P_=P_, c=c, lo=lo, hi=hi, ob=ob, rr=rr: e.matmul(
                            ps[ob][lo:hi, rr * 64:(rr + 1) * 64], lhsT=Vt[:, ti, lo:hi], rhs=P_[:, c * 64:(c + 1) * 64], start=(c == 0), stop=(c == 5)),
                            [vk, pk], [f"ps{ob}"])
                    for c in range(6):
                        S.op("pe", lambda e, P_=P_, c=c, lo=lo, hi=hi, db=db, rr=rr: e.matmul(
                            ps[db][lo:hi, rr * 64:(rr + 1) * 64], lhsT=C.onesb[:, 0:64], rhs=P_[:, c * 64:(c + 1) * 64], start=(c == 0), stop=(c == 5)),
                            ["cst", pk], [f"ps{db}"])
                    if rr == 7:
                        rd_ = rd[rg % 2]
                        oc0 = 256 + rg * 512
                        S.op("dve", lambda e, rd_=rd_, lo=lo, hi=hi, db=db: e.reciprocal(out=rd_[lo:hi, :], in_=ps[db][lo:hi, :]), [f"ps{db}"], [f"na_rd{rg % 2}"])
                        S.op("dve", lambda e, rd_=rd_, lo=lo, hi=hi, ob=ob, oc0=oc0, o_=o_: e.tensor_tensor(
                            out=o_[lo:hi, oc0:oc0 + 512], in0=ps[ob][lo:hi, :], in1=rd_[lo:hi, :], op=ALU.mult),
                            [f"ps{ob}", f"na_rd{rg % 2}"], [f"na_o{b}"])

                for ix in range(len(items) + LAG):
                    if ix < len(items):
                        pend[ix] = stage1(*items[ix], npx)
                        npx += 1
                    jx = ix - LAG
                    if jx >= 0:
                        stage2(*items[jx], *pend.pop(jx))
                S.dma("sp", oT_d[j][:, s * SEG:(s + 1) * SEG], o_, [f"na_o{b}"], ["oT_d"])
    S.barrier()


def conv_segments(NT):
    segs = []
    for s in range(NT // SEG):
        segs.append((s * SEG, 256))
        segs.append((s * SEG + 256, 2048))
    return segs


def even_inproj(C, hT_d, win_d, cw_d, cb_d, dtb_d, alog_d, xbcT_d, lxT_d, zT_d, gT_d, csT_d, cstok_d, xd_d, NT):
    S, ps = C.S, C.ps
    S.barrier()
    NTT = NT // 128
    with contextlib.ExitStack() as st:
        hT = sb(C, st, "ei_hT", [128, 8, NT], BF16)
        prow0 = sb(C, st, "ei_prow", [128, NT], F32)
        yrow = sb(C, st, "ei_yrow", [128, NT], F32)
        cw = sb(C, st, "ei_cw", [128, 24, 4], F32)
        cb = sb(C, st, "ei_cb", [128, 24], F32)
        dtb = sb(C, st, "ei_dtb", [64, 2], F32)
        st1 = contextlib.ExitStack()
        wst = [sb(C, st1, f"ei_wst{i}", [128, 8, 128], F32) for i in range(2)]
        wb = [sb(C, st1, f"ei_wb{i}", [128, 8, 128], BF16) for i in range(2)]
        prow1 = sb(C, st1, "ei_prow1", [128, NT], F32)
        orow = sb(C, st1, "ei_orow", [128, NT], BF16)
        prows = [prow0, prow1]
        hv = hT_d.rearrange("(k p) t -> p k t", p=128)
        for ti in range(NT // 512):
            S.dma("sp", hT[:, :, ti * 512:(ti + 1) * 512], hv[:, :, ti * 512:(ti + 1) * 512], ["hT_d"], ["ei_hT"])
        S.dma("sp", cw, cw_d, ["cw_d"], ["ei_cw"])
        S.dma("sp", cb, cb_d, ["cb_d"], ["ei_cb"])
        S.dma("sp", dtb[:, 0:1], dtb_d, ["dtb_d"], ["ei_dtb"])
        S.dma("sp", dtb[:, 1:2], alog_d, ["alog_d"], ["ei_dtb"])
        wv = win_d.rearrange("(k p) f -> p k f", p=128)
        n = 0
        for j in range(41):
            ws_, wb_ = wst[j % 2], wb[j % 2]
            prow = prows[j % 2]
            prk = f"ei_prow{j % 2}"
            fw = 128 if j < 40 else 64
            S.dma("sp", ws_[:, :, 0:fw], wv[:, :, j * 128:j * 128 + fw], ["win_d"], [f"ei_wst{j % 2}"])
            S.op("act", lambda e, ws_=ws_, wb_=wb_, fw=fw: e.activation(out=wb_[:, :, 0:fw], in_=ws_[:, :, 0:fw], func=AF.Copy),
                 [f"ei_wst{j % 2}"], [f"ei_wb{j % 2}"])
            for ti in range(NT // 512):
                c0 = ti * 512
                pb = n % 2
                n += 1
                for k in range(8):
                    S.op("pe", lambda e, wb_=wb_, k=k, c0=c0, pb=pb, fw=fw: e.matmul(ps[pb][0:fw, :], lhsT=wb_[:, k, 0:fw], rhs=hT[:, k, c0:c0 + 512],
                                                                                start=(k == 0), stop=(k == 7)), [f"ei_wb{j % 2}", "ei_hT"], [f"ps{pb}"])
                fn = AF.Copy if (j < 24 or j == 40) else (AF.Silu if j < 32 else AF.Gelu)
                S.op("act", lambda e, pb=pb, c0=c0, fn=fn, fw=fw, prow=prow: e.activation(out=prow[0:fw, c0:c0 + 512], in_=ps[pb][0:fw, :], func=fn),
                     [f"ps{pb}"], [prk])
            if j < 24:
                for (s0, L) in conv_segments(NT):
                    S.op("dve", lambda e, j=j, s0=s0, L=L, prow=prow: e.tensor_scalar(out=yrow[:, s0:s0 + L], in0=prow[:, s0:s0 + L], scalar1=cw[:, j, 2:3],
                                                                          scalar2=cb[:, j:j + 1], op0=ALU.mult, op1=ALU.add), [prk, "ei_cw", "ei_cb"], ["ei_yrow"])
                    for (kk, do, so, ln) in ((1, 1, 0, L - 1), (0, 2, 0, L - 2), (3, 0, 1, L - 1)):
                        S.op("dve", lambda e, j=j, s0=s0, kk=kk, do=do, so=so, ln=ln, prow=prow: e.scalar_tensor_tensor(
                            out=yrow[:, s0 + do:s0 + do + ln], in0=prow[:, s0 + so:s0 + so + ln], scalar=cw[:, j, kk:kk + 1],
                            in1=yrow[:, s0 + do:s0 + do + ln], op0=ALU.mult, op1=ALU.add), [prk, "ei_cw", "ei_yrow"], ["ei_yrow"])
                if j < 16:
                    S.op("act", lambda e: e.activation(out=orow, in_=yrow, func=AF.Silu), ["ei_yrow"], ["ei_orow"])
                    S.dma("sp", xbcT_d[j * 128:(j + 1) * 128, :], orow, ["ei_orow"], ["xbcT_d"])
                else:
                    S.dma("sp", lxT_d[(j - 16) * 128:(j - 15) * 128, :], yrow, ["ei_yrow"], ["lxT_d"])
            elif j < 32:
                S.dma("sp", zT_d[(j - 24) * 128:(j - 23) * 128, :], prow, [prk], ["zT_d"])
            elif j < 40:
                S.dma("sp", gT_d[(j - 32) * 128:(j - 31) * 128, :], prow, [prk], ["gT_d"])
        st1.close()
        S.barrier()
        prow = prow0
        xx = yrow[0:64, :]
        t2 = sb(C, st, "ei_t2", [64, NT], F32)
        ea = sb(C, st, "ei_ea", [64, 1], F32)
        S.op("dve", lambda e: e.tensor_scalar(out=xx, in0=prow[0:64, :], scalar1=dtb[:, 0:1], scalar2=None, op0=ALU.add), ["ei_prow0", "ei_dtb"], ["ei_yrow"])
        S.op("act", lambda e: e.activation(out=t2, in_=xx, func=AF.Abs), ["ei_yrow"], ["ei_t2"])
        S.op("act", lambda e: e.activation(out=t2, in_=t2, func=AF.Exp, scale=-1.0), ["ei_t2"], ["ei_t2"])
        S.op("act", lambda e: e.activation(out=t2, in_=t2, func=AF.Ln, bias=C.onec[0:64, :], scale=1.0), ["ei_t2", "cst"], ["ei_t2"])
        S.op("dve", lambda e: e.scalar_tensor_tensor(out=xx, in0=xx, scalar=0.0, in1=t2, op0=ALU.max, op1=ALU.add), ["ei_yrow", "ei_t2"], ["ei_yrow"])
        S.op("act", lambda e: e.activation(out=ea, in_=dtb[:, 1:2], func=AF.Exp), ["ei_dtb"], ["ei_ea"])
        la = prow[0:64, :]
        S.op("dve", lambda e: e.tensor_scalar(out=la, in0=xx, scalar1=ea[:, 0:1], scalar2=-1.0, op0=ALU.mult, op1=ALU.mult), ["ei_yrow", "ei_ea"], ["ei_prow0"])
        cs = t2
        for s in range(NT // SEG):
            c0 = s * SEG
            S.op("dve", lambda e, c0=c0: e.tensor_tensor_scan(out=cs[0:32, c0:c0 + SEG], data0=vw(C.onec[0:32, :], 0, [[0, SEG]]), data1=la[0:32, c0:c0 + SEG],
                                                               initial=0.0, op0=ALU.mult, op1=ALU.add), ["ei_prow0", "cst"], ["ei_t2"])

            def rev(ap, a0, L):
                return vw(ap[32:64, :], a0 + L - 1, [[-1, L]])
            S.op("dve", lambda e, c0=c0: e.tensor_tensor_scan(out=rev(cs, c0, 256), data0=vw(C.onec[32:64, :], 0, [[0, 256]]), data1=rev(la, c0, 256),
                                                               initial=0.0, op0=ALU.mult, op1=ALU.add), ["ei_prow0", "cst"], ["ei_t2"])
            S.op("dve", lambda e, c0=c0: e.tensor_tensor_scan(out=rev(cs, c0 + 256, 2048), data0=vw(C.onec[32:64, :], 0, [[0, 2048]]), data1=rev(la, c0 + 256, 2048),
                                                               initial=cs[32:64, c0:c0 + 1], op0=ALU.mult, op1=ALU.add), ["ei_prow0", "cst", "ei_t2"], ["ei_t2"])
        S.dma("sp", csT_d, cs, ["ei_t2"], ["csT_d"])
        cstok = sb(C, st, "ei_cstok", [128, NTT, 64], F32)
        dttok = sb(C, st, "ei_dttok", [128, NTT, 64], F32)
        for tt in range(NTT):
            pb = 2 + tt % 2
            S.op("pe", lambda e, tt=tt, pb=pb: e.transpose(out=ps[pb][:, 0:64], in_=cs[:, tt * 128:(tt + 1) * 128], identity=C.ident[0:64, 0:64]),
                 ["ei_t2", "cst"], [f"ps{pb}"])
            S.op("pe", lambda e, tt=tt, pb=pb: e.transpose(out=ps[pb][:, 64:128], in_=xx[:, tt * 128:(tt + 1) * 128], identity=C.ident[0:64, 0:64]),
                 ["ei_yrow", "cst"], [f"ps{pb}"])
            S.op("act", lambda e, tt=tt, pb=pb: e.activation(out=cstok[:, tt, :], in_=ps[pb][:, 0:64], func=AF.Copy), [f"ps{pb}"], ["ei_cstok"])
            S.op("act", lambda e, tt=tt, pb=pb: e.activation(out=dttok[:, tt, :], in_=ps[pb][:, 64:128], func=AF.Copy), [f"ps{pb}"], ["ei_dttok"])
        S.dma("sp", cstok_d.rearrange("(t p) f -> p t f", p=128), cstok, ["ei_cstok"], ["cstok_d"])
        xb = [sb(C, st, f"ei_xb{i}", [128, 8, 512], BF16) for i in range(2)]
        xdt = [sb(C, st, f"ei_xdt{i}", [128, 1024], BF16) for i in range(2)]
        xbv = xbcT_d[0:1024, :].rearrange("(k p) t -> p k t", p=128)
        m = 0
        for bi in range(NT // 512):
            xb_ = xb[bi % 2]
            S.dma("sp", xb_, xbv[:, :, bi * 512:(bi + 1) * 512], ["xbcT_d"], [f"ei_xb{bi % 2}"])
            for t4 in range(4):
                tt = bi * 4 + t4
                pb = 4 + tt % 2
                pbf = ps[pb].bitcast(BF16)
                for k in range(8):
                    S.op("pe", lambda e, xb_=xb_, k=k, t4=t4, pbf=pbf: e.transpose(out=pbf[:, k * 128:(k + 1) * 128], in_=xb_[:, k, t4 * 128:(t4 + 1) * 128], identity=C.identb),
                         [f"ei_xb{bi % 2}", "cst"], [f"ps{pb}"])
                for d in range(2):
                    xd_ = xdt[m % 2]
                    S.op("dve", lambda e, xd_=xd_, pbf=pbf, tt=tt, d=d: e.tensor_tensor(
                        out=xd_.rearrange("p (h q) -> p h q", h=16), in0=pbf.rearrange("p (h q) -> p h q", h=16),
                        in1=vw(dttok, tt * 64 + d * 32, [[1, 16], [0, 64]]), op=ALU.mult), [f"ps{pb}", "ei_dttok"], [f"ei_xdt{m % 2}"])
                    S.dma("sp", xd_d[d][tt * 128:(tt + 1) * 128, :], xd_, [f"ei_xdt{m % 2}"], ["xd_d"])
                    m += 1
    S.barrier()


def ssd_allowed(d, lt):
    if lt == 0:
        return [(0, 0), (1, 1)]
    base = 2 + 4 * (lt - 1)
    if d == 0:
        return [(stl, None) for stl in range(base)] + [(base + r, r) for r in range(4)]
    return [(0, None), (1, None)] + [(base + r, r) for r in range(4)] + [(stl, None) for stl in range(base + 4, 18)]


def ssd_phase(C, xbcT_d, csT_d, cstok_d, xd_d, zT_d, dcol_d, sg_d, ymixT_d, NT, esel_d, negm_d):
    S, ps = C.S, C.ps
    S.barrier()
    with contextlib.ExitStack() as st:
        C.esel = sb(C, st, "sd_esel", [64, 48, 128], F32)
        C.negm = sb(C, st, "sd_negm", [128, 8, 512], F32)
        S.dma("sp", C.esel, esel_d, ["esel_d"], ["sd_esel"])
        S.dma("sp", C.negm, negm_d, ["negm_d"], ["sd_negm"])
        csT = sb(C, st, "sd_csT", [64, SEG], F32)
        cstok = sb(C, st, "sd_cstok", [128, 18, 64], F32)
        ncstok = sb(C, st, "sd_ncstok", [128, 18, 64], F32)
        Bg = sb(C, st, "sd_B", [128, SEG], BF16)
        Cg = sb(C, st, "sd_C", [128, SEG], BF16)
        xdg = [sb(C, st, f"sd_xd{d}", [128, 18, 256], BF16) for d in range(2)]
        xg = sb(C, st, "sd_x", [128, 2, SEG], BF16)
        csb = [sb(C, st, f"sd_csb{i}", [128, 512], F32) for i in range(8)]
        Gs = [sb(C, st, f"sd_G{i}", [128, 512], BF16) for i in range(3)]
        dd = [sb(C, st, f"sd_dd{i}", [128, 512], F32) for i in range(3)]
        ee = [sb(C, st, f"sd_ee{i}", [128, 512], BF16) for i in range(3)]
        M = [sb(C, st, f"sd_M{i}", [128, 512], BF16) for i in range(3)]
        zt = sb(C, st, "sd_z", [128, 2, 512], F32)
        y = sb(C, st, "sd_y", [128, 2, 512], F32)
        sq = sb(C, st, "sd_sq", [128, 512], F32)
        r = sb(C, st, "sd_r", [128, 512], F32)
        yo = sb(C, st, "sd_yo", [128, 2, 512], BF16)
        dcol = sb(C, st, "sd_dcol", [128, 8], F32)
        sg = sb(C, st, "sd_sg", [128, 8], F32)
        S.dma("sp", dcol, dcol_d, ["dcol_d"], ["sd_dcol"])
        S.dma("sp", sg, sg_d, ["sg_d"], ["sd_sg"])
        ctv = cstok_d.rearrange("(t p) f -> p t f", p=128)
        xdv = [xd_d[d].rearrange("(t p) f -> p t f", p=128) for d in range(2)]
        xv = xbcT_d[0:1024, :].rearrange("(k p) t -> p k t", p=128)
        zv = zT_d.rearrange("(k p) t -> p k t", p=128)
        yv = ymixT_d[0:1024, :].rearrange("(k p) t -> p k t", p=128)
        ng = 0
        nh = 0
        for s in range(NT // SEG):
            sc0 = s * SEG
            S.dma("sp", csT, csT_d[:, sc0:sc0 + SEG], ["csT_d"], ["sd_csT"])
            S.dma("sp", cstok, ctv[:, s * 18:(s + 1) * 18, :], ["cstok_d"], ["sd_cstok"])
            S.op("dve", lambda e: e.tensor_scalar(out=ncstok, in0=cstok, scalar1=-1.0, scalar2=None, op0=ALU.mult), ["sd_cstok"], ["sd_ncstok"])
            for g in range(4):
                S.dma("sp", Bg, xbcT_d[1024 + g * 128:1024 + (g + 1) * 128, sc0:sc0 + SEG], ["xbcT_d"], ["sd_B"])
                S.dma("sp", Cg, xbcT_d[1536 + g * 128:1536 + (g + 1) * 128, sc0:sc0 + SEG], ["xbcT_d"], ["sd_C"])
                for d in range(2):
                    S.dma("sp", xdg[d], xdv[d][:, s * 18:(s + 1) * 18, g * 256:(g + 1) * 256], ["xd_d"], [f"sd_xd{d}"])
                S.dma("sp", xg, xv[:, 2 * g:2 * g + 2, sc0:sc0 + SEG], ["xbcT_d"], ["sd_x"])
                for lt in range(5):
                    l0, W = (0, 256) if lt == 0 else (256 + 512 * (lt - 1), 512)
                    S.dma("sp", zt[:, :, 0:W], zv[:, 2 * g:2 * g + 2, sc0 + l0:sc0 + l0 + W], ["zT_d"], ["sd_z"])
                    for d in range(2):
                        for hh in range(4):
                            row = d * 32 + 4 * g + hh
                            i8 = d * 4 + hh
                            S.op("pe", lambda e, row=row, l0=l0, W=W: e.matmul(ps[7][:, 0:W], lhsT=C.esel[:, row, :], rhs=csT[:, l0:l0 + W], start=True, stop=True),
                                 ["sd_esel", "sd_csT"], ["ps7"])
                            S.op("act", lambda e, i8=i8, W=W: e.activation(out=csb[i8][:, 0:W], in_=ps[7][:, 0:W], func=AF.Copy), ["ps7"], [f"sd_csb{i8}"])
                    started = [False] * 4
                    pairs = [(d, stl, mi) for d in range(2) for (stl, mi) in ssd_allowed(d, lt)]
                    last_b = [p_ for p_ in pairs if p_[0] == 1][-1][1]
                    items = [(pi, d, stl, mi, hh) for pi, (d, stl, mi) in enumerate(pairs) for hh in range(4)]
                    LAG = 2
                    pend = {}
                    cur = {}
                    for ix in range(len(items) + LAG):
                        if ix < len(items):
                            pi, d, stl, mi, hh = items[ix]
                            if hh == 0:
                                gb = 5 + ng % 2
                                G_ = Gs[ng % 3]
                                gk = f"sd_G{ng % 3}"
                                ng += 1
                                S.op("pe", lambda e, stl=stl, l0=l0, W=W, gb=gb: e.matmul(ps[gb][:, 0:W], lhsT=Bg[:, stl * 128:(stl + 1) * 128], rhs=Cg[:, l0:l0 + W], start=True, stop=True),
                                     ["sd_B", "sd_C"], [f"ps{gb}"])
                                S.op("act", lambda e, G_=G_, gb=gb, W=W: e.activation(out=G_[:, 0:W], in_=ps[gb][:, 0:W], func=AF.Copy), [f"ps{gb}"], [gk])
                                cur = dict(G_=G_, gk=gk)
                            row = d * 32 + 4 * g + hh
                            i8 = d * 4 + hh
                            dd_, ee_, M_ = dd[nh % 3], ee[nh % 3], M[nh % 3]
                            dk, ek, mk = f"sd_dd{nh % 3}", f"sd_ee{nh % 3}", f"sd_M{nh % 3}"
                            nh += 1
                            col = cstok[:, stl, row:row + 1]
                            if mi is None:
                                ncol = ncstok[:, stl, row:row + 1]
                                S.op("act", lambda e, ee_=ee_, i8=i8, ncol=ncol, W=W: e.activation(out=ee_[:, 0:W], in_=csb[i8][:, 0:W], func=AF.Exp, bias=ncol, scale=1.0),
                                     [f"sd_csb{i8}", "sd_ncstok"], [ek])
                            else:
                                S.op("dve", lambda e, dd_=dd_, i8=i8, col=col, W=W, d=d, mi=mi: e.scalar_tensor_tensor(
                                    out=dd_[:, 0:W], in0=csb[i8][:, 0:W], scalar=col, in1=C.negm[:, d * 4 + mi, 0:W], op0=ALU.subtract, op1=ALU.add),
                                    [f"sd_csb{i8}", "sd_cstok", "sd_negm"], [dk])
                                S.op("act", lambda e, dd_=dd_, ee_=ee_, W=W: e.activation(out=ee_[:, 0:W], in_=dd_[:, 0:W], func=AF.Exp), [dk], [ek])
                            pend[ix] = (d, stl, hh, ee_, ek, M_, mk, cur["G_"], cur["gk"])
                        jx = ix - LAG
                        if jx < 0:
                            continue
                        d, stl, hh, ee_, ek, M_, mk, G_, gk = pend.pop(jx)
                        S.op("dve", lambda e, ee_=ee_, M_=M_, G_=G_, W=W: e.tensor_tensor(out=M_[:, 0:W], in0=ee_[:, 0:W], in1=G_[:, 0:W], op=ALU.mult), [ek, gk], [mk])
                        is_last = (d == 1 and stl == last_b)
                        ab = hh // 2
                        lo = (hh % 2) * 64
                        S.op("pe", lambda e, d=d, stl=stl, hh=hh, M_=M_, W=W, ab=ab, lo=lo, st_=(not started[hh]), is_last=is_last: e.matmul(
                            ps[ab][lo:lo + 64, 0:W], lhsT=xdg[d][:, stl, hh * 64:(hh + 1) * 64], rhs=M_[:, 0:W], start=st_, stop=is_last),
                            [f"sd_xd{d}", mk], [f"ps{ab}"])
                        started[hh] = True
                    for pr in range(2):
                        ch = 2 * g + pr
                        S.op("dve", lambda e, pr=pr, ch=ch, l0=l0, W=W: e.scalar_tensor_tensor(out=y[:, pr, 0:W], in0=xg[:, pr, l0:l0 + W], scalar=dcol[:, ch:ch + 1],
                                                                                            in1=ps[pr][:, 0:W], op0=ALU.mult, op1=ALU.add),
                             ["sd_x", "sd_dcol", f"ps{pr}"], ["sd_y"])
                        S.op("dve", lambda e, pr=pr, W=W: e.tensor_tensor(out=y[:, pr, 0:W], in0=y[:, pr, 0:W], in1=zt[:, pr, 0:W], op=ALU.mult), ["sd_y", "sd_z"], ["sd_y"])
                        S.op("act", lambda e, pr=pr, W=W: e.activation(out=sq[:, 0:W], in_=y[:, pr, 0:W], func=AF.Square), ["sd_y"], ["sd_sq"])
                        S.op("pe", lambda e, pr=pr, W=W: e.matmul(ps[4][:, 0:W], lhsT=C.ones256, rhs=sq[:, 0:W], start=(pr == 0), stop=(pr == 1)), ["sd_sq", "cst"], ["ps4"])
                    S.op("act", lambda e, W=W: e.activation(out=r[:, 0:W], in_=ps[4][:, 0:W], func=AF.Sqrt, bias=C.epsc, scale=1.0), ["ps4", "cst"], ["sd_r"])
                    S.op("dve", lambda e, W=W: e.reciprocal(out=r[:, 0:W], in_=r[:, 0:W]), ["sd_r"], ["sd_r"])
                    for pr in range(2):
                        ch = 2 * g + pr
                        S.op("dve", lambda e, pr=pr, ch=ch, W=W: e.scalar_tensor_tensor(out=yo[:, pr, 0:W], in0=y[:, pr, 0:W], scalar=sg[:, ch:ch + 1], in1=r[:, 0:W],
                                                                                     op0=ALU.mult, op1=ALU.mult), ["sd_y", "sd_sg", "sd_r"], ["sd_yo"])
                    S.dma("sp", yv[:, 2 * g:2 * g + 2, sc0 + l0:sc0 + l0 + W], yo[:, :, 0:W], ["sd_yo"], ["ymixT_d"])
    S.barrier()


def lru_phase(C, lxT_d, gT_d, wbd_d, bcol_d, lam_d, ymixT_d, NT):
    S, ps = C.S, C.ps
    S.barrier()
    with contextlib.ExitStack() as st:
        lxs = [sb(C, st, f"lr_x{i}", [128, SEG], F32) for i in range(2)]
        gts = [sb(C, st, f"lr_g{i}", [128, SEG], F32) for i in range(2)]
        Ws = [sb(C, st, f"lr_W{i}", [128, 4, 128], F32) for i in range(2)]
        bc = sb(C, st, "lr_bc", [128, 8, 4], F32)
        nsp = sb(C, st, "lr_nsp", [128, 8, 2, 2], F32)
        lam = sb(C, st, "lr_lam", [128, 8, 2], F32)
        rrs = [sb(C, st, f"lr_r{d}", [128, SEG], F32) for d in range(2)]
        iis = [sb(C, st, f"lr_i{d}", [128, SEG], F32) for d in range(2)]
        aas = [sb(C, st, f"lr_a{d}", [128, SEG], F32) for d in range(2)]
        bbs = [sb(C, st, f"lr_b{d}", [128, SEG], F32) for d in range(2)]
        hhs = [[sb(C, st, f"lr_h{i}{d}", [128, SEG], F32) for d in range(2)] for i in range(2)]
        yos = [sb(C, st, f"lr_yo{i}", [128, SEG], BF16) for i in range(2)]
        S.dma("sp", bc, bcol_d, ["bcol_d"], ["lr_bc"])
        S.dma("sp", lam, lam_d, ["lam_d"], ["lr_lam"])
        S.op("act", lambda e: e.activation(out=lam, in_=lam, func=AF.Exp, scale=-1.0), ["lr_lam"], ["lr_lam"])
        S.op("act", lambda e: e.activation(out=lam, in_=lam, func=AF.Ln, bias=C.onec, scale=1.0), ["lr_lam", "cst"], ["lr_lam"])
        S.op("dve", lambda e: e.tensor_scalar(out=nsp[:, :, :, 0], in0=lam, scalar1=-8.0, scalar2=None, op0=ALU.mult), ["lr_lam"], ["lr_nsp"])
        S.op("dve", lambda e: e.tensor_scalar(out=nsp[:, :, :, 1], in0=lam, scalar1=-16.0, scalar2=None, op0=ALU.mult), ["lr_lam"], ["lr_nsp"])
        tiles = [(0, 512), (512, 512), (1024, 512), (1536, 512), (2048, 256)]
        n = 0
        it = 0
        for c in range(8):
            W = Ws[c % 2]
            wk = f"lr_W{c % 2}"
            S.dma("sp", W, wbd_d[c], ["wbd_d"], [wk])
            for s_ in range(NT // SEG):
                sc0 = s_ * SEG
                ip = it % 2
                it += 1
                lx, gt, yo, hh = lxs[ip], gts[ip], yos[ip], hhs[ip]
                xk, gk, yk = f"lr_x{ip}", f"lr_g{ip}", f"lr_yo{ip}"
                S.dma("sp", lx, lxT_d[c * 128:(c + 1) * 128, sc0:sc0 + SEG], ["lxT_d"], [xk])
                S.dma("sp", gt, gT_d[c * 128:(c + 1) * 128, sc0:sc0 + SEG], ["gT_d"], [gk])
                for d in range(2):
                    rr, ii, aa, bb = rrs[d], iis[d], aas[d], bbs[d]
                    rk, ik, ak, bk, hk = f"lr_r{d}", f"lr_i{d}", f"lr_a{d}", f"lr_b{d}", f"lr_h{ip}{d}"
                    for gi, dst, dk in ((0, rr, rk), (1, ii, ik)):
                        for (t0, Wd) in tiles:
                            pb = n % 4
                            n += 1
                            S.op("pe", lambda e, d=d, gi=gi, t0=t0, Wd=Wd, pb=pb, W=W, lx=lx: e.matmul(ps[pb][:, 0:Wd], lhsT=W[:, d * 2 + gi, :], rhs=lx[:, t0:t0 + Wd], start=True, stop=True),
                                 [wk, xk], [f"ps{pb}"])
                            S.op("act", lambda e, dst=dst, d=d, gi=gi, t0=t0, Wd=Wd, pb=pb, c=c: e.activation(
                                out=dst[:, t0:t0 + Wd], in_=ps[pb][:, 0:Wd], func=AF.Sigmoid, bias=bc[:, c, d * 2 + gi:d * 2 + gi + 1], scale=1.0),
                                [f"ps{pb}", "lr_bc"], [dk])
                    S.op("act", lambda e, c=c, d=d, aa=aa, rr=rr: e.activation(out=aa, in_=rr, func=AF.Exp, scale=nsp[:, c, d, 0:1]), [rk, "lr_nsp"], [ak])
                    S.op("act", lambda e, c=c, d=d, bb=bb, rr=rr: e.activation(out=bb, in_=rr, func=AF.Exp, scale=nsp[:, c, d, 1:2]), [rk, "lr_nsp"], [bk])
                    S.op("dve", lambda e, bb=bb: e.tensor_scalar(out=bb, in0=bb, scalar1=-1.0, scalar2=1.0, op0=ALU.mult, op1=ALU.add), [bk], [bk])
                    S.op("act", lambda e, bb=bb: e.activation(out=bb, in_=bb, func=AF.Sqrt), [bk], [bk])
                    S.op("dve", lambda e, bb=bb, ii=ii: e.tensor_tensor(out=bb, in0=bb, in1=ii, op=ALU.mult), [bk, ik], [bk])
                    S.op("dve", lambda e, bb=bb, lx=lx: e.tensor_tensor(out=bb, in0=bb, in1=lx, op=ALU.mult), [bk, xk], [bk])
                    h_ = hh[d]
                    if d == 0:
                        S.op("dve", lambda e, h_=h_, aa=aa, bb=bb: e.tensor_tensor_scan(out=h_, data0=aa, data1=bb, initial=0.0, op0=ALU.mult, op1=ALU.add), [ak, bk], [hk])
                    else:
                        def rev(ap, a0, L):
                            return vw(ap, a0 + L - 1, [[-1, L]])
                        S.op("dve", lambda e, h_=h_, aa=aa, bb=bb: e.tensor_tensor_scan(out=rev(h_, 0, 256), data0=rev(aa, 0, 256), data1=rev(bb, 0, 256), initial=0.0,
                                                                                       op0=ALU.mult, op1=ALU.add), [ak, bk], [hk])
                        S.op("dve", lambda e, h_=h_, aa=aa, bb=bb: e.tensor_tensor_scan(out=rev(h_, 256, 2048), data0=rev(aa, 256, 2048), data1=rev(bb, 256, 2048), initial=h_[:, 0:1],
                                                                                       op0=ALU.mult, op1=ALU.add), [ak, bk, hk], [hk])
                S.op("dve", lambda e, hh=hh: e.tensor_tensor(out=hh[0], in0=hh[0], in1=hh[1], op=ALU.add), [f"lr_h{ip}0", f"lr_h{ip}1"], [f"lr_h{ip}0"])
                S.op("dve", lambda e, hh=hh, yo=yo, gt=gt: e.tensor_tensor(out=yo, in0=hh[0], in1=gt, op=ALU.mult), [f"lr_h{ip}0", gk], [yk])
                S.dma("sp", ymixT_d[1024 + c * 128:1024 + (c + 1) * 128, sc0:sc0 + SEG], yo, [yk], ["ymixT_d"])
    S.barrier()


NCST = 128 * 6 + 64 + 2
LAYERS = 4


def host_consts():
    c = np.zeros((128, NCST), np.float32)
    c[:, 0:128] = np.eye(128)
    c[:, 128:256] = np.arange(128)[None, :]
    c[:, 256:384] = 1.0 / 1024
    bd = np.zeros((128, 128), np.float32)
    bd[:64, :64] = 1.0 / 64
    bd[64:, 64:] = 1.0 / 64
    c[:, 384:512] = bd
    c[:, 512:640] = 1.0 / 256
    c[:, 640:768] = 1.0
    p = np.arange(128)
    kc = p % 64
    qc = np.arange(64)
    cs_ = np.clip(qc - 8, 0, 48)
    c[:, 768:832] = ((kc[:, None] >= cs_[None, :]) & (kc[:, None] < cs_[None, :] + 16)).astype(np.float32)
    c[:, 832] = EPS
    c[:, 833] = 1.0
    esel = np.zeros((64, 48, 128), np.float32)
    for r_ in range(48):
        esel[r_, r_, :] = 1.0
    s_ = np.arange(128)[:, None]
    l_ = np.arange(512)[None, :]
    negm = np.zeros((128, 8, 512), np.float32)
    for r_ in range(4):
        negm[:, r_, :] = np.where(l_ >= 128 * r_ + s_, 0.0, -1.0e6)
        negm[:, 4 + r_, :] = np.where(128 * r_ + s_ >= l_, 0.0, -1.0e6)
    return c, esel, negm


def build_program(NT=2 * SEG, layers=range(LAYERS), debug=False, stop_after=None, with_peer=True):
    nc = bass.Bass("TRN2", target_bir_lowering=False)
    C = Ctx()
    C.nc = nc
    C.S = S = Sched(nc)
    NS = NT // SEG

    def ein(name, shape, dt=F32):
        return nc.dram_tensor(name, list(shape), dt, kind="ExternalInput").ap()

    def scratch(name, shape, dt):
        return nc.dram_tensor(name, list(shape), dt, kind="ExternalOutput" if debug else "Internal").ap()

    x_in = ein("x_in", [1024, NT])
    cT_d = ein("cT", [128, 8, 3])
    cst_d = ein("cst", [128, NCST])
    esel_d = ein("esel", [64, 48, 128])
    negm_d = ein("negm", [128, 8, 512])
    adaw = ein("adaw", [LAYERS, 1024, 6144])
    adabT = ein("adabT", [LAYERS, 128, 48])
    ngT = ein("ngT", [LAYERS, 128, 2, 8])
    win = ein("win", [2, 1024, 5184])
    cw = ein("cw", [2, 128, 24, 4])
    cb = ein("cb", [2, 128, 24])
    dtb = ein("dtb", [2, 64, 1])
    alog = ein("alog", [2, 64, 1])
    dcol = ein("dcol", [2, 128, 8])
    sgc = ein("sgc", [2, 128, 8])
    wbd = ein("wbd", [2, 8, 128, 4, 128])
    bcol = ein("bcol", [2, 128, 8, 4])
    lamc = ein("lamc", [2, 128, 8, 2])
    wout = ein("wout", [2, 2048, 1024])
    wqkv = ein("wqkv", [2, 1024, 3072])
    qg2 = ein("qg2", [2, 128, 1])
    kg2 = ein("kg2", [2, 128, 1])
    rpbG = ein("rpbG", [2, 128, 16, 15, 64])
    wo = ein("wo", [2, 1024, 1024])
    if with_peer:
        pwq = ein("pwq", [LAYERS, 1024, 2048])
        pkeys = ein("pkeys", [LAYERS, 128, 16, 128])
        puT = ein("puT", [LAYERS, 128, 128, 8, 128])
        pv = ein("pv", [LAYERS, 16384, 1024])
    xT = nc.dram_tensor("xT", [1024, NT], F32, kind="ExternalOutput").ap()
    hT_d = scratch("hT_d", [1024, NT], BF16)
    rout_d = scratch("rout_d", [3, 128, NT], F32)
    xbcT_d = scratch("xbcT_d", [2048, NT], BF16)
    lxT_d = scratch("lxT_d", [1024, NT], F32)
    zT_d = scratch("zT_d", [1024, NT], F32)
    gT_d = scratch("gT_d", [1024, NT], F32)
    csT_d = scratch("csT_d", [64, NT], F32)
    cstok_d = scratch("cstok_d", [NT, 64], F32)
    xd_d = scratch("xd_d", [2, NT, 1024], BF16)
    ymixT_d = scratch("ymixT_d", [2048, NT], BF16)
    qkT_d = scratch("qkT_d", [16, 128, NT], BF16)
    v0_d = scratch("v0_d", [NT, 1024], BF16)
    v1_d = scratch("v1_d", [NS * 15 * 128, 1024], BF16)
    oT_d = scratch("oT_d", [8, 128, NT], BF16)

    C.ps = [nc.alloc_psum_tensor(f"psb{i}", [128, 512], F32).ap() for i in range(8)]
    cst = nc.alloc_sbuf_tensor("cst_sb", [128, NCST], F32).ap()
    C.ident, C.iota, C.onesm = cst[:, 0:128], cst[:, 128:256], cst[:, 256:384]
    C.bd64, C.ones256, C.mask01 = cst[:, 384:512], cst[:, 512:640], cst[:, 768:832]
    C.epsc, C.onec = cst[:, 832:833], cst[:, 833:834]
    C.identb = nc.alloc_sbuf_tensor("identb", [128, 128], BF16).ap()
    C.onesb = nc.alloc_sbuf_tensor("onesb", [128, 128], BF16).ap()
    C.scT = nc.alloc_sbuf_tensor("scT", [128, 8, 3], F32).ap()
    C.mods = [nc.alloc_sbuf_tensor(f"mods{l}", [128, 48, 3], F32).ap() for l in range(LAYERS)]
    C.modA = [nc.alloc_sbuf_tensor(f"modA{l}", [128, 2, 8, 3], F32).ap() for l in range(LAYERS)]
    S.dma("sp", cst, cst_d, ["cst_d"], ["cst"])
    S.op("act", lambda e: e.activation(out=C.identb, in_=C.ident, func=AF.Copy), ["cst"], ["cst"])
    S.op("act", lambda e: e.activation(out=C.onesb, in_=cst[:, 640:768], func=AF.Copy), ["cst"], ["cst"])
    S.dma("sp", C.scT, cT_d, ["cT_d"], ["scT"])
    S.op("act", lambda e: e.activation(out=C.scT, in_=C.scT, func=AF.Silu), ["scT"], ["scT"])
    for k in range(8):
        S.dma("sp", xT[k * 128:(k + 1) * 128, :], x_in[k * 128:(k + 1) * 128, :], ["x_in"], ["xT_d"])

    def done(tag):
        return stop_after is not None and tag == stop_after

    for l in layers:
        j = l // 2
        adaln(C, l, adaw[l], adabT[l], ngT[l])
        norm_mod(C, l, 0, xT, hT_d, NT)
        if done(f"nm{l}"):
            break
        if l % 2 == 0:
            even_inproj(C, hT_d, win[j], cw[j], cb[j], dtb[j], alog[j], xbcT_d, lxT_d, zT_d, gT_d, csT_d, cstok_d, xd_d, NT)
            if done(f"ei{l}"):
                break
            ssd_phase(C, xbcT_d, csT_d, cstok_d, xd_d, zT_d, dcol[j], sgc[j], ymixT_d, NT, esel_d, negm_d)
            if done(f"ssd{l}"):
                break
            lru_phase(C, lxT_d, gT_d, wbd[j], bcol[j], lamc[j], ymixT_d, NT)
            if done(f"lru{l}"):
                break
            resid_linear(C, l, ymixT_d, 16, wout[j], 16, xT, NT)
        else:
            na_proj(C, hT_d, wqkv[j], qg2[j], kg2[j], qkT_d, v0_d, v1_d, NT)
            if done(f"np{l}"):
                break
            na_attn(C, qkT_d, v0_d, v1_d, rpbG[j], oT_d, NT)
            if done(f"na{l}"):
                break
            resid_linear(C, l, oT_d.rearrange("k p t -> (k p) t"), 8, wo[j], 16, xT, NT)
        if done(f"mix{l}"):
            break
        norm_mod(C, l, 1, xT, hT_d, NT)
        if not with_peer:
            continue
        peer_route(C, hT_d, pwq[l], pkeys[l], rout_d, NT)
        if done(f"pr{l}"):
            break
        peer_expert(C, hT_d, rout_d, puT[l], pv[l], xT, (lambda ch, m, l=l: C.mods[l][:, 40 + ch, m:m + 1]), NT, seg_m)
    n = S.finalize()
    return nc, n


def host_prep(inp, core, ncores=8):
    f = np.float32
    bs = slice(2 * core, 2 * core + 2)
    x, ctx, c, c_ctx = inp["x"][bs], inp["ctx"][bs], inp["c"][bs], inp["c_ctx"]
    seq = np.concatenate([ctx, x], axis=1)
    x_in = np.ascontiguousarray(seq.reshape(2 * SEG, 1024).T)
    cm = np.stack([c[0], c[1], c_ctx], axis=1)
    cT = np.ascontiguousarray(cm.reshape(8, 128, 3).transpose(1, 0, 2))
    return {"x_in": x_in.astype(f), "cT": cT.astype(f)}


def colT(a, nch):
    a = np.asarray(a, np.float32)
    lead = a.shape[:-1]
    return np.ascontiguousarray(np.moveaxis(a.reshape(*lead, nch, 128), -1, -2))


def host_shared(inp):
    f = np.float32
    g = lambda k: np.asarray(inp[k], f)
    cst, esel, negm = host_consts()
    d = {"cst": cst, "esel": esel, "negm": negm}
    d["adaw"] = g("ada_w")
    d["adabT"] = colT(g("ada_b"), 48)
    d["ngT"] = np.ascontiguousarray(np.stack([colT(g("norm1_g"), 8), colT(g("norm2_g"), 8)], axis=2))
    w = g("ev_w_in")
    z16 = np.zeros((2, 1024, 16), f)
    d["win"] = np.ascontiguousarray(np.concatenate(
        [w[:, :, 0:2048], w[:, :, 2080:3104], w[:, :, 3104:4128], w[:, :, 4128:5152], w[:, :, 2048:2064], z16, w[:, :, 2064:2080], z16], axis=2))
    cwx = np.concatenate([g("ev_conv_w"), g("ev_lru_conv_w")], axis=2)
    d["cw"] = np.ascontiguousarray(cwx.reshape(2, 4, 24, 128).transpose(0, 3, 2, 1))
    d["cb"] = colT(np.concatenate([g("ev_conv_b"), g("ev_lru_conv_b")], axis=1), 24)
    z16b = np.zeros((2, 16), f)
    dtb = g("ev_dt_bias")
    al = g("ev_a_log")
    d["dtb"] = np.ascontiguousarray(np.concatenate([dtb[:, 0], z16b, dtb[:, 1], z16b], axis=1)[:, :, None])
    d["alog"] = np.ascontiguousarray(np.concatenate([al[:, 0], z16b, al[:, 1], z16b], axis=1)[:, :, None])
    d["dcol"] = colT(np.repeat(g("ev_d"), 64, axis=1), 8)
    d["sgc"] = colT(g("ev_ssd_norm_g"), 8)
    wa, wx = g("ev_lru_wa"), g("ev_lru_wx")
    wbd = np.zeros((2, 8, 128, 4, 128), f)
    for dd_ in range(2):
        for gi, ww in enumerate((wa, wx)):
            for n_ in range(16):
                c_, o_ = n_ // 2, (n_ % 2) * 64
                wbd[:, c_, o_:o_ + 64, dd_ * 2 + gi, o_:o_ + 64] = ww[:, dd_, n_]
    d["wbd"] = wbd
    ba, bx = colT(g("ev_lru_ba"), 8), colT(g("ev_lru_bx"), 8)
    d["bcol"] = np.ascontiguousarray(np.stack([ba[:, 0], bx[:, 0], ba[:, 1], bx[:, 1]], axis=-1))
    d["lamc"] = np.ascontiguousarray(np.moveaxis(colT(g("ev_lru_lam"), 8), 1, -1))
    d["wout"] = g("ev_w_out")
    d["wqkv"] = g("od_w_qkv")
    d["qg2"] = np.ascontiguousarray(np.tile(g("od_q_norm_g"), (1, 2))[:, :, None])
    d["kg2"] = np.ascontiguousarray(np.tile(g("od_k_norm_g"), (1, 2))[:, :, None])
    rpb = g("od_rpb")
    p = np.arange(128)
    kc, up = p % 64, p // 64
    qc = np.arange(64)
    dc = np.clip(kc[:, None] - qc[None, :] + 15, 0, 30)
    dr = np.clip(np.arange(15)[None, :] + up[:, None], 0, 14)
    d["rpbG"] = np.ascontiguousarray(rpb[:, :, dr[:, :, None], dc[:, None, :]].transpose(0, 2, 1, 3, 4))
    d["wo"] = g("od_w_o")
    d["pwq"] = g("pe_w_q")
    d["pkeys"] = np.ascontiguousarray(g("pe_keys").reshape(4, 16, 128, 128).transpose(0, 3, 1, 2))
    u = g("pe_u")
    d["puT"] = np.ascontiguousarray(u.reshape(4, 128, 128, 8, 128).transpose(0, 2, 4, 3, 1))
    d["pv"] = g("pe_v")
    return d


_CACHE = {}


def kernel(**inputs):
    if "nc" not in _CACHE:
        _CACHE["nc"] = build_program()[0]
    nc = _CACHE["nc"]
    shared = host_shared(inputs)
    in_maps = []
    for core in range(8):
        m = dict(shared)
        m.update(host_prep(inputs, core))
        in_maps.append(m)
    res = run_bass_kernel_spmd(nc, in_maps, core_ids=list(range(8)))
    out = np.empty((16, 2048, 1024), np.float32)
    for core in range(8):
        xT = np.asarray(res.results[core]["xT"], np.float32)
        seq = xT.T.reshape(2, SEG, 1024)
        out[2 * core:2 * core + 2] = seq[:, 256:, :]
    return out
```
